# Optimizing a Trainium2 kernel written in Bass

```python
import math
import jax
import jax.numpy as jnp
from jax import lax
import numpy as np

D_MODEL = 1024
BATCH = 4
SEQ = 4096
DEPTH = 4

CTX_LEN = 256
GRID_W = 64

RET_HEADS = 4
RET_DK = 256
RET_DV = 256
RET_W = RET_HEADS * RET_DV
RET_CHUNK = 128

CONV_W = D_MODEL
CONV_K = 3

ATT_HEADS = 8
ATT_DH = 64
ATT_W = ATT_HEADS * 2 * ATT_DH
Q_BLOCK = 128

ROPE_BASE = 10000.0
N_BRANCH = 3
LN_EPS = 1e-6
DEEPNORM_ALPHA = (2 * DEPTH) ** 0.25
DEEPNORM_BETA = (8 * DEPTH) ** -0.25

KV_SIZES = (RET_HEADS * RET_DK, RET_W, 2 * ATT_HEADS * ATT_DH, ATT_W)
IN_SIZES = KV_SIZES + (RET_HEADS * RET_DK, RET_W, 2 * ATT_HEADS * ATT_DH, ATT_W, CONV_W, CONV_W, CONV_W, CONV_W, N_BRANCH * D_MODEL)
KV_COLS = RET_HEADS * RET_DK + RET_W + 2 * ATT_HEADS * ATT_DH + ATT_W
IN_COLS = KV_COLS + RET_HEADS * RET_DK + RET_W + 2 * ATT_HEADS * ATT_DH + ATT_W + 4 * CONV_W + N_BRANCH * D_MODEL

kernel_name = "hybrid_retention_conv_diffattn_prefix_dit"


def _split(p, sizes):
    return jnp.split(p, np.cumsum(sizes)[:-1].tolist(), axis=-1)


def _layer_norm(x, gain=None, bias=None):
    xf = x.astype(jnp.float32)
    xc = xf - jnp.mean(xf, axis=-1, keepdims=True)
    y = xc * lax.rsqrt(jnp.mean(xc * xc, axis=-1, keepdims=True) + LN_EPS)
    if gain is not None:
        y = y * gain.astype(jnp.float32) + bias.astype(jnp.float32)
    return y.astype(x.dtype)


def _rms_norm(x):
    xf = x.astype(jnp.float32)
    return (xf * lax.rsqrt(jnp.mean(xf * xf, axis=-1, keepdims=True) + LN_EPS)).astype(x.dtype)


def _axial_rope_tables(rows, cols, head_dim):
    n_freq = head_dim // 4
    inv = ROPE_BASE ** (-jnp.arange(n_freq, dtype=jnp.float32) / n_freq)
    ang = jnp.concatenate([rows[:, None].astype(jnp.float32) * inv,
                           cols[:, None].astype(jnp.float32) * inv], axis=-1)
    return jnp.cos(ang), jnp.sin(ang)


def _apply_rope(x, cos, sin):
    half = x.shape[-1] // 2
    shape = (x.shape[1],) + (1,) * (x.ndim - 3) + (half,)
    c = cos.reshape(shape).astype(x.dtype)
    s = sin.reshape(shape).astype(x.dtype)
    x1, x2 = x[..., :half], x[..., half:]
    return jnp.concatenate([x1 * c - x2 * s, x1 * s + x2 * c], axis=-1)


def _retention_dir(q, k, v, log_g, s0, exclusive):
    b, h, t, dk = q.shape
    dv = v.shape[-1]
    n = t // RET_CHUNK
    qc = q.reshape(b, h, n, RET_CHUNK, dk)
    kc = k.reshape(b, h, n, RET_CHUNK, dk)
    vc = v.reshape(b, h, n, RET_CHUNK, dv)
    i = jnp.arange(RET_CHUNK, dtype=jnp.float32)
    dist = i[:, None] - i[None, :]
    mask = dist > 0 if exclusive else dist >= 0
    dmat = jnp.where(mask, jnp.exp(log_g[:, None, None] * jnp.where(mask, dist, 0.0)), 0.0)
    scores = jnp.einsum('bhncd,bhnsd->bhncs', qc, kc) * dmat[None, :, None]
    o_intra = jnp.einsum('bhncs,bhnse->bhnce', scores, vc)
    k_dec = jnp.exp(log_g[:, None] * (RET_CHUNK - 1 - i)[None, :])
    inc = jnp.einsum('bhncd,hc,bhnce->nbhde', kc, k_dec, vc).astype(jnp.float32)
    g_chunk = jnp.exp(log_g * RET_CHUNK)[None, :, None, None]

    def step(s, inc_n):
        return g_chunk * s + inc_n, s

    _, s_before = lax.scan(step, s0.astype(jnp.float32), inc)
    q_dec = jnp.exp(log_g[:, None] * (i + 1.0)[None, :])
    o_cross = jnp.einsum('bhncd,hc,nbhde->bhnce', qc, q_dec, s_before)
    return (o_intra + o_cross).reshape(b, h, t, dv)


def _retention_state(k, v, log_g, reverse):
    t = k.shape[1]
    pos = jnp.arange(t, dtype=jnp.float32)
    dist = pos if reverse else (t - 1 - pos)
    w = jnp.exp(log_g[:, None] * dist[None, :])
    return jnp.einsum('bthd,ht,bthe->bhde', k, w.astype(k.dtype), v).astype(jnp.float32)


def _retention_branch(q, k, v, g, log_g, s_f, s_b):
    qt, kt, vt = (a.transpose(0, 2, 1, 3) for a in (q, k, v))
    fwd = _retention_dir(qt, kt, vt, log_g[0], s_f, exclusive=False)
    bwd = _retention_dir(qt[:, :, ::-1], kt[:, :, ::-1], vt[:, :, ::-1], log_g[1], s_b, exclusive=True)[:, :, ::-1]
    o = _layer_norm((fwd + bwd).transpose(0, 2, 1, 3))
    return o.reshape(g.shape).astype(g.dtype) * jax.nn.silu(g)


def _conv_branch(gate_b, gate_c, x_in, g, w):
    u = gate_c * x_in
    up = jnp.pad(u, ((0, 0), (1, 1), (0, 0)))
    conv = w[0] * up[:, :-2] + w[1] * up[:, 1:-1] + w[2] * up[:, 2:]
    return gate_b * conv * jax.nn.silu(g)


def _diff_attention(q, k_all, v_all, lam):
    b, t, h, _, d = q.shape
    nb = t // Q_BLOCK
    qb = (q * d ** -0.5).reshape(b, nb, Q_BLOCK, h, 2, d).transpose(1, 0, 2, 3, 4, 5)

    def block(qi):
        s = jnp.einsum('bqhjd,bkhjd->bhjqk', qi, k_all).astype(jnp.float32)
        p = jax.nn.softmax(s, axis=-1)
        a = p[:, :, 0] - lam * p[:, :, 1]
        return jnp.einsum('bhqk,bkhe->bqhe', a.astype(v_all.dtype), v_all)

    o = lax.map(block, qb)
    return o.transpose(1, 0, 2, 3, 4).reshape(b, t, h, -1)


def _diff_branch(q, k_all, v_all, g, lam, lam_init):
    o = _rms_norm(_diff_attention(q, k_all, v_all, lam)) * (1.0 - lam_init)
    return o.reshape(g.shape).astype(g.dtype) * jax.nn.silu(g)


def _merge(y_ret, y_conv, y_att, gate_logits, w_r, w_c, w_a, w_o):
    g_r, g_c, g_a = jnp.split(jax.nn.sigmoid(gate_logits), N_BRANCH, axis=-1)
    m = g_r * (y_ret @ w_r) + g_c * (y_conv @ w_c) + g_a * (y_att @ w_a)
    return m @ w_o


def setup_inputs(seed: int = 0) -> dict:
    key = jax.random.key(seed)
    ks = jax.random.split(key, 16)
    f32 = jnp.float32
    d = D_MODEL

    def nrm(k, shape, s):
        return jax.random.normal(k, shape, f32) * s

    head = jnp.arange(RET_HEADS, dtype=f32)
    decay_init = jnp.log(-jnp.log1p(-jnp.power(2.0, -5.0 - head)))
    return {
        'x': nrm(ks[0], (BATCH, SEQ, d), 1.0),
        'c': nrm(ks[1], (BATCH, d), 1.0),
        'ctx': nrm(ks[2], (BATCH, CTX_LEN, d), 1.0),
        'c_ctx': nrm(ks[3], (d,), 1.0),
        'w_mod': nrm(ks[4], (DEPTH, d, 3 * d), 0.5 * d ** -0.5),
        'b_mod': nrm(ks[5], (DEPTH, 3 * d), 0.02),
        'w_in': nrm(ks[6], (DEPTH, d, IN_COLS), d ** -0.5),
        'ret_decay': decay_init + nrm(ks[7], (DEPTH, 2, RET_HEADS), 0.05),
        'conv_w': nrm(ks[8], (DEPTH, CONV_K, CONV_W), CONV_K ** -0.5),
        'diff_lambda': nrm(ks[9], (DEPTH, 4, ATT_DH), 0.1),
        'w_ret_out': nrm(ks[10], (DEPTH, RET_W, d), RET_W ** -0.5 * DEEPNORM_BETA),
        'w_conv_out': nrm(ks[11], (DEPTH, CONV_W, d), CONV_W ** -0.5 * DEEPNORM_BETA),
        'w_att_out': nrm(ks[12], (DEPTH, ATT_W, d), ATT_W ** -0.5 * DEEPNORM_BETA),
        'w_out': nrm(ks[13], (DEPTH, d, d), d ** -0.5 * DEEPNORM_BETA),
        'ln_g': 1.0 + nrm(ks[14], (DEPTH, d), 0.02),
        'ln_b': nrm(ks[15], (DEPTH, d), 0.02),
    }


def reference(x, c, ctx, c_ctx, w_mod, b_mod, w_in, ret_decay, conv_w, diff_lambda,
              w_ret_out, w_conv_out, w_att_out, w_out, ln_g, ln_b):
    f32 = jnp.float32
    bsz, t_lat, _ = x.shape
    t_ctx = ctx.shape[1]
    n_rows = t_lat // GRID_W
    rows = jnp.repeat(jnp.arange(n_rows, dtype=jnp.int32), GRID_W)
    cols = jnp.tile(jnp.arange(GRID_W, dtype=jnp.int32), n_rows)
    cos_r, sin_r = _axial_rope_tables(rows, cols, RET_DK)
    cos_a, sin_a = _axial_rope_tables(rows, cols, ATT_DH)
    xc = ctx
    for l in range(DEPTH):
        last = l == DEPTH - 1
        lam_init = 0.8 - 0.6 * math.exp(-0.3 * l)
        log_g = -jnp.exp(ret_decay[l].astype(f32))
        lv = diff_lambda[l].astype(f32)
        lam = jnp.exp(jnp.sum(lv[0] * lv[1])) - jnp.exp(jnp.sum(lv[2] * lv[3])) + lam_init

        shift, scale, gate = jnp.split(jax.nn.silu(c) @ w_mod[l] + b_mod[l], 3, axis=-1)
        shift_c, scale_c, gate_c = jnp.split(jax.nn.silu(c_ctx) @ w_mod[l] + b_mod[l], 3, axis=-1)
        h = _layer_norm(x) * (1.0 + scale[:, None]) + shift[:, None]
        hc = _layer_norm(xc) * (1.0 + scale_c) + shift_c

        if last:
            pc = _split(hc @ w_in[l][:, :KV_COLS], KV_SIZES)
        else:
            pc = _split(hc @ w_in[l], IN_SIZES)
        ck_r = pc[0].reshape(bsz, t_ctx, RET_HEADS, RET_DK)
        cv_r = pc[1].reshape(bsz, t_ctx, RET_HEADS, RET_DV)
        ck_a = pc[2].reshape(bsz, t_ctx, ATT_HEADS, 2, ATT_DH)
        cv_a = pc[3].reshape(bsz, t_ctx, ATT_HEADS, 2 * ATT_DH)
        s_f = _retention_state(ck_r, cv_r, log_g[0], reverse=False)
        s_b = _retention_state(ck_r, cv_r, log_g[1], reverse=True)

        rk, rv, ak, av, rq, rg, aq, ag, cb, cc, cx, cg, mg = _split(h @ w_in[l], IN_SIZES)
        q_r = _apply_rope(rq.reshape(bsz, t_lat, RET_HEADS, RET_DK), cos_r, sin_r) * RET_DK ** -0.5
        k_r = _apply_rope(rk.reshape(bsz, t_lat, RET_HEADS, RET_DK), cos_r, sin_r)
        y_ret = _retention_branch(q_r, k_r, rv.reshape(bsz, t_lat, RET_HEADS, RET_DV), rg, log_g, s_f, s_b)
        y_conv = _conv_branch(cb, cc, cx, cg, conv_w[l])
        q_a = _apply_rope(aq.reshape(bsz, t_lat, ATT_HEADS, 2, ATT_DH), cos_a, sin_a)
        k_all = jnp.concatenate([ck_a, _apply_rope(ak.reshape(bsz, t_lat, ATT_HEADS, 2, ATT_DH), cos_a, sin_a)], axis=1)
        v_all = jnp.concatenate([cv_a, av.reshape(bsz, t_lat, ATT_HEADS, 2 * ATT_DH)], axis=1)
        y_att = _diff_branch(q_a, k_all, v_all, ag, lam, lam_init)
        out = _merge(y_ret, y_conv, y_att, mg, w_ret_out[l], w_conv_out[l], w_att_out[l], w_out[l])
        x_new = _layer_norm(DEEPNORM_ALPHA * x + gate[:, None] * out, ln_g[l], ln_b[l])

        if not last:
            crq, crg, caq, cag, ccb, ccc, ccx, ccg, cmg = pc[4:]
            zero = jnp.zeros_like(s_f)
            cq_r = crq.reshape(bsz, t_ctx, RET_HEADS, RET_DK) * RET_DK ** -0.5
            cy_ret = _retention_branch(cq_r, ck_r, cv_r, crg, log_g, zero, zero)
            cy_conv = _conv_branch(ccb, ccc, ccx, ccg, conv_w[l])
            cy_att = _diff_branch(caq.reshape(bsz, t_ctx, ATT_HEADS, 2, ATT_DH), ck_a, cv_a, cag, lam, lam_init)
            out_c = _merge(cy_ret, cy_conv, cy_att, cmg, w_ret_out[l], w_conv_out[l], w_att_out[l], w_out[l])
            xc = _layer_norm(DEEPNORM_ALPHA * xc + gate_c * out_c, ln_g[l], ln_b[l])
        x = x_new
    return x
```

```python
import math
import os
import contextlib
import numpy as np
import concourse.bass as bass
import concourse.mybir as mybir
from concourse.bass_utils import run_bass_kernel_spmd

F32 = mybir.dt.float32
BF16 = mybir.dt.bfloat16
AF = mybir.ActivationFunctionType
ALU = mybir.AluOpType
AX = mybir.AxisListType

ALLENG = ("sp", "pe", "act", "dve", "pool")

D = 1024
KC = 8
NT = 34
TOK = NT * 128
NCTX_T = 2
DEPTH = 4
LN_EPS = 1e-6
ALPHA = (2 * DEPTH) ** 0.25
IN_COLS = 15360


PSUM_NAMES = {"cps", "rpA", "rpB", "rpT", "rpS", "rpO", "rpO2", "rpI", "apA", "apT", "apS", "apO", "gpA", "bpR", "bpT", "bpO",
              "ppT", "pm", "pg"}


def I(name, *a, **kw):
    return (name, a, kw)


class Prog:
    def __init__(self, nc):
        self.nc = nc
        self.ops = []
        self.last_w = {}
        self.readers = {}
        self.dsem_cnt = {}
        self.last_eng = {}
        self.last_dma = {}
        self.pending = {}
        self.trace = [] if os.environ.get("PTRACE") else None

    def op(self, eng, fn, reads=(), writes=(), dsem=None):
        i = len(self.ops)
        deps = {}
        for r in reads:
            j = self.last_w.get(r)
            if j is not None:
                deps[j] = True
            rn = r[0] if isinstance(r, tuple) else r
            if rn in PSUM_NAMES:
                for j in self.readers.get(r, {}).values():
                    if self.ops[j]["eng"] != eng:
                        deps.setdefault(j, False)
        for w in writes:
            j = self.last_w.get(w)
            if j is not None:
                deps.setdefault(j, False)
            for j in self.readers.get(w, {}).values():
                deps.setdefault(j, False)
        pb = self.pending.pop(eng, None)
        if pb:
            for j in pb:
                deps.setdefault(j, False)
        o = dict(eng=eng, fn=fn, deps=deps, dsem=dsem, dcount=None, sig=None)
        if dsem is not None:
            c = self.dsem_cnt.get(dsem, 0) + 16
            self.dsem_cnt[dsem] = c
            o["dcount"] = c
            self.last_dma[dsem] = i
        self.ops.append(o)
        self.last_eng[eng] = i
        rk = eng if dsem is None else ("d", dsem)
        for r in reads:
            self.readers.setdefault(r, {})[rk] = i
        for w in writes:
            self.last_w[w] = i
            self.readers[w] = {}
        return i

    def barrier(self):
        s = set(self.last_eng.values()) | set(self.last_dma.values())
        for e in ALLENG:
            self.pending[e] = set(s) | self.pending.get(e, set())

    def emit(self):
        nc = self.nc
        ops = self.ops
        need_sig = [False] * len(ops)
        for i, o in enumerate(ops):
            for j, raw in o["deps"].items():
                pj = ops[j]
                if pj["dsem"] is not None:
                    continue
                if pj["eng"] == o["eng"] and o["dsem"] is None:
                    if raw and o["eng"] != "pe":
                        need_sig[j] = True
                else:
                    need_sig[j] = True
        cnt = {e: 0 for e in ALLENG}
        for i, o in enumerate(ops):
            if o["dsem"] is None and need_sig[i]:
                cnt[o["eng"]] += 1
                o["sig"] = cnt[o["eng"]]
        dkeys = sorted(self.dsem_cnt.keys(), key=str)
        with contextlib.ExitStack() as st:
            esem = {e: st.enter_context(nc.semaphore("s_" + e)) for e in ALLENG}
            dsem = {k: st.enter_context(nc.semaphore("d_%d" % n)) for n, k in enumerate(dkeys)}
            block = st.enter_context(nc.Block())
            per_eng = {e: [] for e in ALLENG}
            for i, o in enumerate(ops):
                per_eng[o["eng"]].append(i)

            def run(engname, eng):
                waited = {}
                for i in per_eng[engname]:
                    o = ops[i]
                    for j, raw in sorted(o["deps"].items()):
                        pj = ops[j]
                        if pj["dsem"] is not None:
                            key = ("d", pj["dsem"])
                            val = pj["dcount"]
                            sem = dsem[pj["dsem"]]
                        else:
                            if pj["sig"] is None:
                                continue
                            if pj["eng"] == engname and o["dsem"] is None and not (raw and engname != "pe"):
                                continue
                            key = ("e", pj["eng"])
                            val = pj["sig"]
                            sem = esem[pj["eng"]]
                        if waited.get(key, 0) >= val:
                            continue
                        waited[key] = val
                        eng.wait_ge(sem, val)
                        if self.trace is not None:
                            self.trace.append((engname, "WAIT", key, val))
                    nm, a_, kw_ = o["fn"]
                    ins = getattr(eng, nm)(*a_, **kw_)
                    if self.trace is not None:
                        self.trace.append((engname, nm, i, o["sig"], o["dsem"], o["dcount"]))
                    if o["dsem"] is not None:
                        ins.then_inc(dsem[o["dsem"]], 16)
                    elif o["sig"] is not None:
                        ins.then_inc(esem[engname], 1)
                if engname == "sp":
                    for k in dkeys:
                        eng.wait_ge(dsem[k], self.dsem_cnt[k])

            block.sync(lambda e: run("sp", e))
            block.tensor(lambda e: run("pe", e))
            block.scalar(lambda e: run("act", e))
            block.vector(lambda e: run("dve", e))
            block.gpsimd(lambda e: run("pool", e))
        self.stats = dict(n_ops=len(ops), sig=cnt, ndsem=len(dkeys))


class Rot:
    def __init__(self, tiles, name):
        self.tiles = tiles
        self.name = name
        self.i = 0

    def next(self):
        k = self.i % len(self.tiles)
        self.i += 1
        return self.tiles[k], (self.name, k)


def build_program(n_layers=DEPTH, dbg=False, upto=None):
    nc = bass.Bass("TRN2", target_bir_lowering=False)

    def din(name, shape, dt=F32):
        return nc.dram_tensor(name, list(shape), dt, kind="ExternalInput").ap()

    xin = din("xin", [TOK, D])
    cvec = din("cvec", [128, KC, 2])
    w_mod = din("w_mod", [DEPTH, D, 3 * D])
    bmodT = din("bmodT", [DEPTH, 128, 16])
    bgate = din("bgate", [DEPTH, 128, D])
    w_in = din("w_in", [DEPTH, D, IN_COLS])
    rdec = din("rdec", [128, 32])
    convw = din("convw", [DEPTH, 128, 8, 3])
    dlam = din("dlam", [128, DEPTH * 256])
    w_ro = din("w_ro", [DEPTH, D, D])
    w_co = din("w_co", [DEPTH, D, D])
    w_ao = din("w_ao", [DEPTH, D, D])
    w_oo = din("w_oo", [DEPTH, D, D])
    lng = din("lng", [DEPTH, 128, D])
    lnb = din("lnb", [DEPTH, 128, D])
    ident_d = din("ident", [128, 128])
    cosR = din("cosR", [TOK, 256])
    sinR = din("sinR", [TOK, 256])
    cosA = din("cosA", [TOK, 128])
    sinA = din("sinA", [TOK, 128])
    dconst = din("dconst", [128, 4, 128])
    posv_d = din("posv", [128, 4])
    out_d = nc.dram_tensor("out", [TOK - 256, D], F32, kind="ExternalOutput").ap()
    skind = dict(kind="ExternalOutput") if dbg else {}
    yT_d = nc.dram_tensor("yT_s", [3, D, TOK], BF16, **skind).ap()
    gates_d = nc.dram_tensor("gates_s", [TOK, 3 * D], BF16, **skind).ap()
    xbuf_d = nc.dram_tensor("xbuf_s", [TOK, D], F32, **skind).ap()

    hT_dbg = nc.dram_tensor("hT_dbg", [128, KC, TOK], BF16, kind="ExternalOutput").ap() if dbg else None
    P = Prog(nc)
    G = contextlib.ExitStack()

    uid = [0]

    def mk(stack, kind):
        def f(name, shape, dt):
            uid[0] += 1
            name = "%s_u%d" % (name, uid[0])
            if kind == "sb":
                return stack.enter_context(nc.sbuf_tensor(name, list(shape), dt))
            return stack.enter_context(nc.psum_tensor(name, list(shape), dt))
        return f

    gsb = mk(G, "sb")

    with G:
        hT = gsb("hT", [128, KC, TOK], BF16)
        identf = gsb("identf", [128, 128], F32)
        identb = gsb("identb", [128, 128], BF16)
        dcon = gsb("dcon", [128, 4, 128], F32)
        posv = gsb("posv", [128, 4], F32)
        lg_all = gsb("lg_all", [128, 32], F32)
        scv = gsb("scv", [128, KC, 2], F32)
        scvb = gsb("scvb", [128, KC, 2], BF16)
        screp = gsb("screp", [128, KC, 2, 128], BF16)
        onesb = gsb("onesb", [128, 128], F32)
        mhalf = gsb("mhalf", [128, 2], F32)
        modT = gsb("modT", [128, 16, 2], F32)
        gate_holder = [None]
        DT_all = gsb("DT_all", [128, 4, 128], F32)
        rv_all = gsb("rv_all", [128, 4, 8], F32)
        lam_t = gsb("lam_t", [128, 4], F32)
        smallr = Rot([gsb("small%d" % i, [128, 8], F32) for i in range(6)], "small")
        statr = Rot([gsb("stat%d" % i, [128, 2, 6], F32) for i in range(3)], "stat")
        xnb_holder = [None]

        def dma(out, in_, reads, writes, key, eng="sp"):
            P.op(eng, I("dma_start", out=out, in_=in_), reads=reads, writes=writes, dsem=key)

        dma(identf[:], ident_d, [], ["identf"], "c0")
        dma(dcon[:], dconst, [], ["dcon"], "c1")
        dma(posv[:], posv_d, [], ["posv"], "c2")
        dma(lg_all[:], rdec, [], ["lg_all"], "c3")
        dma(scv[:], cvec, [], ["scv"], "c5")
        P.op("pool", I("tensor_copy", out=identb[:], in_=identf[:]), reads=["identf"], writes=["identb"])
        P.op("pool", I("memset", onesb[:], 1.0), writes=["onesb"])
        P.op("pool", I("memset", mhalf[:], -0.5), writes=["mhalf"])
        P.op("act", I("activation", out=lg_all[:], in_=lg_all[:], func=AF.Exp), reads=["lg_all"], writes=["lg_all"])
        P.op("dve", I("tensor_scalar", out=lg_all[:], in0=lg_all[:], scalar1=-1.0, scalar2=None, op0=ALU.mult),
             reads=["lg_all"], writes=["lg_all"])
        P.op("act", I("activation", out=scv[:], in_=scv[:], func=AF.Silu), reads=["scv"], writes=["scv"])
        P.op("dve", I("tensor_copy", out=scvb[:], in_=scv[:]), reads=["scv"], writes=["scvb"])
        for kc in range(KC):
            for wh in range(2):
                P.op("act", I("activation", out=screp[:, kc, wh, :], in_=onesb[:], func=AF.Identity,
                                                                   scale=scv[:, kc, wh:wh + 1]),
                     reads=["onesb", "scv"], writes=["screp"])

        def ln_stats(src_ap_fn, n, rsrc, width):
            st_t, st_r = statr.next()
            sm, sm_r = smallr.next()
            nh = (width + 511) // 512
            for i in range(nh):
                w0, w1 = i * 512, min(width, (i + 1) * 512)
                P.op("dve", I("bn_stats", out=st_t[:, i, :], in_=src_ap_fn(w0, w1)),
                     reads=rsrc, writes=[st_r])
            P.op("dve", I("bn_aggr", out=sm[:, 2:4], in_=st_t[:, 0:nh, :]), reads=[st_r], writes=[sm_r])
            P.op("dve", I("tensor_scalar", out=sm[:, 4:5], in0=sm[:, 3:4], scalar1=LN_EPS, scalar2=None, op0=ALU.add),
                 reads=[sm_r], writes=[sm_r])
            P.op("pool", I("tensor_tensor", out=sm[:, 0:1], in0=sm[:, 4:5], in1=mhalf[:, 0:1], op=ALU.pow), reads=[sm_r, "mhalf"], writes=[sm_r])
            P.op("dve", I("scalar_tensor_tensor", out=sm[:, 1:2], in0=sm[:, 2:3], scalar=-1.0, in1=sm[:, 0:1],
                                                         op0=ALU.mult, op1=ALU.mult), reads=[sm_r], writes=[sm_r])
            return sm, sm_r

        def make_hT(t, xsrc, xres, pTa, pTa_r, pTb, pTb_r):
            wh = 1 if t < NCTX_T else 0
            sm, sm_r = ln_stats(lambda a, b: xsrc[:, a:b], D, [xres], D)
            xnb, xnb_r = xnb_holder[0].next()
            P.op("act", I("activation", out=xnb[:], in_=xsrc[:], func=AF.Identity, bias=sm[:, 1:2], scale=sm[:, 0:1]),
                 reads=[xres, sm_r], writes=[xnb_r])
            for kc in range(KC):
                pT, pT_r = (pTa, pTa_r) if kc < 4 else (pTb, pTb_r)
                P.op("pe", I("transpose", out=pT[:, kc, :], in_=xnb[:, kc * 128:(kc + 1) * 128], identity=identb[:]),
                     reads=[xnb_r, "identb"], writes=[pT_r])
            for kc in range(KC):
                dst = hT[:, kc, t * 128:(t + 1) * 128]
                if kc < 4:
                    P.op("dve", I("tensor_scalar", out=dst, in0=pTa[:, kc, :], scalar1=modT[:, 8 + kc, wh:wh + 1],
                                  scalar2=modT[:, kc, wh:wh + 1], op0=ALU.mult, op1=ALU.add),
                         reads=[pTa_r, "modT"], writes=[("hT", t)])
                else:
                    P.op("act", I("activation", out=dst, in_=pTb[:, kc, :], func=AF.Identity,
                                  bias=modT[:, kc, wh:wh + 1], scale=modT[:, 8 + kc, wh:wh + 1]),
                         reads=[pTb_r, "modT"], writes=[("hT", t)])

        cast_i = [0]

        def load_w(src, ncols, stg_rot, wbf_rot, key):
            stg, stg_r = stg_rot.next()
            wbf, wbf_r = wbf_rot.next()
            dma(stg[:, :, 0:ncols], src.rearrange("(c p) n -> p c n", p=128), [], [stg_r], (key, stg_r[1]))
            engs = ("pool", "dve", "act", "pool")
            for q in range(4):
                en = engs[(q + cast_i[0]) % 4]
                if en == "act":
                    P.op(en, I("copy", out=wbf[:, 2 * q:2 * q + 2, 0:ncols], in_=stg[:, 2 * q:2 * q + 2, 0:ncols]),
                         reads=[stg_r], writes=[wbf_r])
                else:
                    P.op(en, I("tensor_copy", out=wbf[:, 2 * q:2 * q + 2, 0:ncols], in_=stg[:, 2 * q:2 * q + 2, 0:ncols]),
                         reads=[stg_r], writes=[wbf_r])
            cast_i[0] += 1
            return wbf, wbf_r

        def proj_tok(ps, ps_r, t, wbf, wbf_r, c0, c1):
            for kc in range(KC):
                P.op("pe", I("matmul", ps[:, 0:c1 - c0], lhsT=hT[:, kc, t * 128:(t + 1) * 128], rhs=wbf[:, kc, c0:c1],
                                                    start=(kc == 0), stop=(kc == KC - 1)),
                     reads=[("hT", t), wbf_r], writes=[ps_r])

        def setup_mod(l, S):
            sb = mk(S, "sb")
            ps = mk(S, "ps")
            stg = sb("mstg", [128, KC, 512], F32)
            stgb = sb("mstgb", [128, KC, 512], BF16)
            bmt = sb("bmt", [128, 16], F32)
            pm_full = ps("pm", [128, 16, 32], F32)
            pm = pm_full[:, :, 0:2]
            dma(bmt[:], bmodT[l], [], ["bmt"], "m0")
            for jb in range(4):
                dma(stg[:], w_mod[l][:, jb * 512:(jb + 1) * 512].rearrange("(c p) n -> p c n", p=128), [], ["mstg"], "m1")
                P.op("pool", I("tensor_copy", out=stgb[:], in_=stg[:]), reads=["mstg"], writes=["mstgb"])
                for fc in range(4):
                    j = jb * 4 + fc
                    for kc in range(KC):
                        P.op("pe", I("matmul", pm[:, j, :], lhsT=stgb[:, kc, fc * 128:(fc + 1) * 128],
                                                                          rhs=scvb[:, kc, :], start=(kc == 0), stop=(kc == KC - 1)),
                             reads=["mstgb", "scvb"], writes=["pm"])
            for wh in range(2):
                P.op("dve", I("tensor_tensor", out=modT[:, :, wh], in0=pm[:, :, wh], in1=bmt[:], op=ALU.add),
                     reads=["pm", "bmt"], writes=["modT"])
            P.op("dve", I("tensor_scalar", out=modT[:, 8:16, :], in0=modT[:, 8:16, :], scalar1=1.0, scalar2=None, op0=ALU.add),
                 reads=["modT"], writes=["modT"])

        def setup_gate(l, S):
            gate_rep = gate_holder[0]
            sb = mk(S, "sb")
            ps = mk(S, "ps")
            stg = sb("gstg", [128, KC, 512], F32)
            stgb = sb("gstgb", [128, KC, 512], BF16)
            bg = sb("bg", [128, D], F32)
            pg = ps("pg", [128, 512], F32)
            dma(bg[:], bgate[l], [], ["bg"], "g0")
            for jb in range(2):
                dma(stg[:], w_mod[l][:, (4 + jb) * 512:(5 + jb) * 512].rearrange("(c p) n -> p c n", p=128), [], ["gstg"], "g1")
                P.op("pool", I("tensor_copy", out=stgb[:], in_=stg[:]), reads=["gstg"], writes=["gstgb"])
                for wh in range(2):
                    for kc in range(KC):
                        P.op("pe", I("matmul", pg[:], lhsT=screp[:, kc, wh, :], rhs=stgb[:, kc, :],
                                     start=(kc == 0), stop=(kc == KC - 1)),
                             reads=["screp", "gstgb"], writes=["pg"])
                    P.op("dve", I("tensor_tensor", out=gate_rep[:, wh, jb * 512:(jb + 1) * 512], in0=pg[:],
                                  in1=bg[:, jb * 512:(jb + 1) * 512], op=ALU.add),
                         reads=["pg", "bg"], writes=["gate_rep"])

        def setup_rest(l, S):
            sb = mk(S, "sb")
            ps = mk(S, "ps")
            e1 = sb("e1", [128, 128], F32)
            e2 = sb("e2", [128, 128], F32)
            pl = sb("pl", [128, 2, 64], F32)
            for h in range(4):
                lgf = lg_all[:, l * 8 + h:l * 8 + h + 1]
                lgb = lg_all[:, l * 8 + 4 + h:l * 8 + 4 + h + 1]
                P.op("act", I("activation", out=e1[:], in_=dcon[:, 0, :], func=AF.Exp, scale=lgf),
                     reads=["dcon", "lg_all"], writes=["e1"])
                P.op("pool", I("tensor_tensor", out=e1[:], in0=e1[:], in1=dcon[:, 1, :], op=ALU.mult), reads=["e1", "dcon"], writes=["e1"])
                P.op("act", I("activation", out=e2[:], in_=dcon[:, 2, :], func=AF.Exp, scale=lgb),
                     reads=["dcon", "lg_all"], writes=["e2"])
                P.op("pool", I("tensor_tensor", out=e2[:], in0=e2[:], in1=dcon[:, 3, :], op=ALU.mult), reads=["e2", "dcon"], writes=["e2"])
                P.op("pool", I("tensor_tensor", out=e1[:], in0=e1[:], in1=e2[:], op=ALU.add), reads=["e1", "e2"], writes=["e1"])
                P.op("act", I("mul", out=DT_all[:, h, :], in_=e1[:], mul=0.0625), reads=["e1"], writes=["DT_all"])
                for k, (pc, lgx, post) in enumerate([(0, lgf, 0.0625), (1, lgb, 0.0625), (2, lgf, None), (3, lgb, None)]):
                    P.op("act", I("activation", out=rv_all[:, h, k:k + 1], in_=posv[:, pc:pc + 1],
                                                                             func=AF.Exp, scale=lgx),
                         reads=["posv", "lg_all"], writes=["rv_all"])
                    if post is not None:
                        P.op("dve", I("tensor_scalar", out=rv_all[:, h, k:k + 1], in0=rv_all[:, h, k:k + 1],
                                                                                 scalar1=post, scalar2=None, op0=ALU.mult),
                             reads=["rv_all"], writes=["rv_all"])
                for k, lgx in ((4, lgf), (5, lgb)):
                    P.op("act", I("activation", out=rv_all[:, h, k:k + 1], in_=lgx, func=AF.Exp, scale=128.0),
                         reads=["lg_all"], writes=["rv_all"])
            lam_init = 0.8 - 0.6 * math.exp(-0.3 * l)
            dl_all = sb("dl_l", [128, 256], F32)
            dma(dl_all[:], dlam[:, l * 256:(l + 1) * 256], [], ["dl_all"], "c4")
            for i in range(2):
                a0 = dl_all[:, i * 128:i * 128 + 64]
                a1 = dl_all[:, i * 128 + 64:i * 128 + 128]
                P.op("dve", I("tensor_tensor", out=pl[:, i, :], in0=a0, in1=a1, op=ALU.mult),
                     reads=["dl_all"], writes=["pl"])
                P.op("dve", I("reduce_sum", out=lam_t[:, i:i + 1], in_=pl[:, i, :], axis=AX.X), reads=["pl"], writes=["lam_t"])
                P.op("act", I("activation", out=lam_t[:, i:i + 1], in_=lam_t[:, i:i + 1], func=AF.Exp), reads=["lam_t"], writes=["lam_t"])
            P.op("dve", I("tensor_tensor", out=lam_t[:, 2:3], in0=lam_t[:, 0:1], in1=lam_t[:, 1:2], op=ALU.subtract),
                 reads=["lam_t"], writes=["lam_t"])
            P.op("dve", I("tensor_scalar", out=lam_t[:, 3:4], in0=lam_t[:, 2:3], scalar1=lam_init, scalar2=-1.0, op0=ALU.add, op1=ALU.mult),
                 reads=["lam_t"], writes=["lam_t"])
            return lam_init

        def phase_conv(l, S):
            sb = mk(S, "sb")
            ps = mk(S, "ps")
            stg_rot = Rot([sb("cstg%d" % i, [128, KC, 512], F32) for i in range(2)], "cstg")
            wbf_rot = Rot([sb("cwbf%d" % i, [128, KC, 512], BF16) for i in range(2)], "cwbf")
            cw = sb("cw", [128, 8, 3], F32)
            u = sb("u_full", [128, TOK], F32)
            cf = sb("cf_full", [128, TOK], F32)
            tmpr = Rot([sb("ctmp%d" % i, [128, 512], F32) for i in range(4)], "ctmp")
            ychr = Rot([sb("ych%d" % i, [128, 512], BF16) for i in range(2)], "ych")
            psr = Rot([ps("cps%d" % i, [128, 512], F32) for i in range(4)], "cps")
            dma(cw[:], convw[l], [], ["cw"], "cv0")
            chunks = [(c * 512, min(TOK, (c + 1) * 512)) for c in range(9)]
            for g in range(8):
                wbf, wbf_r = load_w(w_in[l][:, 8192 + g * 512:8192 + (g + 1) * 512], 512, stg_rot, wbf_rot, "wc")

                def projT(pst, pst_r, col, t0, t1):
                    for kc in range(KC):
                        P.op("pe", I("matmul", pst[:, 0:t1 - t0], lhsT=wbf[:, kc, col * 128:(col + 1) * 128], rhs=hT[:, kc, t0:t1],
                                                            start=(kc == 0), stop=(kc == KC - 1)),
                             reads=[("hT", tt) for tt in range(t0 // 128, t1 // 128)] + [wbf_r], writes=[pst_r])
                for (t0, t1) in chunks:
                    n = t1 - t0
                    pa, pa_r = psr.next()
                    pb, pb_r = psr.next()
                    projT(pa, pa_r, 1, t0, t1)
                    projT(pb, pb_r, 2, t0, t1)
                    tm, tm_r = tmpr.next()
                    P.op("act", I("copy", out=tm[:, 0:n], in_=pa[:, 0:n]), reads=[pa_r], writes=[tm_r])
                    P.op("dve", I("tensor_tensor", out=u[:, t0:t1], in0=pb[:, 0:n], in1=tm[:, 0:n], op=ALU.mult),
                         reads=[pb_r, tm_r], writes=["u_full"])
                for (s0, s1) in ((0, 256), (256, TOK)):
                    P.op("act", I("mul", out=cf[:, s0:s1], in_=u[:, s0:s1], mul=cw[:, g, 1:2]),
                         reads=["u_full", "cw"], writes=["cf_full"])
                    P.op("dve", I("scalar_tensor_tensor", out=cf[:, s0 + 1:s1], in0=u[:, s0:s1 - 1], scalar=cw[:, g, 0:1],
                                                                                in1=cf[:, s0 + 1:s1], op0=ALU.mult, op1=ALU.add),
                         reads=["u_full", "cw", "cf_full"], writes=["cf_full"])
                    P.op("dve", I("scalar_tensor_tensor", out=cf[:, s0:s1 - 1], in0=u[:, s0 + 1:s1], scalar=cw[:, g, 2:3],
                                                                                in1=cf[:, s0:s1 - 1], op0=ALU.mult, op1=ALU.add),
                         reads=["u_full", "cw", "cf_full"], writes=["cf_full"])
                for (t0, t1) in chunks:
                    n = t1 - t0
                    pa, pa_r = psr.next()
                    pb, pb_r = psr.next()
                    projT(pa, pa_r, 0, t0, t1)
                    projT(pb, pb_r, 3, t0, t1)
                    sg, sg_r = tmpr.next()
                    tt_, tt_r = tmpr.next()
                    ych, ych_r = ychr.next()
                    P.op("act", I("activation", out=sg[:, 0:n], in_=pb[:, 0:n], func=AF.Silu), reads=[pb_r], writes=[sg_r])
                    P.op("dve", I("tensor_tensor", out=tt_[:, 0:n], in0=pa[:, 0:n], in1=sg[:, 0:n], op=ALU.mult),
                         reads=[pa_r, sg_r], writes=[tt_r])
                    P.op("pool", I("tensor_tensor", out=ych[:, 0:n], in0=tt_[:, 0:n], in1=cf[:, t0:t1], op=ALU.mult),
                         reads=[tt_r, "cf_full"], writes=[ych_r])
                    dma(yT_d[1][g * 128:(g + 1) * 128, t0:t1], ych[:, 0:n], [ych_r], ["yT_d"], ("yc", ych_r[1]), eng="pool")

        rope_lim = [99]

        def rope(src1, src2, src_r, c_ap, s_ap, tab_r, dst1, dst2, dst_r, tmps):
            (t1, t1r), (t2, t2r), (t3, t3r), (t4, t4r) = tmps
            re_ = os.environ.get("ROPE_ENG", "pool")
            lst = [
                ("dve", I("tensor_tensor", out=t1, in0=src1, in1=c_ap, op=ALU.mult), [src_r, tab_r], [t1r]),
                ("dve", I("tensor_tensor", out=t2, in0=src2, in1=s_ap, op=ALU.mult), [src_r, tab_r], [t2r]),
                ("dve", I("tensor_tensor", out=t3, in0=src1, in1=s_ap, op=ALU.mult), [src_r, tab_r], [t3r]),
                ("dve", I("tensor_tensor", out=t4, in0=src2, in1=c_ap, op=ALU.mult), [src_r, tab_r], [t4r]),
                (re_, I("tensor_tensor", out=dst1, in0=t1, in1=t2, op=ALU.subtract), [t1r, t2r], [dst_r]),
                (re_, I("tensor_tensor", out=dst2, in0=t3, in1=t4, op=ALU.add), [t3r, t4r], [dst_r]),
            ]
            for en, ins, rd, wr in lst[:rope_lim[0]]:
                P.op(en, ins, reads=rd, writes=wr)

        def phase_ret(l, S):
            sb = mk(S, "sb")
            ps = mk(S, "ps")
            stg_rot = Rot([sb("rstg%d" % i, [128, KC, 512], F32) for i in range(1)], "rstg")
            wbf_rot = Rot([sb("rwbf%d" % i, [128, KC, 512], BF16) for i in range(2)], "rwbf")
            Sb_all = sb("Sb_all", [128, NT, 2, 256], BF16)
            Sst = sb("Sst", [128, 2, 256], F32)
            Sbf = sb("Sbf", [128, 2, 256], BF16)
            tabr = Rot([sb("rtab%d" % i, [128, 2, 256], F32) for i in range(int(os.environ.get("TAB_SLOTS", "2")))], "rtab")
            tmp = [Rot([sb("rt%d_%d" % (k, i), [128, 2, 128], F32) for i in range(int(os.environ.get("TMP_SLOTS", "2")))], "rt%d" % k) for k in range(4)]
            qkb_r = Rot([sb("rqkb%d" % i, [128, 2, 256], BF16) for i in range(3)], "rqkb")
            vbr = Rot([sb("rvb%d" % i, [128, 2, 256], BF16) for i in range(3)], "rvb")
            sgr = Rot([sb("rsg%d" % i, [128, 256], BF16) for i in range(3)], "rsg")
            qkTr = Rot([sb("rqkT%d" % i, [128, 4, 128], BF16) for i in range(2)], "rqkT")
            ATr = Rot([sb("rAT%d" % i, [128, 128], BF16) for i in range(2)], "rAT")
            o_r = Rot([sb("ro%d" % i, [128, 256], F32) for i in range(3)], "ro")
            ybr = Rot([sb("ryb%d" % i, [128, 256], BF16) for i in range(3)], "ryb")
            yTr = Rot([sb("ryT%d" % i, [128, 2, 128], BF16) for i in range(2)], "ryT")
            pAr = Rot([ps("rpA%d" % i, [128, 512], F32) for i in range(int(os.environ.get("PA_SLOTS", "2")))], "rpA")
            pBr = Rot([ps("rpB%d" % i, [128, 512], F32) for i in range(1)], "rpB")
            pTr = Rot([ps("rpT%d" % i, [128, 8, 128], BF16) for i in range(1)], "rpT")
            pSr = Rot([ps("rpS%d" % i, [128, 512], F32) for i in range(1)], "rpS")
            pOr = Rot([ps("rpO%d" % i, [128, 512], F32) for i in range(1)], "rpO")
            pO2r = Rot([ps("rpO2%d" % i, [128, 512], F32) for i in range(1)], "rpO2")
            pIr = Rot([ps("rpI%d" % i, [128, 2, 256], F32) for i in range(1)], "rpI")

            def load_tab(n):
                tb, tb_r = tabr.next()
                dma(tb[:, 0, :], cosR[n * 128:(n + 1) * 128, :], [], [tb_r], ("rtab", tb_r[1]))
                dma(tb[:, 1, :], sinR[n * 128:(n + 1) * 128, :], [], [tb_r], ("rtab", tb_r[1]))
                return tb, tb_r

            gwbf_rot = Rot([sb("rgw%d" % i, [128, KC, 512], BF16) for i in range(2)], "rgw")
            gsr = Rot([sb("rgs%d" % i, [128, 512], BF16) for i in range(2)], "rgs")
            gate_items = [(cb, t) for cb in range(6) for t in range(NT)]
            gate_w = {}
            gi = [0]

            def gate_load(cb):
                gate_w[cb] = load_w(w_in[l][:, 12288 + cb * 512:12288 + (cb + 1) * 512], 512, stg_rot, gwbf_rot, "wg")

            def gate_step(k):
                for _ in range(k):
                    if gi[0] >= len(gate_items):
                        return
                    cb, t = gate_items[gi[0]]
                    gi[0] += 1
                    if t == 0 and cb + 1 < 6:
                        gate_load(cb + 1)
                    gw, gw_r = gate_w[cb]
                    pg_, pg_r = pBr.next()
                    proj_tok(pg_, pg_r, t, gw, gw_r, 0, 512)
                    gs, gs_r = gsr.next()
                    P.op("act", I("activation", out=gs[:], in_=pg_[:], func=AF.Sigmoid), reads=[pg_r], writes=[gs_r])
                    dma(gates_d[t * 128:(t + 1) * 128, cb * 512:(cb + 1) * 512], gs[:], [gs_r], ["gates_d"], ("gg", gs_r[1]), eng="pool")

            gate_load(0)
            for h in range(4):
                wkv, wkv_r = load_w(w_in[l][:, h * 1024:h * 1024 + 512], 512, stg_rot, wbf_rot, "wr")
                wqg, wqg_r = load_w(w_in[l][:, h * 1024 + 512:h * 1024 + 1024], 512, stg_rot, wbf_rot, "wr")
                ub = rv_all[:, h, 3:4]
                uf = rv_all[:, h, 2:3]
                wf = rv_all[:, h, 0:1]
                wb_ = rv_all[:, h, 1:2]
                g128f = rv_all[:, h, 4:5]
                g128b = rv_all[:, h, 5:6]
                P.op("pool", I("memset", Sst[:], 0.0), writes=["Sst"])
                order = [1, 0] + list(range(NT - 1, 1, -1))

                def p1_front(n):
                    pa, pa_r = pAr.next()
                    proj_tok(pa, pa_r, n, wkv, wkv_r, 0, 512)
                    tb, tb_r = load_tab(n)
                    qkb, qkb_rr = qkb_r.next()
                    tm = [tmp[k].next() for k in range(4)]
                    rope(pa[:, 0:128], pa[:, 128:256], pa_r, tb[:, 0, 0:128], tb[:, 1, 0:128], tb_r,
                         qkb[:, 1, 0:128], qkb[:, 1, 128:256], qkb_rr, [(tm[k][0][:, 0, :], tm[k][1]) for k in range(4)])
                    vb, vb_r = vbr.next()
                    P.op("act", I("activation", out=vb[:, 1, :], in_=pa[:, 256:512], func=AF.Identity, scale=ub),
                         reads=[pa_r, "rv_all"], writes=[vb_r])
                    return (qkb, qkb_rr, vb, vb_r)

                def p1_back(n, ctx):
                    qkb, qkb_rr, vb, vb_r = ctx
                    P.op("act", I("copy", out=Sb_all[:, n, :, :], in_=Sst[:]), reads=["Sst"], writes=[("Sb_all", n)])
                    pI, pI_r = pIr.next()
                    for dc in range(2):
                        P.op("pe", I("matmul", pI[:, dc, :], lhsT=qkb[:, 1, dc * 128:(dc + 1) * 128], rhs=vb[:, 1, :], start=True, stop=True),
                             reads=[qkb_rr, vb_r], writes=[pI_r])
                    P.op("dve", I("scalar_tensor_tensor", out=Sst[:], in0=Sst[:], scalar=g128b, in1=pI[:], op0=ALU.mult, op1=ALU.add),
                         reads=["Sst", pI_r, "rv_all"], writes=["Sst"])

                ctxs = {0: p1_front(order[0])}
                for i, n in enumerate(order):
                    if i + 1 < len(order):
                        ctxs[i + 1] = p1_front(order[i + 1])
                    p1_back(n, ctxs.pop(i))
                    gate_step(2 if i % 2 == 0 else 1)

                P.op("pool", I("memset", Sst[:], 0.0), writes=["Sst"])
                P.op("pool", I("memset", Sbf[:], 0.0), writes=["Sbf"])

                def p2_front(n):
                    pa, pa_r = pAr.next()
                    pb, pb_r = pBr.next()
                    proj_tok(pa, pa_r, n, wkv, wkv_r, 0, 512)
                    proj_tok(pb, pb_r, n, wqg, wqg_r, 0, 512)
                    tb, tb_r = load_tab(n)
                    qkb, qkb_rr = qkb_r.next()
                    tm = [tmp[k].next() for k in range(4)]
                    rope(pb[:, 0:128], pb[:, 128:256], pb_r, tb[:, 0, 0:128], tb[:, 1, 0:128], tb_r,
                         qkb[:, 0, 0:128], qkb[:, 0, 128:256], qkb_rr, [(tm[k][0][:, 0, :], tm[k][1]) for k in range(4)])
                    rope(pa[:, 0:128], pa[:, 128:256], pa_r, tb[:, 0, 0:128], tb[:, 1, 0:128], tb_r,
                         qkb[:, 1, 0:128], qkb[:, 1, 128:256], qkb_rr, [(tm[k][0][:, 1, :], tm[k][1]) for k in range(4)])
                    vb, vb_r = vbr.next()
                    P.op("act", I("copy", out=vb[:, 0, :], in_=pa[:, 256:512]), reads=[pa_r], writes=[vb_r])
                    P.op("act", I("activation", out=vb[:, 1, :], in_=pa[:, 256:512], func=AF.Identity, scale=uf),
                         reads=[pa_r, "rv_all"], writes=[vb_r])
                    sg, sg_r = sgr.next()
                    P.op("act", I("activation", out=sg[:], in_=pb[:, 256:512], func=AF.Silu), reads=[pb_r], writes=[sg_r])
                    pT, pT_r = pTr.next()
                    for i in range(4):
                        P.op("pe", I("transpose", out=pT[:, i, :], in_=qkb[:, i // 2, (i % 2) * 128:(i % 2 + 1) * 128], identity=identb[:]),
                             reads=[qkb_rr, "identb"], writes=[pT_r])
                    qkT, qkT_r = qkTr.next()
                    P.op("dve", I("tensor_copy", out=qkT[:], in_=pT[:, 0:4, :]), reads=[pT_r], writes=[qkT_r])
                    return (qkb, qkb_rr, vb, vb_r, sg, sg_r, qkT, qkT_r)

                def p2_main(n, ctx):
                    qkb, qkb_rr, vb, vb_r, sg, sg_r, qkT, qkT_r = ctx
                    pS, pS_r = pSr.next()
                    for dc in range(2):
                        P.op("pe", I("matmul", pS[:, 0:128], lhsT=qkT[:, 2 + dc, :], rhs=qkT[:, dc, :], start=(dc == 0), stop=(dc == 1)),
                             reads=[qkT_r], writes=[pS_r])
                    AT, AT_r = ATr.next()
                    P.op("dve", I("tensor_tensor", out=AT[:], in0=pS[:, 0:128], in1=DT_all[:, h, :], op=ALU.mult),
                         reads=[pS_r, "DT_all"], writes=[AT_r])
                    pO, pO_r = pOr.next()
                    pO2, pO2_r = pO2r.next()
                    for dc in range(2):
                        P.op("pe", I("matmul", pO[:, 256:512], lhsT=qkT[:, dc, :], rhs=Sbf[:, dc, :], start=(dc == 0), stop=(dc == 1)),
                             reads=[qkT_r, "Sbf"], writes=[pO_r])
                    for dc in range(2):
                        P.op("pe", I("matmul", pO2[:, 0:256], lhsT=qkT[:, dc, :], rhs=Sb_all[:, n, dc, :], start=(dc == 0), stop=(dc == 1)),
                             reads=[qkT_r, ("Sb_all", n)], writes=[pO2_r])
                    pI, pI_r = pIr.next()
                    for dc in range(2):
                        P.op("pe", I("matmul", pI[:, dc, :], lhsT=qkb[:, 1, dc * 128:(dc + 1) * 128], rhs=vb[:, 1, :], start=True, stop=True),
                             reads=[qkb_rr, vb_r], writes=[pI_r])
                    P.op("pe", I("matmul", pO[:, 0:256], lhsT=AT[:], rhs=vb[:, 0, :], start=True, stop=True),
                         reads=[AT_r, vb_r], writes=[pO_r])
                    P.op("dve", I("scalar_tensor_tensor", out=Sst[:], in0=Sst[:], scalar=g128f, in1=pI[:], op0=ALU.mult, op1=ALU.add),
                         reads=["Sst", pI_r, "rv_all"], writes=["Sst"])
                    P.op("act", I("copy", out=Sbf[:], in_=Sst[:]), reads=["Sst"], writes=["Sbf"])
                    o1, o1_r = o_r.next()
                    P.op("act", I("copy", out=o1[:], in_=pO[:, 0:256]), reads=[pO_r], writes=[o1_r])
                    P.op("dve", I("scalar_tensor_tensor", out=o1[:], in0=pO[:, 256:512], scalar=wf, in1=o1[:], op0=ALU.mult, op1=ALU.add),
                         reads=[pO_r, o1_r, "rv_all"], writes=[o1_r])
                    P.op("dve", I("scalar_tensor_tensor", out=o1[:], in0=pO2[:, 0:256], scalar=wb_, in1=o1[:], op0=ALU.mult, op1=ALU.add),
                         reads=[pO2_r, o1_r, "rv_all"], writes=[o1_r])
                    sm, sm_r = ln_stats(lambda a, b, o1=o1: o1[:, a:b], 256, [o1_r], 256)
                    P.op("act", I("activation", out=o1[:], in_=o1[:], func=AF.Identity, bias=sm[:, 1:2], scale=sm[:, 0:1]),
                         reads=[o1_r, sm_r], writes=[o1_r])
                    yb, yb_r = ybr.next()
                    P.op("pool", I("tensor_tensor", out=yb[:], in0=o1[:], in1=sg[:], op=ALU.mult), reads=[o1_r, sg_r], writes=[yb_r])
                    return (yb, yb_r)

                def p2_tail(n, ctx):
                    yb, yb_r = ctx
                    pT2, pT2_r = pTr.next()
                    for dc in range(2):
                        P.op("pe", I("transpose", out=pT2[:, 4 + dc, :], in_=yb[:, dc * 128:(dc + 1) * 128], identity=identb[:]),
                             reads=[yb_r, "identb"], writes=[pT2_r])
                    yT, yT_r = yTr.next()
                    P.op("act", I("copy", out=yT[:], in_=pT2[:, 4:6, :]), reads=[pT2_r], writes=[yT_r])
                    dma(yT_d[0][h * 256:(h + 1) * 256, n * 128:(n + 1) * 128].rearrange("(c p) t -> p c t", p=128), yT[:], [yT_r], ["yT_d"],
                        ("yr", yT_r[1]), eng="pool")

                fctx = {0: p2_front(0)}
                tctx = {}
                for n in range(NT):
                    if n + 1 < NT:
                        fctx[n + 1] = p2_front(n + 1)
                    tctx[n] = p2_main(n, fctx.pop(n))
                    if n - 1 >= 0:
                        p2_tail(n - 1, tctx.pop(n - 1))
                p2_tail(NT - 1, tctx.pop(NT - 1))
            gate_step(len(gate_items))

        def phase_att(l, S, lam_init):
            sb = mk(S, "sb")
            ps = mk(S, "ps")
            stg_rot = Rot([sb("astg%d" % i, [128, KC, 512], F32) for i in range(1)], "astg")
            wbf_rot = Rot([sb("awbf%d" % i, [128, KC, 512], BF16) for i in range(1)], "awbf")
            qT = sb("aqT", [128, TOK], BF16)
            kT = sb("akT", [128, TOK], BF16)
            Vx = sb("aVx", [128, NT, 130], BF16)
            sgA = sb("asg", [128, NT, 128], BF16)
            PTr = Rot([sb("aPT%d" % i, [128, NT, 256], BF16) for i in range(2)], "aPT")
            tabr = Rot([sb("atab%d" % i, [128, 2, 128], F32) for i in range(2)], "atab")
            tmp = [Rot([sb("at%d_%d" % (k, i), [128, 4, 32], F32) for i in range(2)], "at%d" % k) for k in range(4)]
            qkbr = Rot([sb("aqkb%d" % i, [128, 256], BF16) for i in range(3)], "aqkb")
            O0r = Rot([sb("aO0%d" % i, [128, 128], F32) for i in range(4)], "aO0")
            ar = Rot([sb("aa%d" % i, [128, 128], F32) for i in range(4)], "aa")
            jr = Rot([sb("aj%d" % i, [128, 128], F32) for i in range(2)], "aj")
            ybr = Rot([sb("ayb%d" % i, [128, 128], BF16) for i in range(6)], "ayb")
            yTr = Rot([sb("ayT%d" % i, [128, 128], BF16) for i in range(2)], "ayT")
            apX = ps("apX", [128, 1024], F32)
            apY = ps("apY", [128, 1024], F32)
            pAr = Rot([apX[:, 0:512], apX[:, 512:1024]], "apA")
            pTr = Rot([ps("apT%d" % i, [128, 8, 128], BF16) for i in range(2)], "apT")
            pS_slots = [(apX, [("apA", 0), ("apA", 1)]), (apY, [("apS", 0)])]
            pS_i = [0]
            pOr = Rot([ps("apO%d" % i, [128, 512], F32) for i in range(2)], "apO")
            P.op("pool", I("memset", Vx[:], 1.0), writes=["aVx"])
            for h in range(8):
                wbf, wbf_r = load_w(w_in[l][:, 4096 + h * 512:4096 + (h + 1) * 512], 512, stg_rot, wbf_rot, "wa")
                def a1_front(t):
                    pa, pa_r = pAr.next()
                    proj_tok(pa, pa_r, t, wbf, wbf_r, 0, 512)
                    tb, tb_r = tabr.next()
                    dma(tb[:, 0, :], cosA[t * 128:(t + 1) * 128, :], [], [tb_r], ("atab", tb_r[1]))
                    dma(tb[:, 1, :], sinA[t * 128:(t + 1) * 128, :], [], [tb_r], ("atab", tb_r[1]))
                    qkb, qkb_r = qkbr.next()
                    tm = [tmp[k].next() for k in range(4)]
                    src = pa[:, 0:256].rearrange("p (a b c) -> p a b c", a=4, b=2, c=32)
                    dst = qkb[:].rearrange("p (a b c) -> p a b c", a=4, b=2, c=32)
                    cA = tb[:, 0, :].rearrange("p (a c) -> p a c", a=4, c=32)
                    sA = tb[:, 1, :].rearrange("p (a c) -> p a c", a=4, c=32)
                    rope(src[:, :, 0, :], src[:, :, 1, :], pa_r, cA, sA, tb_r, dst[:, :, 0, :], dst[:, :, 1, :], qkb_r,
                         [(tm[k][0][:], tm[k][1]) for k in range(4)])
                    P.op("act", I("copy", out=Vx[:, t, 0:128], in_=pa[:, 256:384]), reads=[pa_r], writes=["aVx"])
                    P.op("act", I("activation", out=sgA[:, t, :], in_=pa[:, 384:512], func=AF.Silu), reads=[pa_r], writes=["asg"])
                    return (qkb, qkb_r)

                def a1_back(t, ctx):
                    qkb, qkb_r = ctx
                    pT, pT_r = pTr.next()
                    for i in range(2):
                        P.op("pe", I("transpose", out=pT[:, i, :], in_=qkb[:, i * 128:(i + 1) * 128], identity=identb[:]),
                             reads=[qkb_r, "identb"], writes=[pT_r])
                    P.op("dve", I("tensor_copy", out=qT[:, t * 128:(t + 1) * 128], in_=pT[:, 0, :]), reads=[pT_r], writes=["aqT"])
                    P.op("dve", I("tensor_copy", out=kT[:, t * 128:(t + 1) * 128], in_=pT[:, 1, :]), reads=[pT_r], writes=["akT"])

                actx = {0: a1_front(0)}
                for t in range(NT):
                    if t + 1 < NT:
                        actx[t + 1] = a1_front(t + 1)
                    a1_back(t, actx.pop(t))
                steps = [(b, j) for b in range(NT // 2) for j in range(2)]
                if os.environ.get("ATT_SKIP2"):
                    steps = []
                qstate = {}

                def A_items(b, j):
                    nk = NCTX_T if b == 0 else NT
                    q0 = b * 256
                    PT, PT_r = PTr.next()
                    items = []

                    def mk_item(kg, ng):
                        def f():
                            pSt, pS_rs = pS_slots[pS_i[0] % 2]
                            pS_i[0] += 1
                            pSv = pSt[:, :].rearrange("p (a b) -> p a b", a=4, b=256)
                            for i in range(ng):
                                kt = kg + i
                                P.op("pe", I("matmul", pSv[:, i, :], lhsT=kT[64 * j:64 * j + 64, kt * 128:(kt + 1) * 128],
                                             rhs=qT[64 * j:64 * j + 64, q0:q0 + 256], start=True, stop=True),
                                     reads=["akT", "aqT"], writes=pS_rs)
                            if ng == 4 and not os.environ.get("EXP512"):
                                P.op("act", I("activation", out=PT[:, kg:kg + 4, :], in_=pSv[:, 0:4, :], func=AF.Exp, scale=0.125),
                                     reads=pS_rs, writes=[PT_r])
                            else:
                                for i0 in range(0, ng, 2):
                                    P.op("act", I("activation", out=PT[:, kg + i0:kg + i0 + 2, :], in_=pSv[:, i0:i0 + 2, :], func=AF.Exp, scale=0.125),
                                         reads=pS_rs, writes=[PT_r])
                        return f
                    for kg in range(0, nk, 4):
                        items.append(mk_item(kg, min(4, nk - kg)))
                    return items, (PT, PT_r, nk)

                def B_items(b, j, PTinfo):
                    PT, PT_r, nk = PTinfo
                    items = []
                    for qh in range(2):
                        qt = 2 * b + qh
                        if j == 0:
                            qstate[qt] = (O0r.next(), ar.next())
                        (O0, O0_r), (a_, a_r) = qstate[qt]
                        pO, pO_r = pOr.next()

                        def mk_pv(k0, k1, lastchunk, qt=qt, qh=qh, O0=O0, O0_r=O0_r, a_=a_, a_r=a_r, pO=pO, pO_r=pO_r):
                            def f():
                                for kt in range(k0, k1):
                                    P.op("pe", I("matmul", pO[:, 0:129], lhsT=PT[:, kt, qh * 128:(qh + 1) * 128], rhs=Vx[:, kt, 0:129],
                                                 start=(kt == 0), stop=(kt == nk - 1)),
                                         reads=[PT_r, "aVx"], writes=[pO_r])
                                if lastchunk:
                                    sm, sm_r = smallr.next()
                                    P.op("dve", I("reciprocal", out=sm[:, 0:1], in_=pO[:, 128:129]), reads=[pO_r], writes=[sm_r])
                                    if j == 0:
                                        P.op("dve", I("tensor_scalar", out=O0[:], in0=pO[:, 0:128], scalar1=sm[:, 0:1], scalar2=None, op0=ALU.mult),
                                             reads=[pO_r, sm_r], writes=[O0_r])
                                    else:
                                        P.op("dve", I("tensor_tensor", out=sm[:, 1:2], in0=sm[:, 0:1], in1=lam_t[:, 3:4], op=ALU.mult),
                                             reads=[sm_r, "lam_t"], writes=[sm_r])
                                        P.op("dve", I("scalar_tensor_tensor", out=a_[:], in0=pO[:, 0:128], scalar=sm[:, 1:2], in1=O0[:],
                                                      op0=ALU.mult, op1=ALU.add),
                                             reads=[pO_r, sm_r, O0_r], writes=[a_r])
                                        jk, jk_r = jr.next()
                                        sm2, sm2_r = smallr.next()
                                        P.op("dve", I("scalar_tensor_tensor", out=jk[:], in0=a_[:], scalar=1.0, in1=a_[:], op0=ALU.mult, op1=ALU.mult,
                                                      accum_out=sm2[:, 0:1]),
                                             reads=[a_r], writes=[jk_r, sm2_r])
                                        P.op("dve", I("tensor_scalar", out=sm2[:, 1:2], in0=sm2[:, 0:1], scalar1=1.0 / 128.0, scalar2=LN_EPS,
                                                      op0=ALU.mult, op1=ALU.add), reads=[sm2_r], writes=[sm2_r])
                                        P.op("pool", I("tensor_tensor", out=sm2[:, 3:4], in0=sm2[:, 1:2], in1=mhalf[:, 0:1], op=ALU.pow),
                                             reads=[sm2_r, "mhalf"], writes=[sm2_r])
                                        yb, yb_r = ybr.next()
                                        P.op("dve", I("scalar_tensor_tensor", out=yb[:], in0=a_[:], scalar=sm2[:, 3:4], in1=sgA[:, qt, :],
                                                      op0=ALU.mult, op1=ALU.mult),
                                             reads=[a_r, sm2_r, "asg"], writes=[yb_r])
                                        qstate[qt] = (yb, yb_r)
                            return f
                        cs = 8
                        for k0 in range(0, nk, cs):
                            k1 = min(nk, k0 + cs)
                            items.append(mk_pv(k0, k1, k1 == nk))
                    return items

                def tail_pe(qt):
                    yb, yb_r = qstate.pop(qt)
                    pT, pT_r = pTr.next()
                    P.op("pe", I("transpose", out=pT[:, 0, :], in_=yb[:], identity=identb[:]), reads=[yb_r, "identb"], writes=[pT_r])
                    yT, yT_r = yTr.next()
                    P.op("dve", I("tensor_scalar", out=yT[:], in0=pT[:, 0, :], scalar1=float(1.0 - lam_init), scalar2=None, op0=ALU.mult),
                         reads=[pT_r], writes=[yT_r])
                    dma(yT_d[2][h * 128:(h + 1) * 128, qt * 128:(qt + 1) * 128], yT[:], [yT_r], ["yT_d"], ("ya", yT_r[1]), eng="pool")

                prevB = []
                tails_now = []
                for si in range(len(steps) + 1):
                    if si < len(steps):
                        b, j = steps[si]
                        Ai, PTinfo = A_items(b, j)
                    else:
                        Ai, PTinfo = [], None
                    for k in range(max(len(Ai), len(prevB))):
                        if k < len(Ai):
                            Ai[k]()
                        if k < len(prevB):
                            prevB[k]()
                    for tq in tails_now:
                        tail_pe(tq)
                    tails_now = []
                    if si >= 1 and steps[si - 1][1] == 1:
                        tails_now += [2 * steps[si - 1][0], 2 * steps[si - 1][0] + 1]
                    if not steps:
                        break
                    prevB = B_items(b, j, PTinfo) if si < len(steps) else []
                for tq in tails_now:
                    tail_pe(tq)

        def phase_gates(l, S):
            sb = mk(S, "sb")
            ps = mk(S, "ps")
            stg_rot = Rot([sb("gstg%d" % i, [128, KC, 512], F32) for i in range(2)], "gstg")
            wbf_rot = Rot([sb("gwbf%d" % i, [128, KC, 512], BF16) for i in range(2)], "gwbf")
            gsr = Rot([sb("ggs%d" % i, [128, 512], BF16) for i in range(3)], "ggs")
            pAr = Rot([ps("gpA%d" % i, [128, 512], F32) for i in range(3)], "gpA")
            for cb in range(6):
                wbf, wbf_r = load_w(w_in[l][:, 12288 + cb * 512:12288 + (cb + 1) * 512], 512, stg_rot, wbf_rot, "wg")
                for t in range(NT):
                    pa, pa_r = pAr.next()
                    proj_tok(pa, pa_r, t, wbf, wbf_r, 0, 512)
                    gs, gs_r = gsr.next()
                    P.op("act", I("activation", out=gs[:], in_=pa[:], func=AF.Sigmoid), reads=[pa_r], writes=[gs_r])
                    dma(gates_d[t * 128:(t + 1) * 128, cb * 512:(cb + 1) * 512], gs[:], [gs_r], ["gates_d"], ("gg", gs_r[1]), eng="pool")

        def phase_B(l, S, last):
            sb = mk(S, "sb")
            ps = mk(S, "ps")
            wts = []
            wtl = [sb("bw%d" % wi, [128, KC, D], BF16) for wi in range(4)]
            with contextlib.ExitStack() as S2:
                sb2 = mk(S2, "sb")
                stg_rot = Rot([sb2("bstg%d" % i, [128, KC, 512], F32) for i in range(2)], "bstg")
                for wi, wsrc in enumerate((w_ro, w_co, w_ao, w_oo)):
                    wt = wtl[wi]
                    for half in range(2):
                        stg, stg_r = stg_rot.next()
                        dma(stg[:], wsrc[l][:, half * 512:(half + 1) * 512].rearrange("(c p) n -> p c n", p=128), [], [stg_r], ("bw", stg_r[1]))
                        for q in range(4):
                            en = ("pool", "dve", "act", "pool")[q]
                            if en == "act":
                                P.op(en, I("copy", out=wt[:, 2 * q:2 * q + 2, half * 512:(half + 1) * 512],
                                                                                       in_=stg[:, 2 * q:2 * q + 2, :]),
                                     reads=[stg_r], writes=["bw%d" % wi])
                            else:
                                P.op(en, I("tensor_copy", out=wt[:, 2 * q:2 * q + 2, half * 512:(half + 1) * 512],
                                                                                              in_=stg[:, 2 * q:2 * q + 2, :]),
                                     reads=[stg_r], writes=["bw%d" % wi])
                    wts.append((wt, "bw%d" % wi))
                P.barrier()
            gate_rep = sb("gate_rep", [128, 2, D], F32)
            gate_holder[0] = gate_rep
            with contextlib.ExitStack() as S3:
                setup_gate(l, S3)
                P.barrier()
            xnb_holder[0] = Rot([sb("xnb%d" % i, [128, D], BF16) for i in range(2)], "xnb")
            lngt = sb("lngt", [128, D], F32)
            lnbt = sb("lnbt", [128, D], F32)
            dma(lngt[:], lng[l], [], ["lngt"], "bl0")
            dma(lnbt[:], lnb[l], [], ["lnbt"], "bl1")
            gtr = Rot([sb("bgt%d" % i, [128, 3 * D], BF16) for i in range(1)], "bgt")
            xtr = Rot([sb("bxt%d" % i, [128, D], F32) for i in range(1)], "bxt")
            ytr = Rot([sb("byt%d" % i, [128, 3, KC, 128], BF16) for i in range(1)], "byt")
            mt = sb("bmt_", [128, 512], F32)
            t2 = sb("bt2", [128, 512], F32)
            mb = sb("bmb", [128, D], BF16)
            mT = sb("bmT", [128, KC, 128], BF16)
            xnr = Rot([sb("bxn%d" % i, [128, D], F32) for i in range(2)], "bxn")
            pRr = Rot([ps("bpR%d" % i, [128, 512], F32) for i in range(2)], "bpR")
            pTr = Rot([ps("bpT%d" % i, [128, 8, 128], BF16) for i in range(4)], "bpT")
            pOr = Rot([ps("bpO%d" % i, [128, 512], F32) for i in range(2)], "bpO")
            xsrc_d = xin if l == 0 else xbuf_d
            ztr = Rot([sb("bzt%d" % i, [128, D], F32) for i in range(2)], "bzt")
            mbr = Rot([mb, sb("bmb2", [128, D], BF16)], "bmb")

            def b_front(t):
                yt, yt_r = ytr.next()
                for br in range(3):
                    dma(yt[:, br, :, :], yT_d[br][:, t * 128:(t + 1) * 128].rearrange("(c p) t -> p c t", p=128), ["yT_d"], [yt_r], ("by", br))
                gt, gt_r = gtr.next()
                dma(gt[:], gates_d[t * 128:(t + 1) * 128, :], ["gates_d"], [gt_r], "bg")
                mbt, mbt_r = mbr.next()

                def half(nb):
                    def f():
                        for br in range(3):
                            wt, wt_r = wts[br]
                            pR, pR_r = pRr.next()
                            for kc in range(KC):
                                P.op("pe", I("matmul", pR[:], lhsT=yt[:, br, kc, :], rhs=wt[:, kc, nb * 512:(nb + 1) * 512],
                                             start=(kc == 0), stop=(kc == KC - 1)),
                                     reads=[yt_r, wt_r], writes=[pR_r])
                            gsl = gt[:, br * D + nb * 512:br * D + (nb + 1) * 512]
                            dst = mt if br == 0 else t2
                            dst_r = "bmt_" if br == 0 else "bt2"
                            P.op("dve", I("tensor_tensor", out=dst[:], in0=pR[:], in1=gsl, op=ALU.mult),
                                 reads=[pR_r, gt_r], writes=[dst_r])
                            if br == 1:
                                P.op("dve", I("tensor_tensor", out=mt[:], in0=mt[:], in1=t2[:], op=ALU.add), reads=["bmt_", "bt2"], writes=["bmt_"])
                            if br == 2:
                                P.op("dve", I("tensor_tensor", out=mbt[:, nb * 512:(nb + 1) * 512], in0=mt[:], in1=t2[:], op=ALU.add),
                                     reads=["bmt_", "bt2"], writes=[mbt_r])
                    return f
                return (mbt, mbt_r), [half(0), half(1)]

            def b_back(t, ctx):
                mbt, mbt_r = ctx
                wh = 1 if t < NCTX_T else 0
                xn, xn_r = xnr.next()
                zt, zt_r = ztr.next()
                xt, xt_r = xtr.next()
                st = {}

                def part1():
                    dma(xt[:], xsrc_d[t * 128:(t + 1) * 128, :], [("xbuf", t)], [xt_r], ("bx", xt_r[1]))
                    pTa, pTa_r = pTr.next()
                    pTb, pTb_r = pTr.next()
                    for kc in range(KC):
                        pT, pT_r = (pTa, pTa_r) if kc < 4 else (pTb, pTb_r)
                        P.op("pe", I("transpose", out=pT[:, kc, :], in_=mbt[:, kc * 128:(kc + 1) * 128], identity=identb[:]),
                             reads=[mbt_r, "identb"], writes=[pT_r])
                    P.op("act", I("copy", out=mT[:, 0:4, :], in_=pTa[:, 0:4, :]), reads=[pTa_r], writes=["bmT"])
                    P.op("dve", I("tensor_copy", out=mT[:, 4:8, :], in_=pTb[:, 4:8, :]), reads=[pTb_r], writes=["bmT"])
                    wo, wo_r = wts[3]
                    for nb in range(2):
                        pO, pO_r = pOr.next()
                        for kc in range(KC):
                            P.op("pe", I("matmul", pO[:], lhsT=mT[:, kc, :], rhs=wo[:, kc, nb * 512:(nb + 1) * 512],
                                         start=(kc == 0), stop=(kc == KC - 1)),
                                 reads=["bmT", wo_r], writes=[pO_r])
                        P.op("dve", I("tensor_tensor", out=zt[:, nb * 512:(nb + 1) * 512], in0=pO[:],
                                      in1=gate_rep[:, wh, nb * 512:(nb + 1) * 512], op=ALU.mult),
                             reads=[pO_r, "gate_rep"], writes=[zt_r])
                    P.op("dve", I("scalar_tensor_tensor", out=zt[:], in0=xt[:], scalar=float(ALPHA), in1=zt[:], op0=ALU.mult, op1=ALU.add),
                         reads=[xt_r, zt_r], writes=[zt_r])
                    st["sm"] = ln_stats(lambda a, b: zt[:, a:b], D, [zt_r], D)

                def part2():
                    sm, sm_r = st["sm"]
                    P.op("act", I("activation", out=xn[:], in_=zt[:], func=AF.Identity, bias=sm[:, 1:2], scale=sm[:, 0:1]),
                         reads=[zt_r, sm_r], writes=[xn_r])
                    P.op("pool", I("tensor_tensor", out=xn[:], in0=xn[:], in1=lngt[:], op=ALU.mult), reads=[xn_r, "lngt"], writes=[xn_r])
                    P.op("dve", I("tensor_tensor", out=xn[:], in0=xn[:], in1=lnbt[:], op=ALU.add), reads=[xn_r, "lnbt"], writes=[xn_r])
                    if not last:
                        dma(xbuf_d[t * 128:(t + 1) * 128, :], xn[:], [xn_r], [("xbuf", t)], "bxo", eng="pool")
                    elif t >= NCTX_T:
                        dma(out_d[(t - NCTX_T) * 128:(t - NCTX_T + 1) * 128, :], xn[:], [xn_r], [], "bxo", eng="pool")

                def part3():
                    if not last:
                        make_hT(t, xn, xn_r, *pTr.next(), *pTr.next())
                return [part1, part2, part3]

            c0, h0 = b_front(0)
            h0[0]()
            h0[1]()
            bctx = {0: c0}
            for t in range(NT):
                halves = [lambda: None, lambda: None]
                if t + 1 < NT:
                    bctx[t + 1], halves = b_front(t + 1)
                parts = b_back(t, bctx.pop(t))
                parts[0]()
                halves[0]()
                parts[1]()
                halves[1]()
                parts[2]()

        with contextlib.ExitStack() as S:
            if upto != "const":
                setup_mod(0, S)
            sb = mk(S, "sb")
            ps = mk(S, "ps")
            xnb_holder[0] = Rot([sb("xnb%d" % i, [128, D], BF16) for i in range(2)], "xnb")
            xtr0 = Rot([sb("pxt%d" % i, [128, D], F32) for i in range(2)], "pxt")
            pTr0 = Rot([ps("ppT%d" % i, [128, 8, 128], BF16) for i in range(4)], "ppT")
            for t in range((NT if not (upto or "").startswith("pro") or upto == "pro" else int(upto[3:])) if upto not in ("const", "mod") else 0):
                xt, xt_r = xtr0.next()
                dma(xt[:], xin[t * 128:(t + 1) * 128, :], [], [xt_r], ("px", xt_r[1]))
                make_hT(t, xt, xt_r, *pTr0.next(), *pTr0.next())
            P.barrier()
            if dbg:
                for kc in range(KC):
                    dma(hT_dbg[:, kc, :], hT[:, kc, :], [("hT", t) for t in range(NT)], [], "dbgh")
                P.barrier()
        for l in range(n_layers):
            last = (l == DEPTH - 1)
            if upto in ("const", "mod") or (upto or "").startswith("pro"):
                break
            with contextlib.ExitStack() as S:
                lam_init = setup_rest(l, S)
                P.barrier()
            if upto == "rest":
                break
            if upto in (None, "conv", "B"):
                with contextlib.ExitStack() as S:
                    phase_conv(l, S)
                    P.barrier()
            if upto == "conv":
                break
            if upto in (None, "ret", "B"):
                with contextlib.ExitStack() as S:
                    phase_ret(l, S)
                    P.barrier()
            if upto == "ret":
                break
            if upto in (None, "att", "B"):
                with contextlib.ExitStack() as S:
                    phase_att(l, S, lam_init)
                    P.barrier()
            if upto == "att":
                break
            if upto == "gates":
                with contextlib.ExitStack() as S:
                    phase_gates(l, S)
                    P.barrier()
            if upto == "gates":
                break
            if not last:
                with contextlib.ExitStack() as S:
                    setup_mod(l + 1, S)
                    P.barrier()
            with contextlib.ExitStack() as S:
                phase_B(l, S, last)
                P.barrier()
        P.emit()
    return nc, P


def _rope_tables(head_dim, reps):
    n_freq = head_dim // 4
    inv = (10000.0 ** (-np.arange(n_freq, dtype=np.float32) / np.float32(n_freq))).astype(np.float32)
    rows = np.repeat(np.arange(64, dtype=np.float32), 64)
    cols = np.tile(np.arange(64, dtype=np.float32), 64)
    ang = np.concatenate([rows[:, None] * inv, cols[:, None] * inv], axis=-1).astype(np.float32)
    c = np.cos(ang).astype(np.float32)
    s = np.sin(ang).astype(np.float32)
    half = head_dim // 2
    c = np.concatenate([np.ones((256, half), np.float32), c], 0)
    s = np.concatenate([np.zeros((256, half), np.float32), s], 0)
    return np.ascontiguousarray(np.tile(c, (1, reps))), np.ascontiguousarray(np.tile(s, (1, reps)))


def _perm_cols():
    seg = {}
    names = ["rk", "rv", "ak", "av", "rq", "rg", "aq", "ag", "cb", "cc", "cx", "cg"]
    for i, n in enumerate(names):
        seg[n] = i * 1024
    idx = []
    for h in range(4):
        for n in ("rk", "rv", "rq", "rg"):
            idx.extend(range(seg[n] + h * 256, seg[n] + (h + 1) * 256))
    for h in range(8):
        for n in ("aq", "ak", "av", "ag"):
            idx.extend(range(seg[n] + h * 128, seg[n] + (h + 1) * 128))
    for g in range(8):
        for n in ("cb", "cc", "cx", "cg"):
            idx.extend(range(seg[n] + g * 128, seg[n] + (g + 1) * 128))
    idx.extend(range(12288, 15360))
    return np.asarray(idx, dtype=np.int64)


def _host_inputs(x, c, ctx, c_ctx, w_mod, b_mod, w_in, ret_decay, conv_w, diff_lambda,
                 w_ret_out, w_conv_out, w_att_out, w_out, ln_g, ln_b):
    f = np.float32
    cosR, sinR = _rope_tables(256, 2)
    cosA, sinA = _rope_tables(64, 4)
    p = np.arange(128, dtype=f)
    dist = p[None, :] - p[:, None]
    dconst = np.stack([np.maximum(dist, 0), (dist >= 0).astype(f), np.maximum(-dist, 0), (dist < 0).astype(f)], 1).astype(f)
    posv = np.stack([p + 1, 128 - p, 127 - p, p], 1).astype(f)
    w_in_p = np.ascontiguousarray(np.asarray(w_in, f)[:, :, _perm_cols()])
    bm = np.asarray(b_mod, f)
    bmodT = np.ascontiguousarray(bm[:, :2048].reshape(DEPTH, 16, 128).transpose(0, 2, 1))
    bgate = np.ascontiguousarray(np.broadcast_to(bm[:, None, 2048:], (DEPTH, 128, D)))
    shared = dict(
        w_mod=np.ascontiguousarray(w_mod, f), bmodT=bmodT, bgate=bgate, w_in=w_in_p,
        rdec=np.ascontiguousarray(np.broadcast_to(np.asarray(ret_decay, f).reshape(1, 32), (128, 32))),
        convw=np.ascontiguousarray(np.asarray(conv_w, f).reshape(DEPTH, 3, 8, 128).transpose(0, 3, 2, 1)),
        dlam=np.ascontiguousarray(np.broadcast_to(np.asarray(diff_lambda, f).reshape(1, DEPTH * 256), (128, DEPTH * 256))),
        w_ro=np.ascontiguousarray(w_ret_out, f), w_co=np.ascontiguousarray(w_conv_out, f),
        w_ao=np.ascontiguousarray(w_att_out, f), w_oo=np.ascontiguousarray(w_out, f),
        lng=np.ascontiguousarray(np.broadcast_to(np.asarray(ln_g, f)[:, None, :], (DEPTH, 128, D))),
        lnb=np.ascontiguousarray(np.broadcast_to(np.asarray(ln_b, f)[:, None, :], (DEPTH, 128, D))),
        ident=np.eye(128, dtype=f), cosR=cosR, sinR=sinR, cosA=cosA, sinA=sinA, dconst=dconst, posv=posv,
    )
    in_maps = []
    for core in range(8):
        b = core % 4
        m = dict(shared)
        m["xin"] = np.ascontiguousarray(np.concatenate([np.asarray(ctx[b], f), np.asarray(x[b], f)], 0))
        cv = np.stack([np.asarray(c[b], f), np.asarray(c_ctx, f)], -1)
        m["cvec"] = np.ascontiguousarray(cv.reshape(KC, 128, 2).transpose(1, 0, 2))
        in_maps.append(m)
    return in_maps


_CACHE = {}


def kernel(x, c, ctx, c_ctx, w_mod, b_mod, w_in, ret_decay, conv_w, diff_lambda,
           w_ret_out, w_conv_out, w_att_out, w_out, ln_g, ln_b):
    in_maps = _host_inputs(x, c, ctx, c_ctx, w_mod, b_mod, w_in, ret_decay, conv_w, diff_lambda,
                           w_ret_out, w_conv_out, w_att_out, w_out, ln_g, ln_b)
    if "nc" not in _CACHE:
        _CACHE["nc"] = build_program()[0]
    res = run_bass_kernel_spmd(_CACHE["nc"], in_maps, core_ids=list(range(8)))
    out = np.stack([np.asarray(res.results[b]["out"], np.float32) for b in range(4)], 0)
    return out
```

```python
import math
import os
import contextlib
import numpy as np
import concourse.bass as bass
import concourse.mybir as mybir
from concourse.bass_utils import run_bass_kernel_spmd

F32 = mybir.dt.float32
BF16 = mybir.dt.bfloat16
AF = mybir.ActivationFunctionType
ALU = mybir.AluOpType
AX = mybir.AxisListType

ALLENG = ("sp", "pe", "act", "dve", "pool")

D = 1024
KC = 8
NT = 34
TOK = NT * 128
NCTX_T = 2
DEPTH = 4
LN_EPS = 1e-6
ALPHA = (2 * DEPTH) ** 0.25
IN_COLS = 15360


PSUM_NAMES = {"cps", "rpA", "rpB", "rpT", "rpS", "rpO", "rpO2", "rpI", "apA", "apT", "apS", "apO", "gpA", "bpR", "bpT", "bpO",
              "ppT", "pm", "pg"}


def I(name, *a, **kw):
    return (name, a, kw)


class Prog:
    def __init__(self, nc):
        self.nc = nc
        self.ops = []
        self.last_w = {}
        self.readers = {}
        self.dsem_cnt = {}
        self.last_eng = {}
        self.last_dma = {}
        self.pending = {}
        self.trace = [] if os.environ.get("PTRACE") else None

    def op(self, eng, fn, reads=(), writes=(), dsem=None):
        i = len(self.ops)
        deps = {}
        for r in reads:
            j = self.last_w.get(r)
            if j is not None:
                deps[j] = True
            rn = r[0] if isinstance(r, tuple) else r
            if rn in PSUM_NAMES:
                for j in self.readers.get(r, {}).values():
                    if self.ops[j]["eng"] != eng:
                        deps.setdefault(j, False)
        for w in writes:
            j = self.last_w.get(w)
            if j is not None:
                deps.setdefault(j, False)
            for j in self.readers.get(w, {}).values():
                deps.setdefault(j, False)
        pb = self.pending.pop(eng, None)
        if pb:
            for j in pb:
                deps.setdefault(j, False)
        o = dict(eng=eng, fn=fn, deps=deps, dsem=dsem, dcount=None, sig=None)
        if dsem is not None:
            c = self.dsem_cnt.get(dsem, 0) + 16
            self.dsem_cnt[dsem] = c
            o["dcount"] = c
            self.last_dma[dsem] = i
        self.ops.append(o)
        self.last_eng[eng] = i
        rk = eng if dsem is None else ("d", dsem)
        for r in reads:
            self.readers.setdefault(r, {})[rk] = i
        for w in writes:
            self.last_w[w] = i
            self.readers[w] = {}
        return i

    def barrier(self):
        s = set(self.last_eng.values()) | set(self.last_dma.values())
        for e in ALLENG:
            self.pending[e] = set(s) | self.pending.get(e, set())

    def emit(self):
        nc = self.nc
        ops = self.ops
        need_sig = [False] * len(ops)
        for i, o in enumerate(ops):
            for j, raw in o["deps"].items():
                pj = ops[j]
                if pj["dsem"] is not None:
                    continue
                if pj["eng"] == o["eng"] and o["dsem"] is None:
                    if raw and o["eng"] != "pe":
                        need_sig[j] = True
                else:
                    need_sig[j] = True
        cnt = {e: 0 for e in ALLENG}
        for i, o in enumerate(ops):
            if o["dsem"] is None and need_sig[i]:
                cnt[o["eng"]] += 1
                o["sig"] = cnt[o["eng"]]
        dkeys = sorted(self.dsem_cnt.keys(), key=str)
        with contextlib.ExitStack() as st:
            esem = {e: st.enter_context(nc.semaphore("s_" + e)) for e in ALLENG}
            dsem = {k: st.enter_context(nc.semaphore("d_%d" % n)) for n, k in enumerate(dkeys)}
            block = st.enter_context(nc.Block())
            per_eng = {e: [] for e in ALLENG}
            for i, o in enumerate(ops):
                per_eng[o["eng"]].append(i)

            def run(engname, eng):
                waited = {}
                for i in per_eng[engname]:
                    o = ops[i]
                    for j, raw in sorted(o["deps"].items()):
                        pj = ops[j]
                        if pj["dsem"] is not None:
                            key = ("d", pj["dsem"])
                            val = pj["dcount"]
                            sem = dsem[pj["dsem"]]
                        else:
                            if pj["sig"] is None:
                                continue
                            if pj["eng"] == engname and o["dsem"] is None and not (raw and engname != "pe"):
                                continue
                            key = ("e", pj["eng"])
                            val = pj["sig"]
                            sem = esem[pj["eng"]]
                        if waited.get(key, 0) >= val:
                            continue
                        waited[key] = val
                        eng.wait_ge(sem, val)
                        if self.trace is not None:
                            self.trace.append((engname, "WAIT", key, val))
                    nm, a_, kw_ = o["fn"]
                    ins = getattr(eng, nm)(*a_, **kw_)
                    if self.trace is not None:
                        self.trace.append((engname, nm, i, o["sig"], o["dsem"], o["dcount"]))
                    if o["dsem"] is not None:
                        ins.then_inc(dsem[o["dsem"]], 16)
                    elif o["sig"] is not None:
                        ins.then_inc(esem[engname], 1)
                if engname == "sp":
                    for k in dkeys:
                        eng.wait_ge(dsem[k], self.dsem_cnt[k])

            block.sync(lambda e: run("sp", e))
            block.tensor(lambda e: run("pe", e))
            block.scalar(lambda e: run("act", e))
            block.vector(lambda e: run("dve", e))
            block.gpsimd(lambda e: run("pool", e))
        self.stats = dict(n_ops=len(ops), sig=cnt, ndsem=len(dkeys))


class Rot:
    def __init__(self, tiles, name):
        self.tiles = tiles
        self.name = name
        self.i = 0

    def next(self):
        k = self.i % len(self.tiles)
        self.i += 1
        return self.tiles[k], (self.name, k)


def build_program(n_layers=DEPTH, dbg=False, upto=None):
    nc = bass.Bass("TRN2", target_bir_lowering=False)

    def din(name, shape, dt=F32):
        return nc.dram_tensor(name, list(shape), dt, kind="ExternalInput").ap()

    xin = din("xin", [TOK, D])
    cvec = din("cvec", [128, KC, 2])
    w_mod = din("w_mod", [DEPTH, D, 3 * D])
    bmodT = din("bmodT", [DEPTH, 128, 16])
    bgate = din("bgate", [DEPTH, 128, D])
    w_in = din("w_in", [DEPTH, D, IN_COLS])
    rdec = din("rdec", [128, 32])
    convw = din("convw", [DEPTH, 128, 8, 3])
    dlam = din("dlam", [128, DEPTH * 256])
    w_ro = din("w_ro", [DEPTH, D, D])
    w_co = din("w_co", [DEPTH, D, D])
    w_ao = din("w_ao", [DEPTH, D, D])
    w_oo = din("w_oo", [DEPTH, D, D])
    lng = din("lng", [DEPTH, 128, D])
    lnb = din("lnb", [DEPTH, 128, D])
    ident_d = din("ident", [128, 128])
    cosR = din("cosR", [TOK, 256])
    sinR = din("sinR", [TOK, 256])
    cosA = din("cosA", [TOK, 128])
    sinA = din("sinA", [TOK, 128])
    dconst = din("dconst", [128, 4, 128])
    posv_d = din("posv", [128, 4])
    out_d = nc.dram_tensor("out", [TOK - 256, D], F32, kind="ExternalOutput").ap()
    skind = dict(kind="ExternalOutput") if dbg else {}
    yT_d = nc.dram_tensor("yT_s", [3, D, TOK], BF16, **skind).ap()
    gates_d = nc.dram_tensor("gates_s", [TOK, 3 * D], BF16, **skind).ap()
    xbuf_d = nc.dram_tensor("xbuf_s", [TOK, D], F32, **skind).ap()

    hT_dbg = nc.dram_tensor("hT_dbg", [128, KC, TOK], BF16, kind="ExternalOutput").ap() if dbg else None
    P = Prog(nc)
    G = contextlib.ExitStack()

    uid = [0]

    def mk(stack, kind):
        def f(name, shape, dt):
            uid[0] += 1
            name = "%s_u%d" % (name, uid[0])
            if kind == "sb":
                return stack.enter_context(nc.sbuf_tensor(name, list(shape), dt))
            return stack.enter_context(nc.psum_tensor(name, list(shape), dt))
        return f

    gsb = mk(G, "sb")

    with G:
        hT = gsb("hT", [128, KC, TOK], BF16)
        identf = gsb("identf", [128, 128], F32)
        identb = gsb("identb", [128, 128], BF16)
        dcon = gsb("dcon", [128, 4, 128], F32)
        posv = gsb("posv", [128, 4], F32)
        lg_all = gsb("lg_all", [128, 32], F32)
        scv = gsb("scv", [128, KC, 2], F32)
        scvb = gsb("scvb", [128, KC, 2], BF16)
        screp = gsb("screp", [128, KC, 2, 128], BF16)
        onesb = gsb("onesb", [128, 128], F32)
        mhalf = gsb("mhalf", [128, 2], F32)
        modT = gsb("modT", [128, 16, 2], F32)
        gate_holder = [None]
        DT_all = gsb("DT_all", [128, 4, 128], F32)
        rv_all = gsb("rv_all", [128, 4, 8], F32)
        lam_t = gsb("lam_t", [128, 4], F32)
        smallr = Rot([gsb("small%d" % i, [128, 8], F32) for i in range(6)], "small")
        statr = Rot([gsb("stat%d" % i, [128, 2, 6], F32) for i in range(3)], "stat")
        xnb_holder = [None]

        def dma(out, in_, reads, writes, key, eng="sp"):
            P.op(eng, I("dma_start", out=out, in_=in_), reads=reads, writes=writes, dsem=key)

        dma(identf[:], ident_d, [], ["identf"], "c0")
        dma(dcon[:], dconst, [], ["dcon"], "c1")
        dma(posv[:], posv_d, [], ["posv"], "c2")
        dma(lg_all[:], rdec, [], ["lg_all"], "c3")
        dma(scv[:], cvec, [], ["scv"], "c5")
        P.op("pool", I("tensor_copy", out=identb[:], in_=identf[:]), reads=["identf"], writes=["identb"])
        P.op("pool", I("memset", onesb[:], 1.0), writes=["onesb"])
        P.op("pool", I("memset", mhalf[:], -0.5), writes=["mhalf"])
        P.op("act", I("activation", out=lg_all[:], in_=lg_all[:], func=AF.Exp), reads=["lg_all"], writes=["lg_all"])
        P.op("dve", I("tensor_scalar", out=lg_all[:], in0=lg_all[:], scalar1=-1.0, scalar2=None, op0=ALU.mult),
             reads=["lg_all"], writes=["lg_all"])
        P.op("act", I("activation", out=scv[:], in_=scv[:], func=AF.Silu), reads=["scv"], writes=["scv"])
        P.op("dve", I("tensor_copy", out=scvb[:], in_=scv[:]), reads=["scv"], writes=["scvb"])
        for kc in range(KC):
            for wh in range(2):
                P.op("act", I("activation", out=screp[:, kc, wh, :], in_=onesb[:], func=AF.Identity,
                                                                   scale=scv[:, kc, wh:wh + 1]),
                     reads=["onesb", "scv"], writes=["screp"])

        def ln_stats(src_ap_fn, n, rsrc, width):
            st_t, st_r = statr.next()
            sm, sm_r = smallr.next()
            nh = (width + 511) // 512
            for i in range(nh):
                w0, w1 = i * 512, min(width, (i + 1) * 512)
                P.op("dve", I("bn_stats", out=st_t[:, i, :], in_=src_ap_fn(w0, w1)),
                     reads=rsrc, writes=[st_r])
            P.op("dve", I("bn_aggr", out=sm[:, 2:4], in_=st_t[:, 0:nh, :]), reads=[st_r], writes=[sm_r])
            P.op("dve", I("tensor_scalar", out=sm[:, 4:5], in0=sm[:, 3:4], scalar1=LN_EPS, scalar2=None, op0=ALU.add),
                 reads=[sm_r], writes=[sm_r])
            P.op("pool", I("tensor_tensor", out=sm[:, 0:1], in0=sm[:, 4:5], in1=mhalf[:, 0:1], op=ALU.pow), reads=[sm_r, "mhalf"], writes=[sm_r])
            P.op("dve", I("scalar_tensor_tensor", out=sm[:, 1:2], in0=sm[:, 2:3], scalar=-1.0, in1=sm[:, 0:1],
                                                         op0=ALU.mult, op1=ALU.mult), reads=[sm_r], writes=[sm_r])
            return sm, sm_r

        def make_hT(t, xsrc, xres, pTa, pTa_r, pTb, pTb_r):
            wh = 1 if t < NCTX_T else 0
            sm, sm_r = ln_stats(lambda a, b: xsrc[:, a:b], D, [xres], D)
            xnb, xnb_r = xnb_holder[0].next()
            P.op("act", I("activation", out=xnb[:], in_=xsrc[:], func=AF.Identity, bias=sm[:, 1:2], scale=sm[:, 0:1]),
                 reads=[xres, sm_r], writes=[xnb_r])
            for kc in range(KC):
                pT, pT_r = (pTa, pTa_r) if kc < 4 else (pTb, pTb_r)
                P.op("pe", I("transpose", out=pT[:, kc, :], in_=xnb[:, kc * 128:(kc + 1) * 128], identity=identb[:]),
                     reads=[xnb_r, "identb"], writes=[pT_r])
            for kc in range(KC):
                dst = hT[:, kc, t * 128:(t + 1) * 128]
                if kc < 4:
                    P.op("dve", I("tensor_scalar", out=dst, in0=pTa[:, kc, :], scalar1=modT[:, 8 + kc, wh:wh + 1],
                                  scalar2=modT[:, kc, wh:wh + 1], op0=ALU.mult, op1=ALU.add),
                         reads=[pTa_r, "modT"], writes=[("hT", t)])
                else:
                    P.op("act", I("activation", out=dst, in_=pTb[:, kc, :], func=AF.Identity,
                                  bias=modT[:, kc, wh:wh + 1], scale=modT[:, 8 + kc, wh:wh + 1]),
                         reads=[pTb_r, "modT"], writes=[("hT", t)])

        cast_i = [0]

        def load_w(src, ncols, stg_rot, wbf_rot, key):
            stg, stg_r = stg_rot.next()
            wbf, wbf_r = wbf_rot.next()
            dma(stg[:, :, 0:ncols], src.rearrange("(c p) n -> p c n", p=128), [], [stg_r], (key, stg_r[1]))
            engs = ("pool", "dve", "act", "pool")
            for q in range(4):
                en = engs[(q + cast_i[0]) % 4]
                if en == "act":
                    P.op(en, I("copy", out=wbf[:, 2 * q:2 * q + 2, 0:ncols], in_=stg[:, 2 * q:2 * q + 2, 0:ncols]),
                         reads=[stg_r], writes=[wbf_r])
                else:
                    P.op(en, I("tensor_copy", out=wbf[:, 2 * q:2 * q + 2, 0:ncols], in_=stg[:, 2 * q:2 * q + 2, 0:ncols]),
                         reads=[stg_r], writes=[wbf_r])
            cast_i[0] += 1
            return wbf, wbf_r

        def proj_tok(ps, ps_r, t, wbf, wbf_r, c0, c1):
            for kc in range(KC):
                P.op("pe", I("matmul", ps[:, 0:c1 - c0], lhsT=hT[:, kc, t * 128:(t + 1) * 128], rhs=wbf[:, kc, c0:c1],
                                                    start=(kc == 0), stop=(kc == KC - 1)),
                     reads=[("hT", t), wbf_r], writes=[ps_r])

        def setup_mod(l, S):
            sb = mk(S, "sb")
            ps = mk(S, "ps")
            stg = sb("mstg", [128, KC, 512], F32)
            stgb = sb("mstgb", [128, KC, 512], BF16)
            bmt = sb("bmt", [128, 16], F32)
            pm_full = ps("pm", [128, 16, 32], F32)
            pm = pm_full[:, :, 0:2]
            dma(bmt[:], bmodT[l], [], ["bmt"], "m0")
            for jb in range(4):
                dma(stg[:], w_mod[l][:, jb * 512:(jb + 1) * 512].rearrange("(c p) n -> p c n", p=128), [], ["mstg"], "m1")
                P.op("pool", I("tensor_copy", out=stgb[:], in_=stg[:]), reads=["mstg"], writes=["mstgb"])
                for fc in range(4):
                    j = jb * 4 + fc
                    for kc in range(KC):
                        P.op("pe", I("matmul", pm[:, j, :], lhsT=stgb[:, kc, fc * 128:(fc + 1) * 128],
                                                                          rhs=scvb[:, kc, :], start=(kc == 0), stop=(kc == KC - 1)),
                             reads=["mstgb", "scvb"], writes=["pm"])
            for wh in range(2):
                P.op("dve", I("tensor_tensor", out=modT[:, :, wh], in0=pm[:, :, wh], in1=bmt[:], op=ALU.add),
                     reads=["pm", "bmt"], writes=["modT"])
            P.op("dve", I("tensor_scalar", out=modT[:, 8:16, :], in0=modT[:, 8:16, :], scalar1=1.0, scalar2=None, op0=ALU.add),
                 reads=["modT"], writes=["modT"])

        def setup_gate(l, S):
            gate_rep = gate_holder[0]
            sb = mk(S, "sb")
            ps = mk(S, "ps")
            stg = sb("gstg", [128, KC, 512], F32)
            stgb = sb("gstgb", [128, KC, 512], BF16)
            bg = sb("bg", [128, D], F32)
            pg = ps("pg", [128, 512], F32)
            dma(bg[:], bgate[l], [], ["bg"], "g0")
            for jb in range(2):
                dma(stg[:], w_mod[l][:, (4 + jb) * 512:(5 + jb) * 512].rearrange("(c p) n -> p c n", p=128), [], ["gstg"], "g1")
                P.op("pool", I("tensor_copy", out=stgb[:], in_=stg[:]), reads=["gstg"], writes=["gstgb"])
                for wh in range(2):
                    for kc in range(KC):
                        P.op("pe", I("matmul", pg[:], lhsT=screp[:, kc, wh, :], rhs=stgb[:, kc, :],
                                     start=(kc == 0), stop=(kc == KC - 1)),
                             reads=["screp", "gstgb"], writes=["pg"])
                    P.op("dve", I("tensor_tensor", out=gate_rep[:, wh, jb * 512:(jb + 1) * 512], in0=pg[:],
                                  in1=bg[:, jb * 512:(jb + 1) * 512], op=ALU.add),
                         reads=["pg", "bg"], writes=["gate_rep"])

        def setup_rest(l, S):
            sb = mk(S, "sb")
            ps = mk(S, "ps")
            e1 = sb("e1", [128, 128], F32)
            e2 = sb("e2", [128, 128], F32)
            pl = sb("pl", [128, 2, 64], F32)
            for h in range(4):
                lgf = lg_all[:, l * 8 + h:l * 8 + h + 1]
                lgb = lg_all[:, l * 8 + 4 + h:l * 8 + 4 + h + 1]
                P.op("act", I("activation", out=e1[:], in_=dcon[:, 0, :], func=AF.Exp, scale=lgf),
                     reads=["dcon", "lg_all"], writes=["e1"])
                P.op("pool", I("tensor_tensor", out=e1[:], in0=e1[:], in1=dcon[:, 1, :], op=ALU.mult), reads=["e1", "dcon"], writes=["e1"])
                P.op("act", I("activation", out=e2[:], in_=dcon[:, 2, :], func=AF.Exp, scale=lgb),
                     reads=["dcon", "lg_all"], writes=["e2"])
                P.op("pool", I("tensor_tensor", out=e2[:], in0=e2[:], in1=dcon[:, 3, :], op=ALU.mult), reads=["e2", "dcon"], writes=["e2"])
                P.op("pool", I("tensor_tensor", out=e1[:], in0=e1[:], in1=e2[:], op=ALU.add), reads=["e1", "e2"], writes=["e1"])
                P.op("act", I("mul", out=DT_all[:, h, :], in_=e1[:], mul=0.0625), reads=["e1"], writes=["DT_all"])
                for k, (pc, lgx, post) in enumerate([(0, lgf, 0.0625), (1, lgb, 0.0625), (2, lgf, None), (3, lgb, None)]):
                    P.op("act", I("activation", out=rv_all[:, h, k:k + 1], in_=posv[:, pc:pc + 1],
                                                                             func=AF.Exp, scale=lgx),
                         reads=["posv", "lg_all"], writes=["rv_all"])
                    if post is not None:
                        P.op("dve", I("tensor_scalar", out=rv_all[:, h, k:k + 1], in0=rv_all[:, h, k:k + 1],
                                                                                 scalar1=post, scalar2=None, op0=ALU.mult),
                             reads=["rv_all"], writes=["rv_all"])
                for k, lgx in ((4, lgf), (5, lgb)):
                    P.op("act", I("activation", out=rv_all[:, h, k:k + 1], in_=lgx, func=AF.Exp, scale=128.0),
                         reads=["lg_all"], writes=["rv_all"])
            lam_init = 0.8 - 0.6 * math.exp(-0.3 * l)
            dl_all = sb("dl_l", [128, 256], F32)
            dma(dl_all[:], dlam[:, l * 256:(l + 1) * 256], [], ["dl_all"], "c4")
            for i in range(2):
                a0 = dl_all[:, i * 128:i * 128 + 64]
                a1 = dl_all[:, i * 128 + 64:i * 128 + 128]
                P.op("dve", I("tensor_tensor", out=pl[:, i, :], in0=a0, in1=a1, op=ALU.mult),
                     reads=["dl_all"], writes=["pl"])
                P.op("dve", I("reduce_sum", out=lam_t[:, i:i + 1], in_=pl[:, i, :], axis=AX.X), reads=["pl"], writes=["lam_t"])
                P.op("act", I("activation", out=lam_t[:, i:i + 1], in_=lam_t[:, i:i + 1], func=AF.Exp), reads=["lam_t"], writes=["lam_t"])
            P.op("dve", I("tensor_tensor", out=lam_t[:, 2:3], in0=lam_t[:, 0:1], in1=lam_t[:, 1:2], op=ALU.subtract),
                 reads=["lam_t"], writes=["lam_t"])
            P.op("dve", I("tensor_scalar", out=lam_t[:, 3:4], in0=lam_t[:, 2:3], scalar1=lam_init, scalar2=-1.0, op0=ALU.add, op1=ALU.mult),
                 reads=["lam_t"], writes=["lam_t"])
            return lam_init

        def phase_conv(l, S):
            sb = mk(S, "sb")
            ps = mk(S, "ps")
            stg_rot = Rot([sb("cstg%d" % i, [128, KC, 512], F32) for i in range(2)], "cstg")
            wbf_rot = Rot([sb("cwbf%d" % i, [128, KC, 512], BF16) for i in range(2)], "cwbf")
            cw = sb("cw", [128, 8, 3], F32)
            u = sb("u_full", [128, TOK], F32)
            cf = sb("cf_full", [128, TOK], F32)
            tmpr = Rot([sb("ctmp%d" % i, [128, 512], F32) for i in range(4)], "ctmp")
            ychr = Rot([sb("ych%d" % i, [128, 512], BF16) for i in range(2)], "ych")
            psr = Rot([ps("cps%d" % i, [128, 512], F32) for i in range(4)], "cps")
            dma(cw[:], convw[l], [], ["cw"], "cv0")
            chunks = [(c * 512, min(TOK, (c + 1) * 512)) for c in range(9)]
            for g in range(8):
                wbf, wbf_r = load_w(w_in[l][:, 8192 + g * 512:8192 + (g + 1) * 512], 512, stg_rot, wbf_rot, "wc")

                def projT(pst, pst_r, col, t0, t1):
                    for kc in range(KC):
                        P.op("pe", I("matmul", pst[:, 0:t1 - t0], lhsT=wbf[:, kc, col * 128:(col + 1) * 128], rhs=hT[:, kc, t0:t1],
                                                            start=(kc == 0), stop=(kc == KC - 1)),
                             reads=[("hT", tt) for tt in range(t0 // 128, t1 // 128)] + [wbf_r], writes=[pst_r])
                for (t0, t1) in chunks:
                    n = t1 - t0
                    pa, pa_r = psr.next()
                    pb, pb_r = psr.next()
                    projT(pa, pa_r, 1, t0, t1)
                    projT(pb, pb_r, 2, t0, t1)
                    tm, tm_r = tmpr.next()
                    P.op("act", I("copy", out=tm[:, 0:n], in_=pa[:, 0:n]), reads=[pa_r], writes=[tm_r])
                    P.op("dve", I("tensor_tensor", out=u[:, t0:t1], in0=pb[:, 0:n], in1=tm[:, 0:n], op=ALU.mult),
                         reads=[pb_r, tm_r], writes=["u_full"])
                for (s0, s1) in ((0, 256), (256, TOK)):
                    P.op("act", I("mul", out=cf[:, s0:s1], in_=u[:, s0:s1], mul=cw[:, g, 1:2]),
                         reads=["u_full", "cw"], writes=["cf_full"])
                    P.op("dve", I("scalar_tensor_tensor", out=cf[:, s0 + 1:s1], in0=u[:, s0:s1 - 1], scalar=cw[:, g, 0:1],
                                                                                in1=cf[:, s0 + 1:s1], op0=ALU.mult, op1=ALU.add),
                         reads=["u_full", "cw", "cf_full"], writes=["cf_full"])
                    P.op("dve", I("scalar_tensor_tensor", out=cf[:, s0:s1 - 1], in0=u[:, s0 + 1:s1], scalar=cw[:, g, 2:3],
                                                                                in1=cf[:, s0:s1 - 1], op0=ALU.mult, op1=ALU.add),
                         reads=["u_full", "cw", "cf_full"], writes=["cf_full"])
                for (t0, t1) in chunks:
                    n = t1 - t0
                    pa, pa_r = psr.next()
                    pb, pb_r = psr.next()
                    projT(pa, pa_r, 0, t0, t1)
                    projT(pb, pb_r, 3, t0, t1)
                    sg, sg_r = tmpr.next()
                    tt_, tt_r = tmpr.next()
                    ych, ych_r = ychr.next()
                    P.op("act", I("activation", out=sg[:, 0:n], in_=pb[:, 0:n], func=AF.Silu), reads=[pb_r], writes=[sg_r])
                    P.op("dve", I("tensor_tensor", out=tt_[:, 0:n], in0=pa[:, 0:n], in1=sg[:, 0:n], op=ALU.mult),
                         reads=[pa_r, sg_r], writes=[tt_r])
                    P.op("pool", I("tensor_tensor", out=ych[:, 0:n], in0=tt_[:, 0:n], in1=cf[:, t0:t1], op=ALU.mult),
                         reads=[tt_r, "cf_full"], writes=[ych_r])
                    dma(yT_d[1][g * 128:(g + 1) * 128, t0:t1], ych[:, 0:n], [ych_r], ["yT_d"], ("yc", ych_r[1]), eng="pool")

        rope_lim = [99]

        def rope(src1, src2, src_r, c_ap, s_ap, tab_r, dst1, dst2, dst_r, tmps):
            (t1, t1r), (t2, t2r), (t3, t3r), (t4, t4r) = tmps
            re_ = os.environ.get("ROPE_ENG", "pool")
            lst = [
                ("dve", I("tensor_tensor", out=t1, in0=src1, in1=c_ap, op=ALU.mult), [src_r, tab_r], [t1r]),
                ("dve", I("tensor_tensor", out=t2, in0=src2, in1=s_ap, op=ALU.mult), [src_r, tab_r], [t2r]),
                ("dve", I("tensor_tensor", out=t3, in0=src1, in1=s_ap, op=ALU.mult), [src_r, tab_r], [t3r]),
                ("dve", I("tensor_tensor", out=t4, in0=src2, in1=c_ap, op=ALU.mult), [src_r, tab_r], [t4r]),
                (re_, I("tensor_tensor", out=dst1, in0=t1, in1=t2, op=ALU.subtract), [t1r, t2r], [dst_r]),
                (re_, I("tensor_tensor", out=dst2, in0=t3, in1=t4, op=ALU.add), [t3r, t4r], [dst_r]),
            ]
            for en, ins, rd, wr in lst[:rope_lim[0]]:
                P.op(en, ins, reads=rd, writes=wr)

        def phase_ret(l, S):
            sb = mk(S, "sb")
            ps = mk(S, "ps")
            stg_rot = Rot([sb("rstg%d" % i, [128, KC, 512], F32) for i in range(1)], "rstg")
            wbf_rot = Rot([sb("rwbf%d" % i, [128, KC, 512], BF16) for i in range(2)], "rwbf")
            Sb_all = sb("Sb_all", [128, NT, 2, 256], BF16)
            Sst = sb("Sst", [128, 2, 256], F32)
            Sbf = sb("Sbf", [128, 2, 256], BF16)
            tabr = Rot([sb("rtab%d" % i, [128, 2, 256], F32) for i in range(int(os.environ.get("TAB_SLOTS", "2")))], "rtab")
            tmp = [Rot([sb("rt%d_%d" % (k, i), [128, 2, 128], F32) for i in range(int(os.environ.get("TMP_SLOTS", "2")))], "rt%d" % k) for k in range(4)]
            qkb_r = Rot([sb("rqkb%d" % i, [128, 2, 256], BF16) for i in range(3)], "rqkb")
            vbr = Rot([sb("rvb%d" % i, [128, 2, 256], BF16) for i in range(3)], "rvb")
            sgr = Rot([sb("rsg%d" % i, [128, 256], BF16) for i in range(3)], "rsg")
            qkTr = Rot([sb("rqkT%d" % i, [128, 4, 128], BF16) for i in range(2)], "rqkT")
            ATr = Rot([sb("rAT%d" % i, [128, 128], BF16) for i in range(2)], "rAT")
            o_r = Rot([sb("ro%d" % i, [128, 256], F32) for i in range(3)], "ro")
            ybr = Rot([sb("ryb%d" % i, [128, 256], BF16) for i in range(3)], "ryb")
            yTr = Rot([sb("ryT%d" % i, [128, 2, 128], BF16) for i in range(2)], "ryT")
            pAr = Rot([ps("rpA%d" % i, [128, 512], F32) for i in range(int(os.environ.get("PA_SLOTS", "2")))], "rpA")
            pBr = Rot([ps("rpB%d" % i, [128, 512], F32) for i in range(1)], "rpB")
            pTr = Rot([ps("rpT%d" % i, [128, 8, 128], BF16) for i in range(1)], "rpT")
            pSr = Rot([ps("rpS%d" % i, [128, 512], F32) for i in range(1)], "rpS")
            pOr = Rot([ps("rpO%d" % i, [128, 512], F32) for i in range(1)], "rpO")
            pO2r = Rot([ps("rpO2%d" % i, [128, 512], F32) for i in range(1)], "rpO2")
            pIr = Rot([ps("rpI%d" % i, [128, 2, 256], F32) for i in range(1)], "rpI")

            def load_tab(n):
                tb, tb_r = tabr.next()
                dma(tb[:, 0, :], cosR[n * 128:(n + 1) * 128, :], [], [tb_r], ("rtab", tb_r[1]))
                dma(tb[:, 1, :], sinR[n * 128:(n + 1) * 128, :], [], [tb_r], ("rtab", tb_r[1]))
                return tb, tb_r

            gwbf_rot = Rot([sb("rgw%d" % i, [128, KC, 512], BF16) for i in range(2)], "rgw")
            gsr = Rot([sb("rgs%d" % i, [128, 512], BF16) for i in range(2)], "rgs")
            gate_items = [(cb, t) for cb in range(6) for t in range(NT)]
            gate_w = {}
            gi = [0]

            def gate_load(cb):
                gate_w[cb] = load_w(w_in[l][:, 12288 + cb * 512:12288 + (cb + 1) * 512], 512, stg_rot, gwbf_rot, "wg")

            def gate_step(k):
                for _ in range(k):
                    if gi[0] >= len(gate_items):
                        return
                    cb, t = gate_items[gi[0]]
                    gi[0] += 1
                    if t == 0 and cb + 1 < 6:
                        gate_load(cb + 1)
                    gw, gw_r = gate_w[cb]
                    pg_, pg_r = pBr.next()
                    proj_tok(pg_, pg_r, t, gw, gw_r, 0, 512)
                    gs, gs_r = gsr.next()
                    P.op("act", I("activation", out=gs[:], in_=pg_[:], func=AF.Sigmoid), reads=[pg_r], writes=[gs_r])
                    dma(gates_d[t * 128:(t + 1) * 128, cb * 512:(cb + 1) * 512], gs[:], [gs_r], ["gates_d"], ("gg", gs_r[1]), eng="pool")

            gate_load(0)
            for h in range(4):
                wkv, wkv_r = load_w(w_in[l][:, h * 1024:h * 1024 + 512], 512, stg_rot, wbf_rot, "wr")
                wqg, wqg_r = load_w(w_in[l][:, h * 1024 + 512:h * 1024 + 1024], 512, stg_rot, wbf_rot, "wr")
                ub = rv_all[:, h, 3:4]
                uf = rv_all[:, h, 2:3]
                wf = rv_all[:, h, 0:1]
                wb_ = rv_all[:, h, 1:2]
                g128f = rv_all[:, h, 4:5]
                g128b = rv_all[:, h, 5:6]
                P.op("pool", I("memset", Sst[:], 0.0), writes=["Sst"])
                order = [1, 0] + list(range(NT - 1, 1, -1))

                def p1_front(n):
                    pa, pa_r = pAr.next()
                    proj_tok(pa, pa_r, n, wkv, wkv_r, 0, 512)
                    tb, tb_r = load_tab(n)
                    qkb, qkb_rr = qkb_r.next()
                    tm = [tmp[k].next() for k in range(4)]
                    rope(pa[:, 0:128], pa[:, 128:256], pa_r, tb[:, 0, 0:128], tb[:, 1, 0:128], tb_r,
                         qkb[:, 1, 0:128], qkb[:, 1, 128:256], qkb_rr, [(tm[k][0][:, 0, :], tm[k][1]) for k in range(4)])
                    vb, vb_r = vbr.next()
                    P.op("act", I("activation", out=vb[:, 1, :], in_=pa[:, 256:512], func=AF.Identity, scale=ub),
                         reads=[pa_r, "rv_all"], writes=[vb_r])
                    return (qkb, qkb_rr, vb, vb_r)

                def p1_back(n, ctx):
                    qkb, qkb_rr, vb, vb_r = ctx
                    P.op("act", I("copy", out=Sb_all[:, n, :, :], in_=Sst[:]), reads=["Sst"], writes=[("Sb_all", n)])
                    pI, pI_r = pIr.next()
                    for dc in range(2):
                        P.op("pe", I("matmul", pI[:, dc, :], lhsT=qkb[:, 1, dc * 128:(dc + 1) * 128], rhs=vb[:, 1, :], start=True, stop=True),
                             reads=[qkb_rr, vb_r], writes=[pI_r])
                    P.op("dve", I("scalar_tensor_tensor", out=Sst[:], in0=Sst[:], scalar=g128b, in1=pI[:], op0=ALU.mult, op1=ALU.add),
                         reads=["Sst", pI_r, "rv_all"], writes=["Sst"])

                ctxs = {0: p1_front(order[0])}
                for i, n in enumerate(order):
                    if i + 1 < len(order):
                        ctxs[i + 1] = p1_front(order[i + 1])
                    p1_back(n, ctxs.pop(i))
                    gate_step(2 if i % 2 == 0 else 1)

                P.op("pool", I("memset", Sst[:], 0.0), writes=["Sst"])
                P.op("pool", I("memset", Sbf[:], 0.0), writes=["Sbf"])

                def p2_front(n):
                    pa, pa_r = pAr.next()
                    pb, pb_r = pBr.next()
                    proj_tok(pa, pa_r, n, wkv, wkv_r, 0, 512)
                    proj_tok(pb, pb_r, n, wqg, wqg_r, 0, 512)
                    tb, tb_r = load_tab(n)
                    qkb, qkb_rr = qkb_r.next()
                    tm = [tmp[k].next() for k in range(4)]
                    rope(pb[:, 0:128], pb[:, 128:256], pb_r, tb[:, 0, 0:128], tb[:, 1, 0:128], tb_r,
                         qkb[:, 0, 0:128], qkb[:, 0, 128:256], qkb_rr, [(tm[k][0][:, 0, :], tm[k][1]) for k in range(4)])
                    rope(pa[:, 0:128], pa[:, 128:256], pa_r, tb[:, 0, 0:128], tb[:, 1, 0:128], tb_r,
                         qkb[:, 1, 0:128], qkb[:, 1, 128:256], qkb_rr, [(tm[k][0][:, 1, :], tm[k][1]) for k in range(4)])
                    vb, vb_r = vbr.next()
                    P.op("act", I("copy", out=vb[:, 0, :], in_=pa[:, 256:512]), reads=[pa_r], writes=[vb_r])
                    P.op("act", I("activation", out=vb[:, 1, :], in_=pa[:, 256:512], func=AF.Identity, scale=uf),
                         reads=[pa_r, "rv_all"], writes=[vb_r])
                    sg, sg_r = sgr.next()
                    P.op("act", I("activation", out=sg[:], in_=pb[:, 256:512], func=AF.Silu), reads=[pb_r], writes=[sg_r])
                    pT, pT_r = pTr.next()
                    for i in range(4):
                        P.op("pe", I("transpose", out=pT[:, i, :], in_=qkb[:, i // 2, (i % 2) * 128:(i % 2 + 1) * 128], identity=identb[:]),
                             reads=[qkb_rr, "identb"], writes=[pT_r])
                    qkT, qkT_r = qkTr.next()
                    P.op("dve", I("tensor_copy", out=qkT[:], in_=pT[:, 0:4, :]), reads=[pT_r], writes=[qkT_r])
                    return (qkb, qkb_rr, vb, vb_r, sg, sg_r, qkT, qkT_r)

                def p2_main(n, ctx):
                    qkb, qkb_rr, vb, vb_r, sg, sg_r, qkT, qkT_r = ctx
                    pS, pS_r = pSr.next()
                    for dc in range(2):
                        P.op("pe", I("matmul", pS[:, 0:128], lhsT=qkT[:, 2 + dc, :], rhs=qkT[:, dc, :], start=(dc == 0), stop=(dc == 1)),
                             reads=[qkT_r], writes=[pS_r])
                    AT, AT_r = ATr.next()
                    P.op("dve", I("tensor_tensor", out=AT[:], in0=pS[:, 0:128], in1=DT_all[:, h, :], op=ALU.mult),
                         reads=[pS_r, "DT_all"], writes=[AT_r])
                    pO, pO_r = pOr.next()
                    pO2, pO2_r = pO2r.next()
                    for dc in range(2):
                        P.op("pe", I("matmul", pO[:, 256:512], lhsT=qkT[:, dc, :], rhs=Sbf[:, dc, :], start=(dc == 0), stop=(dc == 1)),
                             reads=[qkT_r, "Sbf"], writes=[pO_r])
                    for dc in range(2):
                        P.op("pe", I("matmul", pO2[:, 0:256], lhsT=qkT[:, dc, :], rhs=Sb_all[:, n, dc, :], start=(dc == 0), stop=(dc == 1)),
                             reads=[qkT_r, ("Sb_all", n)], writes=[pO2_r])
                    pI, pI_r = pIr.next()
                    for dc in range(2):
                        P.op("pe", I("matmul", pI[:, dc, :], lhsT=qkb[:, 1, dc * 128:(dc + 1) * 128], rhs=vb[:, 1, :], start=True, stop=True),
                             reads=[qkb_rr, vb_r], writes=[pI_r])
                    P.op("pe", I("matmul", pO[:, 0:256], lhsT=AT[:], rhs=vb[:, 0, :], start=True, stop=True),
                         reads=[AT_r, vb_r], writes=[pO_r])
                    P.op("dve", I("scalar_tensor_tensor", out=Sst[:], in0=Sst[:], scalar=g128f, in1=pI[:], op0=ALU.mult, op1=ALU.add),
                         reads=["Sst", pI_r, "rv_all"], writes=["Sst"])
                    P.op("act", I("copy", out=Sbf[:], in_=Sst[:]), reads=["Sst"], writes=["Sbf"])
                    o1, o1_r = o_r.next()
                    P.op("act", I("copy", out=o1[:], in_=pO[:, 0:256]), reads=[pO_r], writes=[o1_r])
                    P.op("dve", I("scalar_tensor_tensor", out=o1[:], in0=pO[:, 256:512], scalar=wf, in1=o1[:], op0=ALU.mult, op1=ALU.add),
                         reads=[pO_r, o1_r, "rv_all"], writes=[o1_r])
                    P.op("dve", I("scalar_tensor_tensor", out=o1[:], in0=pO2[:, 0:256], scalar=wb_, in1=o1[:], op0=ALU.mult, op1=ALU.add),
                         reads=[pO2_r, o1_r, "rv_all"], writes=[o1_r])
                    sm, sm_r = ln_stats(lambda a, b, o1=o1: o1[:, a:b], 256, [o1_r], 256)
                    P.op("act", I("activation", out=o1[:], in_=o1[:], func=AF.Identity, bias=sm[:, 1:2], scale=sm[:, 0:1]),
                         reads=[o1_r, sm_r], writes=[o1_r])
                    yb, yb_r = ybr.next()
                    P.op("pool", I("tensor_tensor", out=yb[:], in0=o1[:], in1=sg[:], op=ALU.mult), reads=[o1_r, sg_r], writes=[yb_r])
                    return (yb, yb_r)

                def p2_tail(n, ctx):
                    yb, yb_r = ctx
                    pT2, pT2_r = pTr.next()
                    for dc in range(2):
                        P.op("pe", I("transpose", out=pT2[:, 4 + dc, :], in_=yb[:, dc * 128:(dc + 1) * 128], identity=identb[:]),
                             reads=[yb_r, "identb"], writes=[pT2_r])
                    yT, yT_r = yTr.next()
                    P.op("act", I("copy", out=yT[:], in_=pT2[:, 4:6, :]), reads=[pT2_r], writes=[yT_r])
                    dma(yT_d[0][h * 256:(h + 1) * 256, n * 128:(n + 1) * 128].rearrange("(c p) t -> p c t", p=128), yT[:], [yT_r], ["yT_d"],
                        ("yr", yT_r[1]), eng="pool")

                fctx = {0: p2_front(0)}
                tctx = {}
                for n in range(NT):
                    if n + 1 < NT:
                        fctx[n + 1] = p2_front(n + 1)
                    tctx[n] = p2_main(n, fctx.pop(n))
                    if n - 1 >= 0:
                        p2_tail(n - 1, tctx.pop(n - 1))
                p2_tail(NT - 1, tctx.pop(NT - 1))
            gate_step(len(gate_items))

        def phase_att(l, S, lam_init):
            sb = mk(S, "sb")
            ps = mk(S, "ps")
            stg_rot = Rot([sb("astg%d" % i, [128, KC, 512], F32) for i in range(1)], "astg")
            wbf_rot = Rot([sb("awbf%d" % i, [128, KC, 512], BF16) for i in range(1)], "awbf")
            qT = sb("aqT", [128, TOK], BF16)
            kT = sb("akT", [128, TOK], BF16)
            Vx = sb("aVx", [128, NT, 130], BF16)
            sgA = sb("asg", [128, NT, 128], BF16)
            PTr = Rot([sb("aPT%d" % i, [128, NT, 256], BF16) for i in range(2)], "aPT")
            tabr = Rot([sb("atab%d" % i, [128, 2, 128], F32) for i in range(2)], "atab")
            tmp = [Rot([sb("at%d_%d" % (k, i), [128, 4, 32], F32) for i in range(2)], "at%d" % k) for k in range(4)]
            qkbr = Rot([sb("aqkb%d" % i, [128, 256], BF16) for i in range(3)], "aqkb")
            O0r = Rot([sb("aO0%d" % i, [128, 128], F32) for i in range(4)], "aO0")
            ar = Rot([sb("aa%d" % i, [128, 128], F32) for i in range(4)], "aa")
            jr = Rot([sb("aj%d" % i, [128, 128], F32) for i in range(2)], "aj")
            ybr = Rot([sb("ayb%d" % i, [128, 128], BF16) for i in range(6)], "ayb")
            yTr = Rot([sb("ayT%d" % i, [128, 128], BF16) for i in range(2)], "ayT")
            apX = ps("apX", [128, 1024], F32)
            apY = ps("apY", [128, 1024], F32)
            pAr = Rot([apX[:, 0:512], apX[:, 512:1024]], "apA")
            pTr = Rot([ps("apT%d" % i, [128, 8, 128], BF16) for i in range(2)], "apT")
            pS_slots = [(apX, [("apA", 0), ("apA", 1)]), (apY, [("apS", 0)])]
            pS_i = [0]
            pOr = Rot([ps("apO%d" % i, [128, 512], F32) for i in range(2)], "apO")
            P.op("pool", I("memset", Vx[:], 1.0), writes=["aVx"])
            for h in range(8):
                wbf, wbf_r = load_w(w_in[l][:, 4096 + h * 512:4096 + (h + 1) * 512], 512, stg_rot, wbf_rot, "wa")
                def a1_front(t):
                    pa, pa_r = pAr.next()
                    proj_tok(pa, pa_r, t, wbf, wbf_r, 0, 512)
                    tb, tb_r = tabr.next()
                    dma(tb[:, 0, :], cosA[t * 128:(t + 1) * 128, :], [], [tb_r], ("atab", tb_r[1]))
                    dma(tb[:, 1, :], sinA[t * 128:(t + 1) * 128, :], [], [tb_r], ("atab", tb_r[1]))
                    qkb, qkb_r = qkbr.next()
                    tm = [tmp[k].next() for k in range(4)]
                    src = pa[:, 0:256].rearrange("p (a b c) -> p a b c", a=4, b=2, c=32)
                    dst = qkb[:].rearrange("p (a b c) -> p a b c", a=4, b=2, c=32)
                    cA = tb[:, 0, :].rearrange("p (a c) -> p a c", a=4, c=32)
                    sA = tb[:, 1, :].rearrange("p (a c) -> p a c", a=4, c=32)
                    rope(src[:, :, 0, :], src[:, :, 1, :], pa_r, cA, sA, tb_r, dst[:, :, 0, :], dst[:, :, 1, :], qkb_r,
                         [(tm[k][0][:], tm[k][1]) for k in range(4)])
                    P.op("act", I("copy", out=Vx[:, t, 0:128], in_=pa[:, 256:384]), reads=[pa_r], writes=["aVx"])
                    P.op("act", I("activation", out=sgA[:, t, :], in_=pa[:, 384:512], func=AF.Silu), reads=[pa_r], writes=["asg"])
                    return (qkb, qkb_r)

                def a1_back(t, ctx):
                    qkb, qkb_r = ctx
                    pT, pT_r = pTr.next()
                    for i in range(2):
                        P.op("pe", I("transpose", out=pT[:, i, :], in_=qkb[:, i * 128:(i + 1) * 128], identity=identb[:]),
                             reads=[qkb_r, "identb"], writes=[pT_r])
                    P.op("dve", I("tensor_copy", out=qT[:, t * 128:(t + 1) * 128], in_=pT[:, 0, :]), reads=[pT_r], writes=["aqT"])
                    P.op("dve", I("tensor_copy", out=kT[:, t * 128:(t + 1) * 128], in_=pT[:, 1, :]), reads=[pT_r], writes=["akT"])

                actx = {0: a1_front(0)}
                for t in range(NT):
                    if t + 1 < NT:
                        actx[t + 1] = a1_front(t + 1)
                    a1_back(t, actx.pop(t))
                steps = [(b, j) for b in range(NT // 2) for j in range(2)]
                if os.environ.get("ATT_SKIP2"):
                    steps = []
                qstate = {}

                def A_items(b, j):
                    nk = NCTX_T if b == 0 else NT
                    q0 = b * 256
                    PT, PT_r = PTr.next()
                    items = []

                    def mk_item(kg, ng):
                        def f():
                            pSt, pS_rs = pS_slots[pS_i[0] % 2]
                            pS_i[0] += 1
                            pSv = pSt[:, :].rearrange("p (a b) -> p a b", a=4, b=256)
                            for i in range(ng):
                                kt = kg + i
                                P.op("pe", I("matmul", pSv[:, i, :], lhsT=kT[64 * j:64 * j + 64, kt * 128:(kt + 1) * 128],
                                             rhs=qT[64 * j:64 * j + 64, q0:q0 + 256], start=True, stop=True),
                                     reads=["akT", "aqT"], writes=pS_rs)
                            if ng == 4 and not os.environ.get("EXP512"):
                                P.op("act", I("activation", out=PT[:, kg:kg + 4, :], in_=pSv[:, 0:4, :], func=AF.Exp, scale=0.125),
                                     reads=pS_rs, writes=[PT_r])
                            else:
                                for i0 in range(0, ng, 2):
                                    P.op("act", I("activation", out=PT[:, kg + i0:kg + i0 + 2, :], in_=pSv[:, i0:i0 + 2, :], func=AF.Exp, scale=0.125),
                                         reads=pS_rs, writes=[PT_r])
                        return f
                    for kg in range(0, nk, 4):
                        items.append(mk_item(kg, min(4, nk - kg)))
                    return items, (PT, PT_r, nk)

                def B_items(b, j, PTinfo):
                    PT, PT_r, nk = PTinfo
                    items = []
                    for qh in range(2):
                        qt = 2 * b + qh
                        if j == 0:
                            qstate[qt] = (O0r.next(), ar.next())
                        (O0, O0_r), (a_, a_r) = qstate[qt]
                        pO, pO_r = pOr.next()

                        def mk_pv(k0, k1, lastchunk, qt=qt, qh=qh, O0=O0, O0_r=O0_r, a_=a_, a_r=a_r, pO=pO, pO_r=pO_r):
                            def f():
                                for kt in range(k0, k1):
                                    P.op("pe", I("matmul", pO[:, 0:129], lhsT=PT[:, kt, qh * 128:(qh + 1) * 128], rhs=Vx[:, kt, 0:129],
                                                 start=(kt == 0), stop=(kt == nk - 1)),
                                         reads=[PT_r, "aVx"], writes=[pO_r])
                                if lastchunk:
                                    sm, sm_r = smallr.next()
                                    P.op("dve", I("reciprocal", out=sm[:, 0:1], in_=pO[:, 128:129]), reads=[pO_r], writes=[sm_r])
                                    if j == 0:
                                        P.op("dve", I("tensor_scalar", out=O0[:], in0=pO[:, 0:128], scalar1=sm[:, 0:1], scalar2=None, op0=ALU.mult),
                                             reads=[pO_r, sm_r], writes=[O0_r])
                                    else:
                                        P.op("dve", I("tensor_tensor", out=sm[:, 1:2], in0=sm[:, 0:1], in1=lam_t[:, 3:4], op=ALU.mult),
                                             reads=[sm_r, "lam_t"], writes=[sm_r])
                                        P.op("dve", I("scalar_tensor_tensor", out=a_[:], in0=pO[:, 0:128], scalar=sm[:, 1:2], in1=O0[:],
                                                      op0=ALU.mult, op1=ALU.add),
                                             reads=[pO_r, sm_r, O0_r], writes=[a_r])
                                        jk, jk_r = jr.next()
                                        sm2, sm2_r = smallr.next()
                                        P.op("dve", I("scalar_tensor_tensor", out=jk[:], in0=a_[:], scalar=1.0, in1=a_[:], op0=ALU.mult, op1=ALU.mult,
                                                      accum_out=sm2[:, 0:1]),
                                             reads=[a_r], writes=[jk_r, sm2_r])
                                        P.op("dve", I("tensor_scalar", out=sm2[:, 1:2], in0=sm2[:, 0:1], scalar1=1.0 / 128.0, scalar2=LN_EPS,
                                                      op0=ALU.mult, op1=ALU.add), reads=[sm2_r], writes=[sm2_r])
                                        P.op("pool", I("tensor_tensor", out=sm2[:, 3:4], in0=sm2[:, 1:2], in1=mhalf[:, 0:1], op=ALU.pow),
                                             reads=[sm2_r, "mhalf"], writes=[sm2_r])
                                        yb, yb_r = ybr.next()
                                        P.op("dve", I("scalar_tensor_tensor", out=yb[:], in0=a_[:], scalar=sm2[:, 3:4], in1=sgA[:, qt, :],
                                                      op0=ALU.mult, op1=ALU.mult),
                                             reads=[a_r, sm2_r, "asg"], writes=[yb_r])
                                        qstate[qt] = (yb, yb_r)
                            return f
                        cs = int(os.environ.get("ATT_CS", "6"))
                        for k0 in range(0, nk, cs):
                            k1 = min(nk, k0 + cs)
                            items.append(mk_pv(k0, k1, k1 == nk))
                    return items

                def tail_pe(qt):
                    yb, yb_r = qstate.pop(qt)
                    pT, pT_r = pTr.next()
                    P.op("pe", I("transpose", out=pT[:, 0, :], in_=yb[:], identity=identb[:]), reads=[yb_r, "identb"], writes=[pT_r])
                    yT, yT_r = yTr.next()
                    P.op("dve", I("tensor_scalar", out=yT[:], in0=pT[:, 0, :], scalar1=float(1.0 - lam_init), scalar2=None, op0=ALU.mult),
                         reads=[pT_r], writes=[yT_r])
                    dma(yT_d[2][h * 128:(h + 1) * 128, qt * 128:(qt + 1) * 128], yT[:], [yT_r], ["yT_d"], ("ya", yT_r[1]), eng="pool")

                prevB = []
                tails_now = []
                for si in range(len(steps) + 1):
                    if si < len(steps):
                        b, j = steps[si]
                        Ai, PTinfo = A_items(b, j)
                    else:
                        Ai, PTinfo = [], None
                    for k in range(max(len(Ai), len(prevB))):
                        if k < len(Ai):
                            Ai[k]()
                        if k < len(prevB):
                            prevB[k]()
                    for tq in tails_now:
                        tail_pe(tq)
                    tails_now = []
                    if si >= 1 and steps[si - 1][1] == 1:
                        tails_now += [2 * steps[si - 1][0], 2 * steps[si - 1][0] + 1]
                    if not steps:
                        break
                    prevB = B_items(b, j, PTinfo) if si < len(steps) else []
                for tq in tails_now:
                    tail_pe(tq)

        def phase_gates(l, S):
            sb = mk(S, "sb")
            ps = mk(S, "ps")
            stg_rot = Rot([sb("gstg%d" % i, [128, KC, 512], F32) for i in range(2)], "gstg")
            wbf_rot = Rot([sb("gwbf%d" % i, [128, KC, 512], BF16) for i in range(2)], "gwbf")
            gsr = Rot([sb("ggs%d" % i, [128, 512], BF16) for i in range(3)], "ggs")
            pAr = Rot([ps("gpA%d" % i, [128, 512], F32) for i in range(3)], "gpA")
            for cb in range(6):
                wbf, wbf_r = load_w(w_in[l][:, 12288 + cb * 512:12288 + (cb + 1) * 512], 512, stg_rot, wbf_rot, "wg")
                for t in range(NT):
                    pa, pa_r = pAr.next()
                    proj_tok(pa, pa_r, t, wbf, wbf_r, 0, 512)
                    gs, gs_r = gsr.next()
                    P.op("act", I("activation", out=gs[:], in_=pa[:], func=AF.Sigmoid), reads=[pa_r], writes=[gs_r])
                    dma(gates_d[t * 128:(t + 1) * 128, cb * 512:(cb + 1) * 512], gs[:], [gs_r], ["gates_d"], ("gg", gs_r[1]), eng="pool")

        def phase_B(l, S, last):
            sb = mk(S, "sb")
            ps = mk(S, "ps")
            wts = []
            wtl = [sb("bw%d" % wi, [128, KC, D], BF16) for wi in range(4)]
            with contextlib.ExitStack() as S2:
                sb2 = mk(S2, "sb")
                stg_rot = Rot([sb2("bstg%d" % i, [128, KC, 512], F32) for i in range(2)], "bstg")
                for wi, wsrc in enumerate((w_ro, w_co, w_ao, w_oo)):
                    wt = wtl[wi]
                    for half in range(2):
                        stg, stg_r = stg_rot.next()
                        dma(stg[:], wsrc[l][:, half * 512:(half + 1) * 512].rearrange("(c p) n -> p c n", p=128), [], [stg_r], ("bw", stg_r[1]))
                        for q in range(4):
                            en = ("pool", "dve", "act", "pool")[q]
                            if en == "act":
                                P.op(en, I("copy", out=wt[:, 2 * q:2 * q + 2, half * 512:(half + 1) * 512],
                                                                                       in_=stg[:, 2 * q:2 * q + 2, :]),
                                     reads=[stg_r], writes=["bw%d" % wi])
                            else:
                                P.op(en, I("tensor_copy", out=wt[:, 2 * q:2 * q + 2, half * 512:(half + 1) * 512],
                                                                                              in_=stg[:, 2 * q:2 * q + 2, :]),
                                     reads=[stg_r], writes=["bw%d" % wi])
                    wts.append((wt, "bw%d" % wi))
                P.barrier()
            gate_rep = sb("gate_rep", [128, 2, D], F32)
            gate_holder[0] = gate_rep
            with contextlib.ExitStack() as S3:
                setup_gate(l, S3)
                P.barrier()
            xnb_holder[0] = Rot([sb("xnb%d" % i, [128, D], BF16) for i in range(2)], "xnb")
            lngt = sb("lngt", [128, D], F32)
            lnbt = sb("lnbt", [128, D], F32)
            dma(lngt[:], lng[l], [], ["lngt"], "bl0")
            dma(lnbt[:], lnb[l], [], ["lnbt"], "bl1")
            gtr = Rot([sb("bgt%d" % i, [128, 3 * D], BF16) for i in range(1)], "bgt")
            xtr = Rot([sb("bxt%d" % i, [128, D], F32) for i in range(1)], "bxt")
            ytr = Rot([sb("byt%d" % i, [128, 3, KC, 128], BF16) for i in range(1)], "byt")
            mt = sb("bmt_", [128, 512], F32)
            t2 = sb("bt2", [128, 512], F32)
            mb = sb("bmb", [128, D], BF16)
            mT = sb("bmT", [128, KC, 128], BF16)
            xnr = Rot([sb("bxn%d" % i, [128, D], F32) for i in range(2)], "bxn")
            pRr = Rot([ps("bpR%d" % i, [128, 512], F32) for i in range(2)], "bpR")
            pTr = Rot([ps("bpT%d" % i, [128, 8, 128], BF16) for i in range(4)], "bpT")
            pOr = Rot([ps("bpO%d" % i, [128, 512], F32) for i in range(2)], "bpO")
            xsrc_d = xin if l == 0 else xbuf_d
            ztr = Rot([sb("bzt%d" % i, [128, D], F32) for i in range(2)], "bzt")
            mbr = Rot([mb, sb("bmb2", [128, D], BF16)], "bmb")

            def b_front(t):
                yt, yt_r = ytr.next()
                for br in range(3):
                    dma(yt[:, br, :, :], yT_d[br][:, t * 128:(t + 1) * 128].rearrange("(c p) t -> p c t", p=128), ["yT_d"], [yt_r], ("by", br))
                gt, gt_r = gtr.next()
                dma(gt[:], gates_d[t * 128:(t + 1) * 128, :], ["gates_d"], [gt_r], "bg")
                mbt, mbt_r = mbr.next()

                def half(nb):
                    def f():
                        for br in range(3):
                            wt, wt_r = wts[br]
                            pR, pR_r = pRr.next()
                            for kc in range(KC):
                                P.op("pe", I("matmul", pR[:], lhsT=yt[:, br, kc, :], rhs=wt[:, kc, nb * 512:(nb + 1) * 512],
                                             start=(kc == 0), stop=(kc == KC - 1)),
                                     reads=[yt_r, wt_r], writes=[pR_r])
                            gsl = gt[:, br * D + nb * 512:br * D + (nb + 1) * 512]
                            dst = mt if br == 0 else t2
                            dst_r = "bmt_" if br == 0 else "bt2"
                            P.op("dve", I("tensor_tensor", out=dst[:], in0=pR[:], in1=gsl, op=ALU.mult),
                                 reads=[pR_r, gt_r], writes=[dst_r])
                            if br == 1:
                                P.op("dve", I("tensor_tensor", out=mt[:], in0=mt[:], in1=t2[:], op=ALU.add), reads=["bmt_", "bt2"], writes=["bmt_"])
                            if br == 2:
                                P.op("dve", I("tensor_tensor", out=mbt[:, nb * 512:(nb + 1) * 512], in0=mt[:], in1=t2[:], op=ALU.add),
                                     reads=["bmt_", "bt2"], writes=[mbt_r])
                    return f
                return (mbt, mbt_r), [half(0), half(1)]

            def b_back(t, ctx):
                mbt, mbt_r = ctx
                wh = 1 if t < NCTX_T else 0
                xn, xn_r = xnr.next()
                zt, zt_r = ztr.next()
                xt, xt_r = xtr.next()
                st = {}

                def part1():
                    dma(xt[:], xsrc_d[t * 128:(t + 1) * 128, :], [("xbuf", t)], [xt_r], ("bx", xt_r[1]))
                    pTa, pTa_r = pTr.next()
                    pTb, pTb_r = pTr.next()
                    for kc in range(KC):
                        pT, pT_r = (pTa, pTa_r) if kc < 4 else (pTb, pTb_r)
                        P.op("pe", I("transpose", out=pT[:, kc, :], in_=mbt[:, kc * 128:(kc + 1) * 128], identity=identb[:]),
                             reads=[mbt_r, "identb"], writes=[pT_r])
                    P.op("act", I("copy", out=mT[:, 0:4, :], in_=pTa[:, 0:4, :]), reads=[pTa_r], writes=["bmT"])
                    P.op("dve", I("tensor_copy", out=mT[:, 4:8, :], in_=pTb[:, 4:8, :]), reads=[pTb_r], writes=["bmT"])
                    wo, wo_r = wts[3]
                    for nb in range(2):
                        pO, pO_r = pOr.next()
                        for kc in range(KC):
                            P.op("pe", I("matmul", pO[:], lhsT=mT[:, kc, :], rhs=wo[:, kc, nb * 512:(nb + 1) * 512],
                                         start=(kc == 0), stop=(kc == KC - 1)),
                                 reads=["bmT", wo_r], writes=[pO_r])
                        P.op("dve", I("tensor_tensor", out=zt[:, nb * 512:(nb + 1) * 512], in0=pO[:],
                                      in1=gate_rep[:, wh, nb * 512:(nb + 1) * 512], op=ALU.mult),
                             reads=[pO_r, "gate_rep"], writes=[zt_r])
                    P.op("dve", I("scalar_tensor_tensor", out=zt[:], in0=xt[:], scalar=float(ALPHA), in1=zt[:], op0=ALU.mult, op1=ALU.add),
                         reads=[xt_r, zt_r], writes=[zt_r])
                    st["sm"] = ln_stats(lambda a, b: zt[:, a:b], D, [zt_r], D)

                def part2():
                    sm, sm_r = st["sm"]
                    P.op("act", I("activation", out=xn[:], in_=zt[:], func=AF.Identity, bias=sm[:, 1:2], scale=sm[:, 0:1]),
                         reads=[zt_r, sm_r], writes=[xn_r])
                    P.op("pool", I("tensor_tensor", out=xn[:], in0=xn[:], in1=lngt[:], op=ALU.mult), reads=[xn_r, "lngt"], writes=[xn_r])
                    P.op("dve", I("tensor_tensor", out=xn[:], in0=xn[:], in1=lnbt[:], op=ALU.add), reads=[xn_r, "lnbt"], writes=[xn_r])
                    if not last:
                        dma(xbuf_d[t * 128:(t + 1) * 128, :], xn[:], [xn_r], [("xbuf", t)], "bxo", eng="pool")
                    elif t >= NCTX_T:
                        dma(out_d[(t - NCTX_T) * 128:(t - NCTX_T + 1) * 128, :], xn[:], [xn_r], [], "bxo", eng="pool")

                def part3():
                    if not last:
                        make_hT(t, xn, xn_r, *pTr.next(), *pTr.next())
                return [part1, part2, part3]

            c0, h0 = b_front(0)
            h0[0]()
            h0[1]()
            bctx = {0: c0}
            for t in range(NT):
                halves = [lambda: None, lambda: None]
                if t + 1 < NT:
                    bctx[t + 1], halves = b_front(t + 1)
                parts = b_back(t, bctx.pop(t))
                parts[0]()
                halves[0]()
                parts[1]()
                halves[1]()
                parts[2]()

        with contextlib.ExitStack() as S:
            if upto != "const":
                setup_mod(0, S)
            sb = mk(S, "sb")
            ps = mk(S, "ps")
            xnb_holder[0] = Rot([sb("xnb%d" % i, [128, D], BF16) for i in range(2)], "xnb")
            xtr0 = Rot([sb("pxt%d" % i, [128, D], F32) for i in range(2)], "pxt")
            pTr0 = Rot([ps("ppT%d" % i, [128, 8, 128], BF16) for i in range(4)], "ppT")
            for t in range((NT if not (upto or "").startswith("pro") or upto == "pro" else int(upto[3:])) if upto not in ("const", "mod") else 0):
                xt, xt_r = xtr0.next()
                dma(xt[:], xin[t * 128:(t + 1) * 128, :], [], [xt_r], ("px", xt_r[1]))
                make_hT(t, xt, xt_r, *pTr0.next(), *pTr0.next())
            P.barrier()
            if dbg:
                for kc in range(KC):
                    dma(hT_dbg[:, kc, :], hT[:, kc, :], [("hT", t) for t in range(NT)], [], "dbgh")
                P.barrier()
        for l in range(n_layers):
            last = (l == DEPTH - 1)
            if upto in ("const", "mod") or (upto or "").startswith("pro"):
                break
            with contextlib.ExitStack() as S:
                lam_init = setup_rest(l, S)
                P.barrier()
            if upto == "rest":
                break
            if upto in (None, "conv", "B"):
                with contextlib.ExitStack() as S:
                    phase_conv(l, S)
                    P.barrier()
            if upto == "conv":
                break
            if upto in (None, "ret", "B"):
                with contextlib.ExitStack() as S:
                    phase_ret(l, S)
                    P.barrier()
            if upto == "ret":
                break
            if upto in (None, "att", "B"):
                with contextlib.ExitStack() as S:
                    phase_att(l, S, lam_init)
                    P.barrier()
            if upto == "att":
                break
            if upto == "gates":
                with contextlib.ExitStack() as S:
                    phase_gates(l, S)
                    P.barrier()
            if upto == "gates":
                break
            if not last:
                with contextlib.ExitStack() as S:
                    setup_mod(l + 1, S)
                    P.barrier()
            with contextlib.ExitStack() as S:
                phase_B(l, S, last)
                P.barrier()
        P.emit()
    return nc, P


def _rope_tables(head_dim, reps):
    n_freq = head_dim // 4
    inv = (10000.0 ** (-np.arange(n_freq, dtype=np.float32) / np.float32(n_freq))).astype(np.float32)
    rows = np.repeat(np.arange(64, dtype=np.float32), 64)
    cols = np.tile(np.arange(64, dtype=np.float32), 64)
    ang = np.concatenate([rows[:, None] * inv, cols[:, None] * inv], axis=-1).astype(np.float32)
    c = np.cos(ang).astype(np.float32)
    s = np.sin(ang).astype(np.float32)
    half = head_dim // 2
    c = np.concatenate([np.ones((256, half), np.float32), c], 0)
    s = np.concatenate([np.zeros((256, half), np.float32), s], 0)
    return np.ascontiguousarray(np.tile(c, (1, reps))), np.ascontiguousarray(np.tile(s, (1, reps)))


def _perm_cols():
    seg = {}
    names = ["rk", "rv", "ak", "av", "rq", "rg", "aq", "ag", "cb", "cc", "cx", "cg"]
    for i, n in enumerate(names):
        seg[n] = i * 1024
    idx = []
    for h in range(4):
        for n in ("rk", "rv", "rq", "rg"):
            idx.extend(range(seg[n] + h * 256, seg[n] + (h + 1) * 256))
    for h in range(8):
        for n in ("aq", "ak", "av", "ag"):
            idx.extend(range(seg[n] + h * 128, seg[n] + (h + 1) * 128))
    for g in range(8):
        for n in ("cb", "cc", "cx", "cg"):
            idx.extend(range(seg[n] + g * 128, seg[n] + (g + 1) * 128))
    idx.extend(range(12288, 15360))
    return np.asarray(idx, dtype=np.int64)


def _host_inputs(x, c, ctx, c_ctx, w_mod, b_mod, w_in, ret_decay, conv_w, diff_lambda,
                 w_ret_out, w_conv_out, w_att_out, w_out, ln_g, ln_b):
    f = np.float32
    cosR, sinR = _rope_tables(256, 2)
    cosA, sinA = _rope_tables(64, 4)
    p = np.arange(128, dtype=f)
    dist = p[None, :] - p[:, None]
    dconst = np.stack([np.maximum(dist, 0), (dist >= 0).astype(f), np.maximum(-dist, 0), (dist < 0).astype(f)], 1).astype(f)
    posv = np.stack([p + 1, 128 - p, 127 - p, p], 1).astype(f)
    w_in_p = np.ascontiguousarray(np.asarray(w_in, f)[:, :, _perm_cols()])
    bm = np.asarray(b_mod, f)
    bmodT = np.ascontiguousarray(bm[:, :2048].reshape(DEPTH, 16, 128).transpose(0, 2, 1))
    bgate = np.ascontiguousarray(np.broadcast_to(bm[:, None, 2048:], (DEPTH, 128, D)))
    shared = dict(
        w_mod=np.ascontiguousarray(w_mod, f), bmodT=bmodT, bgate=bgate, w_in=w_in_p,
        rdec=np.ascontiguousarray(np.broadcast_to(np.asarray(ret_decay, f).reshape(1, 32), (128, 32))),
        convw=np.ascontiguousarray(np.asarray(conv_w, f).reshape(DEPTH, 3, 8, 128).transpose(0, 3, 2, 1)),
        dlam=np.ascontiguousarray(np.broadcast_to(np.asarray(diff_lambda, f).reshape(1, DEPTH * 256), (128, DEPTH * 256))),
        w_ro=np.ascontiguousarray(w_ret_out, f), w_co=np.ascontiguousarray(w_conv_out, f),
        w_ao=np.ascontiguousarray(w_att_out, f), w_oo=np.ascontiguousarray(w_out, f),
        lng=np.ascontiguousarray(np.broadcast_to(np.asarray(ln_g, f)[:, None, :], (DEPTH, 128, D))),
        lnb=np.ascontiguousarray(np.broadcast_to(np.asarray(ln_b, f)[:, None, :], (DEPTH, 128, D))),
        ident=np.eye(128, dtype=f), cosR=cosR, sinR=sinR, cosA=cosA, sinA=sinA, dconst=dconst, posv=posv,
    )
    in_maps = []
    for core in range(8):
        b = core % 4
        m = dict(shared)
        m["xin"] = np.ascontiguousarray(np.concatenate([np.asarray(ctx[b], f), np.asarray(x[b], f)], 0))
        cv = np.stack([np.asarray(c[b], f), np.asarray(c_ctx, f)], -1)
        m["cvec"] = np.ascontiguousarray(cv.reshape(KC, 128, 2).transpose(1, 0, 2))
        in_maps.append(m)
    return in_maps


_CACHE = {}


def kernel(x, c, ctx, c_ctx, w_mod, b_mod, w_in, ret_decay, conv_w, diff_lambda,
           w_ret_out, w_conv_out, w_att_out, w_out, ln_g, ln_b):
    in_maps = _host_inputs(x, c, ctx, c_ctx, w_mod, b_mod, w_in, ret_decay, conv_w, diff_lambda,
                           w_ret_out, w_conv_out, w_att_out, w_out, ln_g, ln_b)
    if "nc" not in _CACHE:
        _CACHE["nc"] = build_program()[0]
    res = run_bass_kernel_spmd(_CACHE["nc"], in_maps, core_ids=list(range(8)))
    out = np.stack([np.asarray(res.results[b]["out"], np.float32) for b in range(4)], 0)
    return out
```

```python
import math
import os
import contextlib
import numpy as np
import concourse.bass as bass
import concourse.mybir as mybir
from concourse.bass_utils import run_bass_kernel_spmd

F32 = mybir.dt.float32
BF16 = mybir.dt.bfloat16
AF = mybir.ActivationFunctionType
ALU = mybir.AluOpType
AX = mybir.AxisListType

ALLENG = ("sp", "pe", "act", "dve", "pool")

D = 1024
KC = 8
NT = 34
TOK = NT * 128
NCTX_T = 2
DEPTH = 4
LN_EPS = 1e-6
ALPHA = (2 * DEPTH) ** 0.25
IN_COLS = 15360


PSUM_NAMES = {"cps", "rpA", "rpB", "rpT", "rpS", "rpO", "rpO2", "rpI", "apA", "apT", "apS", "apO", "gpA", "bpR", "bpT", "bpO",
              "ppT", "pm", "pg"}


def I(name, *a, **kw):
    return (name, a, kw)


class Prog:
    def __init__(self, nc):
        self.nc = nc
        self.ops = []
        self.last_w = {}
        self.readers = {}
        self.dsem_cnt = {}
        self.last_eng = {}
        self.last_dma = {}
        self.pending = {}
        self.trace = [] if os.environ.get("PTRACE") else None

    def op(self, eng, fn, reads=(), writes=(), dsem=None):
        i = len(self.ops)
        deps = {}
        for r in reads:
            j = self.last_w.get(r)
            if j is not None:
                deps[j] = True
            rn = r[0] if isinstance(r, tuple) else r
            if rn in PSUM_NAMES:
                for j in self.readers.get(r, {}).values():
                    if self.ops[j]["eng"] != eng:
                        deps.setdefault(j, False)
        for w in writes:
            j = self.last_w.get(w)
            if j is not None:
                deps.setdefault(j, False)
            for j in self.readers.get(w, {}).values():
                deps.setdefault(j, False)
        pb = self.pending.pop(eng, None)
        if pb:
            for j in pb:
                deps.setdefault(j, False)
        o = dict(eng=eng, fn=fn, deps=deps, dsem=dsem, dcount=None, sig=None)
        if dsem is not None:
            c = self.dsem_cnt.get(dsem, 0) + 16
            self.dsem_cnt[dsem] = c
            o["dcount"] = c
            self.last_dma[dsem] = i
        self.ops.append(o)
        self.last_eng[eng] = i
        rk = eng if dsem is None else ("d", dsem)
        for r in reads:
            self.readers.setdefault(r, {})[rk] = i
        for w in writes:
            self.last_w[w] = i
            self.readers[w] = {}
        return i

    def barrier(self):
        s = set(self.last_eng.values()) | set(self.last_dma.values())
        for e in ALLENG:
            self.pending[e] = set(s) | self.pending.get(e, set())

    def emit(self):
        nc = self.nc
        ops = self.ops
        need_sig = [False] * len(ops)
        for i, o in enumerate(ops):
            for j, raw in o["deps"].items():
                pj = ops[j]
                if pj["dsem"] is not None:
                    continue
                if pj["eng"] == o["eng"] and o["dsem"] is None:
                    if raw and o["eng"] != "pe":
                        need_sig[j] = True
                else:
                    need_sig[j] = True
        cnt = {e: 0 for e in ALLENG}
        for i, o in enumerate(ops):
            if o["dsem"] is None and need_sig[i]:
                cnt[o["eng"]] += 1
                o["sig"] = cnt[o["eng"]]
        dkeys = sorted(self.dsem_cnt.keys(), key=str)
        with contextlib.ExitStack() as st:
            esem = {e: st.enter_context(nc.semaphore("s_" + e)) for e in ALLENG}
            dsem = {k: st.enter_context(nc.semaphore("d_%d" % n)) for n, k in enumerate(dkeys)}
            block = st.enter_context(nc.Block())
            per_eng = {e: [] for e in ALLENG}
            for i, o in enumerate(ops):
                per_eng[o["eng"]].append(i)

            def run(engname, eng):
                waited = {}
                for i in per_eng[engname]:
                    o = ops[i]
                    for j, raw in sorted(o["deps"].items()):
                        pj = ops[j]
                        if pj["dsem"] is not None:
                            key = ("d", pj["dsem"])
                            val = pj["dcount"]
                            sem = dsem[pj["dsem"]]
                        else:
                            if pj["sig"] is None:
                                continue
                            if pj["eng"] == engname and o["dsem"] is None and not (raw and engname != "pe"):
                                continue
                            key = ("e", pj["eng"])
                            val = pj["sig"]
                            sem = esem[pj["eng"]]
                        if waited.get(key, 0) >= val:
                            continue
                        waited[key] = val
                        eng.wait_ge(sem, val)
                        if self.trace is not None:
                            self.trace.append((engname, "WAIT", key, val))
                    nm, a_, kw_ = o["fn"]
                    ins = getattr(eng, nm)(*a_, **kw_)
                    if self.trace is not None:
                        self.trace.append((engname, nm, i, o["sig"], o["dsem"], o["dcount"]))
                    if o["dsem"] is not None:
                        ins.then_inc(dsem[o["dsem"]], 16)
                    elif o["sig"] is not None:
                        ins.then_inc(esem[engname], 1)
                if engname == "sp":
                    for k in dkeys:
                        eng.wait_ge(dsem[k], self.dsem_cnt[k])

            block.sync(lambda e: run("sp", e))
            block.tensor(lambda e: run("pe", e))
            block.scalar(lambda e: run("act", e))
            block.vector(lambda e: run("dve", e))
            block.gpsimd(lambda e: run("pool", e))
        self.stats = dict(n_ops=len(ops), sig=cnt, ndsem=len(dkeys))


class Rot:
    def __init__(self, tiles, name):
        self.tiles = tiles
        self.name = name
        self.i = 0

    def next(self):
        k = self.i % len(self.tiles)
        self.i += 1
        return self.tiles[k], (self.name, k)


def build_program(n_layers=DEPTH, dbg=False, upto=None):
    nc = bass.Bass("TRN2", target_bir_lowering=False)

    def din(name, shape, dt=F32):
        return nc.dram_tensor(name, list(shape), dt, kind="ExternalInput").ap()

    xin = din("xin", [TOK, D])
    cvec = din("cvec", [128, KC, 2])
    w_mod = din("w_mod", [DEPTH, D, 3 * D])
    bmodT = din("bmodT", [DEPTH, 128, 16])
    bgate = din("bgate", [DEPTH, 128, D])
    w_in = din("w_in", [DEPTH, D, IN_COLS])
    rdec = din("rdec", [128, 32])
    convw = din("convw", [DEPTH, 128, 8, 3])
    dlam = din("dlam", [128, DEPTH * 256])
    w_ro = din("w_ro", [DEPTH, D, D])
    w_co = din("w_co", [DEPTH, D, D])
    w_ao = din("w_ao", [DEPTH, D, D])
    w_oo = din("w_oo", [DEPTH, D, D])
    lng = din("lng", [DEPTH, 128, D])
    lnb = din("lnb", [DEPTH, 128, D])
    ident_d = din("ident", [128, 128])
    cosR = din("cosR", [TOK, 256])
    sinR = din("sinR", [TOK, 256])
    cosA = din("cosA", [TOK, 128])
    sinA = din("sinA", [TOK, 128])
    dconst = din("dconst", [128, 4, 128])
    posv_d = din("posv", [128, 4])
    out_d = nc.dram_tensor("out", [TOK - 256, D], F32, kind="ExternalOutput").ap()
    skind = dict(kind="ExternalOutput") if dbg else {}
    yT_d = nc.dram_tensor("yT_s", [3, D, TOK], BF16, **skind).ap()
    gates_d = nc.dram_tensor("gates_s", [TOK, 3 * D], BF16, **skind).ap()
    xbuf_d = nc.dram_tensor("xbuf_s", [TOK, D], F32, **skind).ap()

    hT_dbg = nc.dram_tensor("hT_dbg", [128, KC, TOK], BF16, kind="ExternalOutput").ap() if dbg else None
    P = Prog(nc)
    G = contextlib.ExitStack()

    uid = [0]

    def mk(stack, kind):
        def f(name, shape, dt):
            uid[0] += 1
            name = "%s_u%d" % (name, uid[0])
            if kind == "sb":
                return stack.enter_context(nc.sbuf_tensor(name, list(shape), dt))
            return stack.enter_context(nc.psum_tensor(name, list(shape), dt))
        return f

    gsb = mk(G, "sb")

    with G:
        hT = gsb("hT", [128, KC, TOK], BF16)
        identf = gsb("identf", [128, 128], F32)
        identb = gsb("identb", [128, 128], BF16)
        dcon = gsb("dcon", [128, 4, 128], F32)
        posv = gsb("posv", [128, 4], F32)
        lg_all = gsb("lg_all", [128, 32], F32)
        scv = gsb("scv", [128, KC, 2], F32)
        scvb = gsb("scvb", [128, KC, 2], BF16)
        screp = gsb("screp", [128, KC, 2, 128], BF16)
        onesb = gsb("onesb", [128, 128], F32)
        mhalf = gsb("mhalf", [128, 2], F32)
        modT = gsb("modT", [128, 16, 2], F32)
        gate_holder = [None]
        DT_all = gsb("DT_all", [128, 4, 128], F32)
        rv_all = gsb("rv_all", [128, 4, 8], F32)
        lam_t = gsb("lam_t", [128, 4], F32)
        smallr = Rot([gsb("small%d" % i, [128, 8], F32) for i in range(6)], "small")
        statr = Rot([gsb("stat%d" % i, [128, 2, 6], F32) for i in range(3)], "stat")
        xnb_holder = [None]

        def dma(out, in_, reads, writes, key, eng="sp"):
            P.op(eng, I("dma_start", out=out, in_=in_), reads=reads, writes=writes, dsem=key)

        dma(identf[:], ident_d, [], ["identf"], "c0")
        dma(dcon[:], dconst, [], ["dcon"], "c1")
        dma(posv[:], posv_d, [], ["posv"], "c2")
        dma(lg_all[:], rdec, [], ["lg_all"], "c3")
        dma(scv[:], cvec, [], ["scv"], "c5")
        P.op("pool", I("tensor_copy", out=identb[:], in_=identf[:]), reads=["identf"], writes=["identb"])
        P.op("pool", I("memset", onesb[:], 1.0), writes=["onesb"])
        P.op("pool", I("memset", mhalf[:], -0.5), writes=["mhalf"])
        P.op("act", I("activation", out=lg_all[:], in_=lg_all[:], func=AF.Exp), reads=["lg_all"], writes=["lg_all"])
        P.op("dve", I("tensor_scalar", out=lg_all[:], in0=lg_all[:], scalar1=-1.0, scalar2=None, op0=ALU.mult),
             reads=["lg_all"], writes=["lg_all"])
        P.op("act", I("activation", out=scv[:], in_=scv[:], func=AF.Silu), reads=["scv"], writes=["scv"])
        P.op("dve", I("tensor_copy", out=scvb[:], in_=scv[:]), reads=["scv"], writes=["scvb"])
        for kc in range(KC):
            for wh in range(2):
                P.op("act", I("activation", out=screp[:, kc, wh, :], in_=onesb[:], func=AF.Identity,
                                                                   scale=scv[:, kc, wh:wh + 1]),
                     reads=["onesb", "scv"], writes=["screp"])

        def ln_stats(src_ap_fn, n, rsrc, width):
            st_t, st_r = statr.next()
            sm, sm_r = smallr.next()
            nh = (width + 511) // 512
            for i in range(nh):
                w0, w1 = i * 512, min(width, (i + 1) * 512)
                P.op("dve", I("bn_stats", out=st_t[:, i, :], in_=src_ap_fn(w0, w1)),
                     reads=rsrc, writes=[st_r])
            P.op("dve", I("bn_aggr", out=sm[:, 2:4], in_=st_t[:, 0:nh, :]), reads=[st_r], writes=[sm_r])
            P.op("dve", I("tensor_scalar", out=sm[:, 4:5], in0=sm[:, 3:4], scalar1=LN_EPS, scalar2=None, op0=ALU.add),
                 reads=[sm_r], writes=[sm_r])
            P.op("pool", I("tensor_tensor", out=sm[:, 0:1], in0=sm[:, 4:5], in1=mhalf[:, 0:1], op=ALU.pow), reads=[sm_r, "mhalf"], writes=[sm_r])
            P.op("dve", I("scalar_tensor_tensor", out=sm[:, 1:2], in0=sm[:, 2:3], scalar=-1.0, in1=sm[:, 0:1],
                                                         op0=ALU.mult, op1=ALU.mult), reads=[sm_r], writes=[sm_r])
            return sm, sm_r

        def make_hT(t, xsrc, xres, pTa, pTa_r, pTb, pTb_r):
            wh = 1 if t < NCTX_T else 0
            sm, sm_r = ln_stats(lambda a, b: xsrc[:, a:b], D, [xres], D)
            xnb, xnb_r = xnb_holder[0].next()
            P.op("act", I("activation", out=xnb[:], in_=xsrc[:], func=AF.Identity, bias=sm[:, 1:2], scale=sm[:, 0:1]),
                 reads=[xres, sm_r], writes=[xnb_r])
            for kc in range(KC):
                pT, pT_r = (pTa, pTa_r) if kc < 4 else (pTb, pTb_r)
                P.op("pe", I("transpose", out=pT[:, kc, :], in_=xnb[:, kc * 128:(kc + 1) * 128], identity=identb[:]),
                     reads=[xnb_r, "identb"], writes=[pT_r])
            for kc in range(KC):
                dst = hT[:, kc, t * 128:(t + 1) * 128]
                if kc < 4:
                    P.op("dve", I("tensor_scalar", out=dst, in0=pTa[:, kc, :], scalar1=modT[:, 8 + kc, wh:wh + 1],
                                  scalar2=modT[:, kc, wh:wh + 1], op0=ALU.mult, op1=ALU.add),
                         reads=[pTa_r, "modT"], writes=[("hT", t)])
                else:
                    P.op("act", I("activation", out=dst, in_=pTb[:, kc, :], func=AF.Identity,
                                  bias=modT[:, kc, wh:wh + 1], scale=modT[:, 8 + kc, wh:wh + 1]),
                         reads=[pTb_r, "modT"], writes=[("hT", t)])

        cast_i = [0]

        def load_w(src, ncols, stg_rot, wbf_rot, key):
            stg, stg_r = stg_rot.next()
            wbf, wbf_r = wbf_rot.next()
            dma(stg[:, :, 0:ncols], src.rearrange("(c p) n -> p c n", p=128), [], [stg_r], (key, stg_r[1]))
            engs = ("pool", "dve", "act", "pool")
            for q in range(4):
                en = engs[(q + cast_i[0]) % 4]
                if en == "act":
                    P.op(en, I("copy", out=wbf[:, 2 * q:2 * q + 2, 0:ncols], in_=stg[:, 2 * q:2 * q + 2, 0:ncols]),
                         reads=[stg_r], writes=[wbf_r])
                else:
                    P.op(en, I("tensor_copy", out=wbf[:, 2 * q:2 * q + 2, 0:ncols], in_=stg[:, 2 * q:2 * q + 2, 0:ncols]),
                         reads=[stg_r], writes=[wbf_r])
            cast_i[0] += 1
            return wbf, wbf_r

        def proj_tok(ps, ps_r, t, wbf, wbf_r, c0, c1):
            for kc in range(KC):
                P.op("pe", I("matmul", ps[:, 0:c1 - c0], lhsT=hT[:, kc, t * 128:(t + 1) * 128], rhs=wbf[:, kc, c0:c1],
                                                    start=(kc == 0), stop=(kc == KC - 1)),
                     reads=[("hT", t), wbf_r], writes=[ps_r])

        def setup_mod(l, S):
            sb = mk(S, "sb")
            ps = mk(S, "ps")
            stg = sb("mstg", [128, KC, 512], F32)
            stgb = sb("mstgb", [128, KC, 512], BF16)
            bmt = sb("bmt", [128, 16], F32)
            pm_full = ps("pm", [128, 16, 32], F32)
            pm = pm_full[:, :, 0:2]
            dma(bmt[:], bmodT[l], [], ["bmt"], "m0")
            for jb in range(4):
                dma(stg[:], w_mod[l][:, jb * 512:(jb + 1) * 512].rearrange("(c p) n -> p c n", p=128), [], ["mstg"], "m1")
                P.op("pool", I("tensor_copy", out=stgb[:], in_=stg[:]), reads=["mstg"], writes=["mstgb"])
                for fc in range(4):
                    j = jb * 4 + fc
                    for kc in range(KC):
                        P.op("pe", I("matmul", pm[:, j, :], lhsT=stgb[:, kc, fc * 128:(fc + 1) * 128],
                                                                          rhs=scvb[:, kc, :], start=(kc == 0), stop=(kc == KC - 1)),
                             reads=["mstgb", "scvb"], writes=["pm"])
            for wh in range(2):
                P.op("dve", I("tensor_tensor", out=modT[:, :, wh], in0=pm[:, :, wh], in1=bmt[:], op=ALU.add),
                     reads=["pm", "bmt"], writes=["modT"])
            P.op("dve", I("tensor_scalar", out=modT[:, 8:16, :], in0=modT[:, 8:16, :], scalar1=1.0, scalar2=None, op0=ALU.add),
                 reads=["modT"], writes=["modT"])

        def setup_gate(l, S):
            gate_rep = gate_holder[0]
            sb = mk(S, "sb")
            ps = mk(S, "ps")
            stg = sb("gstg", [128, KC, 512], F32)
            stgb = sb("gstgb", [128, KC, 512], BF16)
            bg = sb("bg", [128, D], F32)
            pg = ps("pg", [128, 512], F32)
            dma(bg[:], bgate[l], [], ["bg"], "g0")
            for jb in range(2):
                dma(stg[:], w_mod[l][:, (4 + jb) * 512:(5 + jb) * 512].rearrange("(c p) n -> p c n", p=128), [], ["gstg"], "g1")
                P.op("pool", I("tensor_copy", out=stgb[:], in_=stg[:]), reads=["gstg"], writes=["gstgb"])
                for wh in range(2):
                    for kc in range(KC):
                        P.op("pe", I("matmul", pg[:], lhsT=screp[:, kc, wh, :], rhs=stgb[:, kc, :],
                                     start=(kc == 0), stop=(kc == KC - 1)),
                             reads=["screp", "gstgb"], writes=["pg"])
                    P.op("dve", I("tensor_tensor", out=gate_rep[:, wh, jb * 512:(jb + 1) * 512], in0=pg[:],
                                  in1=bg[:, jb * 512:(jb + 1) * 512], op=ALU.add),
                         reads=["pg", "bg"], writes=["gate_rep"])

        def setup_rest(l, S):
            sb = mk(S, "sb")
            ps = mk(S, "ps")
            e1 = sb("e1", [128, 128], F32)
            e2 = sb("e2", [128, 128], F32)
            pl = sb("pl", [128, 2, 64], F32)
            for h in range(4):
                lgf = lg_all[:, l * 8 + h:l * 8 + h + 1]
                lgb = lg_all[:, l * 8 + 4 + h:l * 8 + 4 + h + 1]
                P.op("act", I("activation", out=e1[:], in_=dcon[:, 0, :], func=AF.Exp, scale=lgf),
                     reads=["dcon", "lg_all"], writes=["e1"])
                P.op("pool", I("tensor_tensor", out=e1[:], in0=e1[:], in1=dcon[:, 1, :], op=ALU.mult), reads=["e1", "dcon"], writes=["e1"])
                P.op("act", I("activation", out=e2[:], in_=dcon[:, 2, :], func=AF.Exp, scale=lgb),
                     reads=["dcon", "lg_all"], writes=["e2"])
                P.op("pool", I("tensor_tensor", out=e2[:], in0=e2[:], in1=dcon[:, 3, :], op=ALU.mult), reads=["e2", "dcon"], writes=["e2"])
                P.op("pool", I("tensor_tensor", out=e1[:], in0=e1[:], in1=e2[:], op=ALU.add), reads=["e1", "e2"], writes=["e1"])
                P.op("act", I("mul", out=DT_all[:, h, :], in_=e1[:], mul=0.0625), reads=["e1"], writes=["DT_all"])
                for k, (pc, lgx, post) in enumerate([(0, lgf, 0.0625), (1, lgb, 0.0625), (2, lgf, None), (3, lgb, None)]):
                    P.op("act", I("activation", out=rv_all[:, h, k:k + 1], in_=posv[:, pc:pc + 1],
                                                                             func=AF.Exp, scale=lgx),
                         reads=["posv", "lg_all"], writes=["rv_all"])
                    if post is not None:
                        P.op("dve", I("tensor_scalar", out=rv_all[:, h, k:k + 1], in0=rv_all[:, h, k:k + 1],
                                                                                 scalar1=post, scalar2=None, op0=ALU.mult),
                             reads=["rv_all"], writes=["rv_all"])
                for k, lgx in ((4, lgf), (5, lgb)):
                    P.op("act", I("activation", out=rv_all[:, h, k:k + 1], in_=lgx, func=AF.Exp, scale=128.0),
                         reads=["lg_all"], writes=["rv_all"])
            lam_init = 0.8 - 0.6 * math.exp(-0.3 * l)
            dl_all = sb("dl_l", [128, 256], F32)
            dma(dl_all[:], dlam[:, l * 256:(l + 1) * 256], [], ["dl_all"], "c4")
            for i in range(2):
                a0 = dl_all[:, i * 128:i * 128 + 64]
                a1 = dl_all[:, i * 128 + 64:i * 128 + 128]
                P.op("dve", I("tensor_tensor", out=pl[:, i, :], in0=a0, in1=a1, op=ALU.mult),
                     reads=["dl_all"], writes=["pl"])
                P.op("dve", I("reduce_sum", out=lam_t[:, i:i + 1], in_=pl[:, i, :], axis=AX.X), reads=["pl"], writes=["lam_t"])
                P.op("act", I("activation", out=lam_t[:, i:i + 1], in_=lam_t[:, i:i + 1], func=AF.Exp), reads=["lam_t"], writes=["lam_t"])
            P.op("dve", I("tensor_tensor", out=lam_t[:, 2:3], in0=lam_t[:, 0:1], in1=lam_t[:, 1:2], op=ALU.subtract),
                 reads=["lam_t"], writes=["lam_t"])
            P.op("dve", I("tensor_scalar", out=lam_t[:, 3:4], in0=lam_t[:, 2:3], scalar1=lam_init, scalar2=-1.0, op0=ALU.add, op1=ALU.mult),
                 reads=["lam_t"], writes=["lam_t"])
            return lam_init

        def phase_conv(l, S):
            sb = mk(S, "sb")
            ps = mk(S, "ps")
            stg_rot = Rot([sb("cstg%d" % i, [128, KC, 512], F32) for i in range(2)], "cstg")
            wbf_rot = Rot([sb("cwbf%d" % i, [128, KC, 512], BF16) for i in range(2)], "cwbf")
            cw = sb("cw", [128, 8, 3], F32)
            u = sb("u_full", [128, TOK], F32)
            cf = sb("cf_full", [128, TOK], F32)
            tmpr = Rot([sb("ctmp%d" % i, [128, 512], F32) for i in range(4)], "ctmp")
            ychr = Rot([sb("ych%d" % i, [128, 512], BF16) for i in range(2)], "ych")
            psr = Rot([ps("cps%d" % i, [128, 512], F32) for i in range(4)], "cps")
            dma(cw[:], convw[l], [], ["cw"], "cv0")
            chunks = [(c * 512, min(TOK, (c + 1) * 512)) for c in range(9)]
            for g in range(8):
                wbf, wbf_r = load_w(w_in[l][:, 8192 + g * 512:8192 + (g + 1) * 512], 512, stg_rot, wbf_rot, "wc")

                def projT(pst, pst_r, col, t0, t1):
                    for kc in range(KC):
                        P.op("pe", I("matmul", pst[:, 0:t1 - t0], lhsT=wbf[:, kc, col * 128:(col + 1) * 128], rhs=hT[:, kc, t0:t1],
                                                            start=(kc == 0), stop=(kc == KC - 1)),
                             reads=[("hT", tt) for tt in range(t0 // 128, t1 // 128)] + [wbf_r], writes=[pst_r])
                for (t0, t1) in chunks:
                    n = t1 - t0
                    pa, pa_r = psr.next()
                    pb, pb_r = psr.next()
                    projT(pa, pa_r, 1, t0, t1)
                    projT(pb, pb_r, 2, t0, t1)
                    tm, tm_r = tmpr.next()
                    P.op("act", I("copy", out=tm[:, 0:n], in_=pa[:, 0:n]), reads=[pa_r], writes=[tm_r])
                    P.op("dve", I("tensor_tensor", out=u[:, t0:t1], in0=pb[:, 0:n], in1=tm[:, 0:n], op=ALU.mult),
                         reads=[pb_r, tm_r], writes=["u_full"])
                for (s0, s1) in ((0, 256), (256, TOK)):
                    P.op("act", I("mul", out=cf[:, s0:s1], in_=u[:, s0:s1], mul=cw[:, g, 1:2]),
                         reads=["u_full", "cw"], writes=["cf_full"])
                    P.op("dve", I("scalar_tensor_tensor", out=cf[:, s0 + 1:s1], in0=u[:, s0:s1 - 1], scalar=cw[:, g, 0:1],
                                                                                in1=cf[:, s0 + 1:s1], op0=ALU.mult, op1=ALU.add),
                         reads=["u_full", "cw", "cf_full"], writes=["cf_full"])
                    P.op("dve", I("scalar_tensor_tensor", out=cf[:, s0:s1 - 1], in0=u[:, s0 + 1:s1], scalar=cw[:, g, 2:3],
                                                                                in1=cf[:, s0:s1 - 1], op0=ALU.mult, op1=ALU.add),
                         reads=["u_full", "cw", "cf_full"], writes=["cf_full"])
                for (t0, t1) in chunks:
                    n = t1 - t0
                    pa, pa_r = psr.next()
                    pb, pb_r = psr.next()
                    projT(pa, pa_r, 0, t0, t1)
                    projT(pb, pb_r, 3, t0, t1)
                    sg, sg_r = tmpr.next()
                    tt_, tt_r = tmpr.next()
                    ych, ych_r = ychr.next()
                    P.op("act", I("activation", out=sg[:, 0:n], in_=pb[:, 0:n], func=AF.Silu), reads=[pb_r], writes=[sg_r])
                    P.op("dve", I("tensor_tensor", out=tt_[:, 0:n], in0=pa[:, 0:n], in1=sg[:, 0:n], op=ALU.mult),
                         reads=[pa_r, sg_r], writes=[tt_r])
                    P.op(os.environ.get("POOL_ALT", "pool"), I("tensor_tensor", out=ych[:, 0:n], in0=tt_[:, 0:n], in1=cf[:, t0:t1], op=ALU.mult),
                         reads=[tt_r, "cf_full"], writes=[ych_r])
                    dma(yT_d[1][g * 128:(g + 1) * 128, t0:t1], ych[:, 0:n], [ych_r], ["yT_d"], ("yc", ych_r[1]), eng="pool")

        rope_lim = [99]

        def rope(src1, src2, src_r, c_ap, s_ap, tab_r, dst1, dst2, dst_r, tmps):
            (t1, t1r), (t2, t2r), (t3, t3r), (t4, t4r) = tmps
            re_ = os.environ.get("ROPE_ENG", "dve")
            lst = [
                ("dve", I("tensor_tensor", out=t1, in0=src1, in1=c_ap, op=ALU.mult), [src_r, tab_r], [t1r]),
                ("dve", I("tensor_tensor", out=t2, in0=src2, in1=s_ap, op=ALU.mult), [src_r, tab_r], [t2r]),
                ("dve", I("tensor_tensor", out=t3, in0=src1, in1=s_ap, op=ALU.mult), [src_r, tab_r], [t3r]),
                ("dve", I("tensor_tensor", out=t4, in0=src2, in1=c_ap, op=ALU.mult), [src_r, tab_r], [t4r]),
                (re_, I("tensor_tensor", out=dst1, in0=t1, in1=t2, op=ALU.subtract), [t1r, t2r], [dst_r]),
                (re_, I("tensor_tensor", out=dst2, in0=t3, in1=t4, op=ALU.add), [t3r, t4r], [dst_r]),
            ]
            for en, ins, rd, wr in lst[:rope_lim[0]]:
                P.op(en, ins, reads=rd, writes=wr)

        def phase_ret(l, S):
            sb = mk(S, "sb")
            ps = mk(S, "ps")
            stg_rot = Rot([sb("rstg%d" % i, [128, KC, 512], F32) for i in range(1)], "rstg")
            wbf_rot = Rot([sb("rwbf%d" % i, [128, KC, 512], BF16) for i in range(2)], "rwbf")
            Sb_all = sb("Sb_all", [128, NT, 2, 256], BF16)
            Sst = sb("Sst", [128, 2, 256], F32)
            Sbf = sb("Sbf", [128, 2, 256], BF16)
            tabr = Rot([sb("rtab%d" % i, [128, 2, 256], F32) for i in range(int(os.environ.get("TAB_SLOTS", "2")))], "rtab")
            tmp = [Rot([sb("rt%d_%d" % (k, i), [128, 2, 128], F32) for i in range(int(os.environ.get("TMP_SLOTS", "2")))], "rt%d" % k) for k in range(4)]
            qkb_r = Rot([sb("rqkb%d" % i, [128, 2, 256], BF16) for i in range(3)], "rqkb")
            vbr = Rot([sb("rvb%d" % i, [128, 2, 256], BF16) for i in range(3)], "rvb")
            sgr = Rot([sb("rsg%d" % i, [128, 256], BF16) for i in range(3)], "rsg")
            qkTr = Rot([sb("rqkT%d" % i, [128, 4, 128], BF16) for i in range(2)], "rqkT")
            ATr = Rot([sb("rAT%d" % i, [128, 128], BF16) for i in range(2)], "rAT")
            o_r = Rot([sb("ro%d" % i, [128, 256], F32) for i in range(3)], "ro")
            ybr = Rot([sb("ryb%d" % i, [128, 256], BF16) for i in range(3)], "ryb")
            yTr = Rot([sb("ryT%d" % i, [128, 2, 128], BF16) for i in range(2)], "ryT")
            pAr = Rot([ps("rpA%d" % i, [128, 512], F32) for i in range(int(os.environ.get("PA_SLOTS", "2")))], "rpA")
            pBr = Rot([ps("rpB%d" % i, [128, 512], F32) for i in range(1)], "rpB")
            pTr = Rot([ps("rpT%d" % i, [128, 8, 128], BF16) for i in range(1)], "rpT")
            pSr = Rot([ps("rpS%d" % i, [128, 512], F32) for i in range(1)], "rpS")
            pOr = Rot([ps("rpO%d" % i, [128, 512], F32) for i in range(1)], "rpO")
            pO2r = Rot([ps("rpO2%d" % i, [128, 512], F32) for i in range(1)], "rpO2")
            pIr = Rot([ps("rpI%d" % i, [128, 2, 256], F32) for i in range(1)], "rpI")

            def load_tab(n):
                tb, tb_r = tabr.next()
                dma(tb[:, 0, :], cosR[n * 128:(n + 1) * 128, :], [], [tb_r], ("rtab", tb_r[1]))
                dma(tb[:, 1, :], sinR[n * 128:(n + 1) * 128, :], [], [tb_r], ("rtab", tb_r[1]))
                return tb, tb_r

            gwbf_rot = Rot([sb("rgw%d" % i, [128, KC, 512], BF16) for i in range(2)], "rgw")
            gsr = Rot([sb("rgs%d" % i, [128, 512], BF16) for i in range(2)], "rgs")
            gate_items = [(cb, t) for cb in range(6) for t in range(NT)]
            gate_w = {}
            gi = [0]

            def gate_load(cb):
                gate_w[cb] = load_w(w_in[l][:, 12288 + cb * 512:12288 + (cb + 1) * 512], 512, stg_rot, gwbf_rot, "wg")

            def gate_step(k):
                for _ in range(k):
                    if gi[0] >= len(gate_items):
                        return
                    cb, t = gate_items[gi[0]]
                    gi[0] += 1
                    if t == 0 and cb + 1 < 6:
                        gate_load(cb + 1)
                    gw, gw_r = gate_w[cb]
                    pg_, pg_r = pBr.next()
                    proj_tok(pg_, pg_r, t, gw, gw_r, 0, 512)
                    gs, gs_r = gsr.next()
                    P.op("act", I("activation", out=gs[:], in_=pg_[:], func=AF.Sigmoid), reads=[pg_r], writes=[gs_r])
                    dma(gates_d[t * 128:(t + 1) * 128, cb * 512:(cb + 1) * 512], gs[:], [gs_r], ["gates_d"], ("gg", gs_r[1]), eng="pool")

            gate_load(0)
            for h in range(4):
                wkv, wkv_r = load_w(w_in[l][:, h * 1024:h * 1024 + 512], 512, stg_rot, wbf_rot, "wr")
                wqg, wqg_r = load_w(w_in[l][:, h * 1024 + 512:h * 1024 + 1024], 512, stg_rot, wbf_rot, "wr")
                ub = rv_all[:, h, 3:4]
                uf = rv_all[:, h, 2:3]
                wf = rv_all[:, h, 0:1]
                wb_ = rv_all[:, h, 1:2]
                g128f = rv_all[:, h, 4:5]
                g128b = rv_all[:, h, 5:6]
                P.op("pool", I("memset", Sst[:], 0.0), writes=["Sst"])
                order = [1, 0] + list(range(NT - 1, 1, -1))

                def p1_front(n):
                    pa, pa_r = pAr.next()
                    proj_tok(pa, pa_r, n, wkv, wkv_r, 0, 512)
                    tb, tb_r = load_tab(n)
                    qkb, qkb_rr = qkb_r.next()
                    tm = [tmp[k].next() for k in range(4)]
                    rope(pa[:, 0:128], pa[:, 128:256], pa_r, tb[:, 0, 0:128], tb[:, 1, 0:128], tb_r,
                         qkb[:, 1, 0:128], qkb[:, 1, 128:256], qkb_rr, [(tm[k][0][:, 0, :], tm[k][1]) for k in range(4)])
                    vb, vb_r = vbr.next()
                    P.op("act", I("activation", out=vb[:, 1, :], in_=pa[:, 256:512], func=AF.Identity, scale=ub),
                         reads=[pa_r, "rv_all"], writes=[vb_r])
                    return (qkb, qkb_rr, vb, vb_r)

                def p1_back(n, ctx):
                    qkb, qkb_rr, vb, vb_r = ctx
                    P.op("act", I("copy", out=Sb_all[:, n, :, :], in_=Sst[:]), reads=["Sst"], writes=[("Sb_all", n)])
                    pI, pI_r = pIr.next()
                    for dc in range(2):
                        P.op("pe", I("matmul", pI[:, dc, :], lhsT=qkb[:, 1, dc * 128:(dc + 1) * 128], rhs=vb[:, 1, :], start=True, stop=True),
                             reads=[qkb_rr, vb_r], writes=[pI_r])
                    P.op("dve", I("scalar_tensor_tensor", out=Sst[:], in0=Sst[:], scalar=g128b, in1=pI[:], op0=ALU.mult, op1=ALU.add),
                         reads=["Sst", pI_r, "rv_all"], writes=["Sst"])

                ctxs = {0: p1_front(order[0])}
                for i, n in enumerate(order):
                    if i + 1 < len(order):
                        ctxs[i + 1] = p1_front(order[i + 1])
                    p1_back(n, ctxs.pop(i))
                    gate_step(2 if i % 2 == 0 else 1)

                P.op("pool", I("memset", Sst[:], 0.0), writes=["Sst"])
                P.op("pool", I("memset", Sbf[:], 0.0), writes=["Sbf"])

                def p2_front(n):
                    pa, pa_r = pAr.next()
                    pb, pb_r = pBr.next()
                    proj_tok(pa, pa_r, n, wkv, wkv_r, 0, 512)
                    proj_tok(pb, pb_r, n, wqg, wqg_r, 0, 512)
                    tb, tb_r = load_tab(n)
                    qkb, qkb_rr = qkb_r.next()
                    tm = [tmp[k].next() for k in range(4)]
                    rope(pb[:, 0:128], pb[:, 128:256], pb_r, tb[:, 0, 0:128], tb[:, 1, 0:128], tb_r,
                         qkb[:, 0, 0:128], qkb[:, 0, 128:256], qkb_rr, [(tm[k][0][:, 0, :], tm[k][1]) for k in range(4)])
                    rope(pa[:, 0:128], pa[:, 128:256], pa_r, tb[:, 0, 0:128], tb[:, 1, 0:128], tb_r,
                         qkb[:, 1, 0:128], qkb[:, 1, 128:256], qkb_rr, [(tm[k][0][:, 1, :], tm[k][1]) for k in range(4)])
                    vb, vb_r = vbr.next()
                    P.op("act", I("copy", out=vb[:, 0, :], in_=pa[:, 256:512]), reads=[pa_r], writes=[vb_r])
                    P.op("act", I("activation", out=vb[:, 1, :], in_=pa[:, 256:512], func=AF.Identity, scale=uf),
                         reads=[pa_r, "rv_all"], writes=[vb_r])
                    sg, sg_r = sgr.next()
                    P.op("act", I("activation", out=sg[:], in_=pb[:, 256:512], func=AF.Silu), reads=[pb_r], writes=[sg_r])
                    pT, pT_r = pTr.next()
                    for i in range(4):
                        P.op("pe", I("transpose", out=pT[:, i, :], in_=qkb[:, i // 2, (i % 2) * 128:(i % 2 + 1) * 128], identity=identb[:]),
                             reads=[qkb_rr, "identb"], writes=[pT_r])
                    qkT, qkT_r = qkTr.next()
                    P.op("dve", I("tensor_copy", out=qkT[:], in_=pT[:, 0:4, :]), reads=[pT_r], writes=[qkT_r])
                    return (qkb, qkb_rr, vb, vb_r, sg, sg_r, qkT, qkT_r)

                def p2_main(n, ctx):
                    qkb, qkb_rr, vb, vb_r, sg, sg_r, qkT, qkT_r = ctx
                    pS, pS_r = pSr.next()
                    for dc in range(2):
                        P.op("pe", I("matmul", pS[:, 0:128], lhsT=qkT[:, 2 + dc, :], rhs=qkT[:, dc, :], start=(dc == 0), stop=(dc == 1)),
                             reads=[qkT_r], writes=[pS_r])
                    AT, AT_r = ATr.next()
                    P.op("dve", I("tensor_tensor", out=AT[:], in0=pS[:, 0:128], in1=DT_all[:, h, :], op=ALU.mult),
                         reads=[pS_r, "DT_all"], writes=[AT_r])
                    pO, pO_r = pOr.next()
                    pO2, pO2_r = pO2r.next()
                    for dc in range(2):
                        P.op("pe", I("matmul", pO[:, 256:512], lhsT=qkT[:, dc, :], rhs=Sbf[:, dc, :], start=(dc == 0), stop=(dc == 1)),
                             reads=[qkT_r, "Sbf"], writes=[pO_r])
                    for dc in range(2):
                        P.op("pe", I("matmul", pO2[:, 0:256], lhsT=qkT[:, dc, :], rhs=Sb_all[:, n, dc, :], start=(dc == 0), stop=(dc == 1)),
                             reads=[qkT_r, ("Sb_all", n)], writes=[pO2_r])
                    pI, pI_r = pIr.next()
                    for dc in range(2):
                        P.op("pe", I("matmul", pI[:, dc, :], lhsT=qkb[:, 1, dc * 128:(dc + 1) * 128], rhs=vb[:, 1, :], start=True, stop=True),
                             reads=[qkb_rr, vb_r], writes=[pI_r])
                    P.op("pe", I("matmul", pO[:, 0:256], lhsT=AT[:], rhs=vb[:, 0, :], start=True, stop=True),
                         reads=[AT_r, vb_r], writes=[pO_r])
                    P.op("dve", I("scalar_tensor_tensor", out=Sst[:], in0=Sst[:], scalar=g128f, in1=pI[:], op0=ALU.mult, op1=ALU.add),
                         reads=["Sst", pI_r, "rv_all"], writes=["Sst"])
                    P.op("act", I("copy", out=Sbf[:], in_=Sst[:]), reads=["Sst"], writes=["Sbf"])
                    o1, o1_r = o_r.next()
                    P.op("act", I("copy", out=o1[:], in_=pO[:, 0:256]), reads=[pO_r], writes=[o1_r])
                    P.op("dve", I("scalar_tensor_tensor", out=o1[:], in0=pO[:, 256:512], scalar=wf, in1=o1[:], op0=ALU.mult, op1=ALU.add),
                         reads=[pO_r, o1_r, "rv_all"], writes=[o1_r])
                    P.op("dve", I("scalar_tensor_tensor", out=o1[:], in0=pO2[:, 0:256], scalar=wb_, in1=o1[:], op0=ALU.mult, op1=ALU.add),
                         reads=[pO2_r, o1_r, "rv_all"], writes=[o1_r])
                    sm, sm_r = ln_stats(lambda a, b, o1=o1: o1[:, a:b], 256, [o1_r], 256)
                    P.op("act", I("activation", out=o1[:], in_=o1[:], func=AF.Identity, bias=sm[:, 1:2], scale=sm[:, 0:1]),
                         reads=[o1_r, sm_r], writes=[o1_r])
                    yb, yb_r = ybr.next()
                    P.op(os.environ.get("POOL_ALT", "pool"), I("tensor_tensor", out=yb[:], in0=o1[:], in1=sg[:], op=ALU.mult), reads=[o1_r, sg_r], writes=[yb_r])
                    return (yb, yb_r)

                def p2_tail(n, ctx):
                    yb, yb_r = ctx
                    pT2, pT2_r = pTr.next()
                    for dc in range(2):
                        P.op("pe", I("transpose", out=pT2[:, 4 + dc, :], in_=yb[:, dc * 128:(dc + 1) * 128], identity=identb[:]),
                             reads=[yb_r, "identb"], writes=[pT2_r])
                    yT, yT_r = yTr.next()
                    P.op("act", I("copy", out=yT[:], in_=pT2[:, 4:6, :]), reads=[pT2_r], writes=[yT_r])
                    dma(yT_d[0][h * 256:(h + 1) * 256, n * 128:(n + 1) * 128].rearrange("(c p) t -> p c t", p=128), yT[:], [yT_r], ["yT_d"],
                        ("yr", yT_r[1]), eng="pool")

                fctx = {0: p2_front(0)}
                tctx = {}
                for n in range(NT):
                    if n + 1 < NT:
                        fctx[n + 1] = p2_front(n + 1)
                    tctx[n] = p2_main(n, fctx.pop(n))
                    if n - 1 >= 0:
                        p2_tail(n - 1, tctx.pop(n - 1))
                p2_tail(NT - 1, tctx.pop(NT - 1))
            gate_step(len(gate_items))

        def phase_att(l, S, lam_init):
            sb = mk(S, "sb")
            ps = mk(S, "ps")
            stg_rot = Rot([sb("astg%d" % i, [128, KC, 512], F32) for i in range(1)], "astg")
            wbf_rot = Rot([sb("awbf%d" % i, [128, KC, 512], BF16) for i in range(1)], "awbf")
            qT = sb("aqT", [128, TOK], BF16)
            kT = sb("akT", [128, TOK], BF16)
            Vx = sb("aVx", [128, NT, 130], BF16)
            sgA = sb("asg", [128, NT, 128], BF16)
            PTr = Rot([sb("aPT%d" % i, [128, NT, 256], BF16) for i in range(2)], "aPT")
            tabr = Rot([sb("atab%d" % i, [128, 2, 128], F32) for i in range(2)], "atab")
            tmp = [Rot([sb("at%d_%d" % (k, i), [128, 4, 32], F32) for i in range(2)], "at%d" % k) for k in range(4)]
            qkbr = Rot([sb("aqkb%d" % i, [128, 256], BF16) for i in range(3)], "aqkb")
            O0r = Rot([sb("aO0%d" % i, [128, 128], F32) for i in range(4)], "aO0")
            ar = Rot([sb("aa%d" % i, [128, 128], F32) for i in range(4)], "aa")
            jr = Rot([sb("aj%d" % i, [128, 128], F32) for i in range(2)], "aj")
            ybr = Rot([sb("ayb%d" % i, [128, 128], BF16) for i in range(6)], "ayb")
            yTr = Rot([sb("ayT%d" % i, [128, 128], BF16) for i in range(2)], "ayT")
            apX = ps("apX", [128, 1024], F32)
            apY = ps("apY", [128, 1024], F32)
            pAr = Rot([apX[:, 0:512], apX[:, 512:1024]], "apA")
            pTr = Rot([ps("apT%d" % i, [128, 8, 128], BF16) for i in range(2)], "apT")
            pS_slots = [(apX, [("apA", 0), ("apA", 1)]), (apY, [("apS", 0)])]
            pS_i = [0]
            pOr = Rot([ps("apO%d" % i, [128, 512], F32) for i in range(2)], "apO")
            P.op("pool", I("memset", Vx[:], 1.0), writes=["aVx"])
            for h in range(8):
                wbf, wbf_r = load_w(w_in[l][:, 4096 + h * 512:4096 + (h + 1) * 512], 512, stg_rot, wbf_rot, "wa")
                def a1_front(t):
                    pa, pa_r = pAr.next()
                    proj_tok(pa, pa_r, t, wbf, wbf_r, 0, 512)
                    tb, tb_r = tabr.next()
                    dma(tb[:, 0, :], cosA[t * 128:(t + 1) * 128, :], [], [tb_r], ("atab", tb_r[1]))
                    dma(tb[:, 1, :], sinA[t * 128:(t + 1) * 128, :], [], [tb_r], ("atab", tb_r[1]))
                    qkb, qkb_r = qkbr.next()
                    tm = [tmp[k].next() for k in range(4)]
                    src = pa[:, 0:256].rearrange("p (a b c) -> p a b c", a=4, b=2, c=32)
                    dst = qkb[:].rearrange("p (a b c) -> p a b c", a=4, b=2, c=32)
                    cA = tb[:, 0, :].rearrange("p (a c) -> p a c", a=4, c=32)
                    sA = tb[:, 1, :].rearrange("p (a c) -> p a c", a=4, c=32)
                    rope(src[:, :, 0, :], src[:, :, 1, :], pa_r, cA, sA, tb_r, dst[:, :, 0, :], dst[:, :, 1, :], qkb_r,
                         [(tm[k][0][:], tm[k][1]) for k in range(4)])
                    P.op("act", I("copy", out=Vx[:, t, 0:128], in_=pa[:, 256:384]), reads=[pa_r], writes=["aVx"])
                    P.op("act", I("activation", out=sgA[:, t, :], in_=pa[:, 384:512], func=AF.Silu), reads=[pa_r], writes=["asg"])
                    return (qkb, qkb_r)

                def a1_back(t, ctx):
                    qkb, qkb_r = ctx
                    pT, pT_r = pTr.next()
                    for i in range(2):
                        P.op("pe", I("transpose", out=pT[:, i, :], in_=qkb[:, i * 128:(i + 1) * 128], identity=identb[:]),
                             reads=[qkb_r, "identb"], writes=[pT_r])
                    P.op("dve", I("tensor_copy", out=qT[:, t * 128:(t + 1) * 128], in_=pT[:, 0, :]), reads=[pT_r], writes=["aqT"])
                    P.op("dve", I("tensor_copy", out=kT[:, t * 128:(t + 1) * 128], in_=pT[:, 1, :]), reads=[pT_r], writes=["akT"])

                actx = {0: a1_front(0)}
                for t in range(NT):
                    if t + 1 < NT:
                        actx[t + 1] = a1_front(t + 1)
                    a1_back(t, actx.pop(t))
                steps = [(b, j) for b in range(NT // 2) for j in range(2)]
                if os.environ.get("ATT_SKIP2"):
                    steps = []
                qstate = {}

                def A_items(b, j):
                    nk = NCTX_T if b == 0 else NT
                    q0 = b * 256
                    PT, PT_r = PTr.next()
                    items = []

                    def mk_item(kg, ng):
                        def f():
                            pSt, pS_rs = pS_slots[pS_i[0] % 2]
                            pS_i[0] += 1
                            pSv = pSt[:, :].rearrange("p (a b) -> p a b", a=4, b=256)
                            for i in range(ng):
                                kt = kg + i
                                P.op("pe", I("matmul", pSv[:, i, :], lhsT=kT[64 * j:64 * j + 64, kt * 128:(kt + 1) * 128],
                                             rhs=qT[64 * j:64 * j + 64, q0:q0 + 256], start=True, stop=True),
                                     reads=["akT", "aqT"], writes=pS_rs)
                            if ng == 4 and not os.environ.get("EXP512"):
                                P.op("act", I("activation", out=PT[:, kg:kg + 4, :], in_=pSv[:, 0:4, :], func=AF.Exp, scale=0.125),
                                     reads=pS_rs, writes=[PT_r])
                            else:
                                for i0 in range(0, ng, 2):
                                    P.op("act", I("activation", out=PT[:, kg + i0:kg + i0 + 2, :], in_=pSv[:, i0:i0 + 2, :], func=AF.Exp, scale=0.125),
                                         reads=pS_rs, writes=[PT_r])
                        return f
                    for kg in range(0, nk, 4):
                        items.append(mk_item(kg, min(4, nk - kg)))
                    return items, (PT, PT_r, nk)

                def B_items(b, j, PTinfo):
                    PT, PT_r, nk = PTinfo
                    items = []
                    for qh in range(2):
                        qt = 2 * b + qh
                        if j == 0:
                            qstate[qt] = (O0r.next(), ar.next())
                        (O0, O0_r), (a_, a_r) = qstate[qt]
                        pO, pO_r = pOr.next()

                        def mk_pv(k0, k1, lastchunk, qt=qt, qh=qh, O0=O0, O0_r=O0_r, a_=a_, a_r=a_r, pO=pO, pO_r=pO_r):
                            def f():
                                for kt in range(k0, k1):
                                    P.op("pe", I("matmul", pO[:, 0:129], lhsT=PT[:, kt, qh * 128:(qh + 1) * 128], rhs=Vx[:, kt, 0:129],
                                                 start=(kt == 0), stop=(kt == nk - 1)),
                                         reads=[PT_r, "aVx"], writes=[pO_r])
                                if lastchunk:
                                    sm, sm_r = smallr.next()
                                    P.op("dve", I("reciprocal", out=sm[:, 0:1], in_=pO[:, 128:129]), reads=[pO_r], writes=[sm_r])
                                    if j == 0:
                                        P.op("dve", I("tensor_scalar", out=O0[:], in0=pO[:, 0:128], scalar1=sm[:, 0:1], scalar2=None, op0=ALU.mult),
                                             reads=[pO_r, sm_r], writes=[O0_r])
                                    else:
                                        P.op("dve", I("tensor_tensor", out=sm[:, 1:2], in0=sm[:, 0:1], in1=lam_t[:, 3:4], op=ALU.mult),
                                             reads=[sm_r, "lam_t"], writes=[sm_r])
                                        P.op("dve", I("scalar_tensor_tensor", out=a_[:], in0=pO[:, 0:128], scalar=sm[:, 1:2], in1=O0[:],
                                                      op0=ALU.mult, op1=ALU.add),
                                             reads=[pO_r, sm_r, O0_r], writes=[a_r])
                                        jk, jk_r = jr.next()
                                        sm2, sm2_r = smallr.next()
                                        P.op("dve", I("scalar_tensor_tensor", out=jk[:], in0=a_[:], scalar=1.0, in1=a_[:], op0=ALU.mult, op1=ALU.mult,
                                                      accum_out=sm2[:, 0:1]),
                                             reads=[a_r], writes=[jk_r, sm2_r])
                                        P.op("dve", I("tensor_scalar", out=sm2[:, 1:2], in0=sm2[:, 0:1], scalar1=1.0 / 128.0, scalar2=LN_EPS,
                                                      op0=ALU.mult, op1=ALU.add), reads=[sm2_r], writes=[sm2_r])
                                        P.op("pool", I("tensor_tensor", out=sm2[:, 3:4], in0=sm2[:, 1:2], in1=mhalf[:, 0:1], op=ALU.pow),
                                             reads=[sm2_r, "mhalf"], writes=[sm2_r])
                                        yb, yb_r = ybr.next()
                                        P.op("dve", I("scalar_tensor_tensor", out=yb[:], in0=a_[:], scalar=sm2[:, 3:4], in1=sgA[:, qt, :],
                                                      op0=ALU.mult, op1=ALU.mult),
                                             reads=[a_r, sm2_r, "asg"], writes=[yb_r])
                                        qstate[qt] = (yb, yb_r)
                            return f
                        cs = int(os.environ.get("ATT_CS", "6"))
                        for k0 in range(0, nk, cs):
                            k1 = min(nk, k0 + cs)
                            items.append(mk_pv(k0, k1, k1 == nk))
                    return items

                def tail_pe(qt):
                    yb, yb_r = qstate.pop(qt)
                    pT, pT_r = pTr.next()
                    P.op("pe", I("transpose", out=pT[:, 0, :], in_=yb[:], identity=identb[:]), reads=[yb_r, "identb"], writes=[pT_r])
                    yT, yT_r = yTr.next()
                    P.op("dve", I("tensor_scalar", out=yT[:], in0=pT[:, 0, :], scalar1=float(1.0 - lam_init), scalar2=None, op0=ALU.mult),
                         reads=[pT_r], writes=[yT_r])
                    dma(yT_d[2][h * 128:(h + 1) * 128, qt * 128:(qt + 1) * 128], yT[:], [yT_r], ["yT_d"], ("ya", yT_r[1]), eng="pool")

                prevB = []
                tails_now = []
                for si in range(len(steps) + 1):
                    if si < len(steps):
                        b, j = steps[si]
                        Ai, PTinfo = A_items(b, j)
                    else:
                        Ai, PTinfo = [], None
                    for k in range(max(len(Ai), len(prevB))):
                        if k < len(Ai):
                            Ai[k]()
                        if k < len(prevB):
                            prevB[k]()
                    for tq in tails_now:
                        tail_pe(tq)
                    tails_now = []
                    if si >= 1 and steps[si - 1][1] == 1:
                        tails_now += [2 * steps[si - 1][0], 2 * steps[si - 1][0] + 1]
                    if not steps:
                        break
                    prevB = B_items(b, j, PTinfo) if si < len(steps) else []
                for tq in tails_now:
                    tail_pe(tq)

        def phase_gates(l, S):
            sb = mk(S, "sb")
            ps = mk(S, "ps")
            stg_rot = Rot([sb("gstg%d" % i, [128, KC, 512], F32) for i in range(2)], "gstg")
            wbf_rot = Rot([sb("gwbf%d" % i, [128, KC, 512], BF16) for i in range(2)], "gwbf")
            gsr = Rot([sb("ggs%d" % i, [128, 512], BF16) for i in range(3)], "ggs")
            pAr = Rot([ps("gpA%d" % i, [128, 512], F32) for i in range(3)], "gpA")
            for cb in range(6):
                wbf, wbf_r = load_w(w_in[l][:, 12288 + cb * 512:12288 + (cb + 1) * 512], 512, stg_rot, wbf_rot, "wg")
                for t in range(NT):
                    pa, pa_r = pAr.next()
                    proj_tok(pa, pa_r, t, wbf, wbf_r, 0, 512)
                    gs, gs_r = gsr.next()
                    P.op("act", I("activation", out=gs[:], in_=pa[:], func=AF.Sigmoid), reads=[pa_r], writes=[gs_r])
                    dma(gates_d[t * 128:(t + 1) * 128, cb * 512:(cb + 1) * 512], gs[:], [gs_r], ["gates_d"], ("gg", gs_r[1]), eng="pool")

        def phase_B(l, S, last):
            sb = mk(S, "sb")
            ps = mk(S, "ps")
            wts = []
            wtl = [sb("bw%d" % wi, [128, KC, D], BF16) for wi in range(4)]
            with contextlib.ExitStack() as S2:
                sb2 = mk(S2, "sb")
                stg_rot = Rot([sb2("bstg%d" % i, [128, KC, 512], F32) for i in range(2)], "bstg")
                for wi, wsrc in enumerate((w_ro, w_co, w_ao, w_oo)):
                    wt = wtl[wi]
                    for half in range(2):
                        stg, stg_r = stg_rot.next()
                        dma(stg[:], wsrc[l][:, half * 512:(half + 1) * 512].rearrange("(c p) n -> p c n", p=128), [], [stg_r], ("bw", stg_r[1]))
                        for q in range(4):
                            en = ("pool", "dve", "act", "pool")[q]
                            if en == "act":
                                P.op(en, I("copy", out=wt[:, 2 * q:2 * q + 2, half * 512:(half + 1) * 512],
                                                                                       in_=stg[:, 2 * q:2 * q + 2, :]),
                                     reads=[stg_r], writes=["bw%d" % wi])
                            else:
                                P.op(en, I("tensor_copy", out=wt[:, 2 * q:2 * q + 2, half * 512:(half + 1) * 512],
                                                                                              in_=stg[:, 2 * q:2 * q + 2, :]),
                                     reads=[stg_r], writes=["bw%d" % wi])
                    wts.append((wt, "bw%d" % wi))
                P.barrier()
            gate_rep = sb("gate_rep", [128, 2, D], F32)
            gate_holder[0] = gate_rep
            with contextlib.ExitStack() as S3:
                setup_gate(l, S3)
                P.barrier()
            xnb_holder[0] = Rot([sb("xnb%d" % i, [128, D], BF16) for i in range(2)], "xnb")
            lngt = sb("lngt", [128, D], F32)
            lnbt = sb("lnbt", [128, D], F32)
            dma(lngt[:], lng[l], [], ["lngt"], "bl0")
            dma(lnbt[:], lnb[l], [], ["lnbt"], "bl1")
            gtr = Rot([sb("bgt%d" % i, [128, 3 * D], BF16) for i in range(1)], "bgt")
            xtr = Rot([sb("bxt%d" % i, [128, D], F32) for i in range(1)], "bxt")
            ytr = Rot([sb("byt%d" % i, [128, 3, KC, 128], BF16) for i in range(1)], "byt")
            mt = sb("bmt_", [128, 512], F32)
            t2 = sb("bt2", [128, 512], F32)
            mb = sb("bmb", [128, D], BF16)
            mT = sb("bmT", [128, KC, 128], BF16)
            xnr = Rot([sb("bxn%d" % i, [128, D], F32) for i in range(2)], "bxn")
            pRr = Rot([ps("bpR%d" % i, [128, 512], F32) for i in range(2)], "bpR")
            pTr = Rot([ps("bpT%d" % i, [128, 8, 128], BF16) for i in range(4)], "bpT")
            pOr = Rot([ps("bpO%d" % i, [128, 512], F32) for i in range(2)], "bpO")
            xsrc_d = xin if l == 0 else xbuf_d
            ztr = Rot([sb("bzt%d" % i, [128, D], F32) for i in range(2)], "bzt")
            mbr = Rot([mb, sb("bmb2", [128, D], BF16)], "bmb")

            def b_front(t):
                yt, yt_r = ytr.next()
                for br in range(3):
                    dma(yt[:, br, :, :], yT_d[br][:, t * 128:(t + 1) * 128].rearrange("(c p) t -> p c t", p=128), ["yT_d"], [yt_r], ("by", br))
                gt, gt_r = gtr.next()
                dma(gt[:], gates_d[t * 128:(t + 1) * 128, :], ["gates_d"], [gt_r], "bg")
                mbt, mbt_r = mbr.next()

                def half(nb):
                    def f():
                        for br in range(3):
                            wt, wt_r = wts[br]
                            pR, pR_r = pRr.next()
                            for kc in range(KC):
                                P.op("pe", I("matmul", pR[:], lhsT=yt[:, br, kc, :], rhs=wt[:, kc, nb * 512:(nb + 1) * 512],
                                             start=(kc == 0), stop=(kc == KC - 1)),
                                     reads=[yt_r, wt_r], writes=[pR_r])
                            gsl = gt[:, br * D + nb * 512:br * D + (nb + 1) * 512]
                            dst = mt if br == 0 else t2
                            dst_r = "bmt_" if br == 0 else "bt2"
                            P.op("dve", I("tensor_tensor", out=dst[:], in0=pR[:], in1=gsl, op=ALU.mult),
                                 reads=[pR_r, gt_r], writes=[dst_r])
                            if br == 1:
                                P.op("dve", I("tensor_tensor", out=mt[:], in0=mt[:], in1=t2[:], op=ALU.add), reads=["bmt_", "bt2"], writes=["bmt_"])
                            if br == 2:
                                P.op("dve", I("tensor_tensor", out=mbt[:, nb * 512:(nb + 1) * 512], in0=mt[:], in1=t2[:], op=ALU.add),
                                     reads=["bmt_", "bt2"], writes=[mbt_r])
                    return f
                return (mbt, mbt_r), [half(0), half(1)]

            def b_back(t, ctx):
                mbt, mbt_r = ctx
                wh = 1 if t < NCTX_T else 0
                xn, xn_r = xnr.next()
                zt, zt_r = ztr.next()
                xt, xt_r = xtr.next()
                st = {}

                def part1():
                    dma(xt[:], xsrc_d[t * 128:(t + 1) * 128, :], [("xbuf", t)], [xt_r], ("bx", xt_r[1]))
                    pTa, pTa_r = pTr.next()
                    pTb, pTb_r = pTr.next()
                    for kc in range(KC):
                        pT, pT_r = (pTa, pTa_r) if kc < 4 else (pTb, pTb_r)
                        P.op("pe", I("transpose", out=pT[:, kc, :], in_=mbt[:, kc * 128:(kc + 1) * 128], identity=identb[:]),
                             reads=[mbt_r, "identb"], writes=[pT_r])
                    P.op("act", I("copy", out=mT[:, 0:4, :], in_=pTa[:, 0:4, :]), reads=[pTa_r], writes=["bmT"])
                    P.op("dve", I("tensor_copy", out=mT[:, 4:8, :], in_=pTb[:, 4:8, :]), reads=[pTb_r], writes=["bmT"])
                    wo, wo_r = wts[3]
                    for nb in range(2):
                        pO, pO_r = pOr.next()
                        for kc in range(KC):
                            P.op("pe", I("matmul", pO[:], lhsT=mT[:, kc, :], rhs=wo[:, kc, nb * 512:(nb + 1) * 512],
                                         start=(kc == 0), stop=(kc == KC - 1)),
                                 reads=["bmT", wo_r], writes=[pO_r])
                        P.op("dve", I("tensor_tensor", out=zt[:, nb * 512:(nb + 1) * 512], in0=pO[:],
                                      in1=gate_rep[:, wh, nb * 512:(nb + 1) * 512], op=ALU.mult),
                             reads=[pO_r, "gate_rep"], writes=[zt_r])
                    P.op("dve", I("scalar_tensor_tensor", out=zt[:], in0=xt[:], scalar=float(ALPHA), in1=zt[:], op0=ALU.mult, op1=ALU.add),
                         reads=[xt_r, zt_r], writes=[zt_r])
                    st["sm"] = ln_stats(lambda a, b: zt[:, a:b], D, [zt_r], D)

                def part2():
                    sm, sm_r = st["sm"]
                    P.op("act", I("activation", out=xn[:], in_=zt[:], func=AF.Identity, bias=sm[:, 1:2], scale=sm[:, 0:1]),
                         reads=[zt_r, sm_r], writes=[xn_r])
                    P.op(os.environ.get("POOL_ALT", "pool"), I("tensor_tensor", out=xn[:], in0=xn[:], in1=lngt[:], op=ALU.mult), reads=[xn_r, "lngt"], writes=[xn_r])
                    P.op("dve", I("tensor_tensor", out=xn[:], in0=xn[:], in1=lnbt[:], op=ALU.add), reads=[xn_r, "lnbt"], writes=[xn_r])
                    if not last:
                        dma(xbuf_d[t * 128:(t + 1) * 128, :], xn[:], [xn_r], [("xbuf", t)], "bxo", eng="pool")
                    elif t >= NCTX_T:
                        dma(out_d[(t - NCTX_T) * 128:(t - NCTX_T + 1) * 128, :], xn[:], [xn_r], [], "bxo", eng="pool")

                def part3():
                    if not last:
                        make_hT(t, xn, xn_r, *pTr.next(), *pTr.next())
                return [part1, part2, part3]

            c0, h0 = b_front(0)
            h0[0]()
            h0[1]()
            bctx = {0: c0}
            for t in range(NT):
                halves = [lambda: None, lambda: None]
                if t + 1 < NT:
                    bctx[t + 1], halves = b_front(t + 1)
                parts = b_back(t, bctx.pop(t))
                parts[0]()
                halves[0]()
                parts[1]()
                halves[1]()
                parts[2]()

        with contextlib.ExitStack() as S:
            if upto != "const":
                setup_mod(0, S)
            sb = mk(S, "sb")
            ps = mk(S, "ps")
            xnb_holder[0] = Rot([sb("xnb%d" % i, [128, D], BF16) for i in range(2)], "xnb")
            xtr0 = Rot([sb("pxt%d" % i, [128, D], F32) for i in range(2)], "pxt")
            pTr0 = Rot([ps("ppT%d" % i, [128, 8, 128], BF16) for i in range(4)], "ppT")
            for t in range((NT if not (upto or "").startswith("pro") or upto == "pro" else int(upto[3:])) if upto not in ("const", "mod") else 0):
                xt, xt_r = xtr0.next()
                dma(xt[:], xin[t * 128:(t + 1) * 128, :], [], [xt_r], ("px", xt_r[1]))
                make_hT(t, xt, xt_r, *pTr0.next(), *pTr0.next())
            P.barrier()
            if dbg:
                for kc in range(KC):
                    dma(hT_dbg[:, kc, :], hT[:, kc, :], [("hT", t) for t in range(NT)], [], "dbgh")
                P.barrier()
        for l in range(n_layers):
            last = (l == DEPTH - 1)
            if upto in ("const", "mod") or (upto or "").startswith("pro"):
                break
            with contextlib.ExitStack() as S:
                lam_init = setup_rest(l, S)
                P.barrier()
            if upto == "rest":
                break
            if upto in (None, "conv", "B"):
                with contextlib.ExitStack() as S:
                    phase_conv(l, S)
                    P.barrier()
            if upto == "conv":
                break
            if upto in (None, "ret", "B"):
                with contextlib.ExitStack() as S:
                    phase_ret(l, S)
                    P.barrier()
            if upto == "ret":
                break
            if upto in (None, "att", "B"):
                with contextlib.ExitStack() as S:
                    phase_att(l, S, lam_init)
                    P.barrier()
            if upto == "att":
                break
            if upto == "gates":
                with contextlib.ExitStack() as S:
                    phase_gates(l, S)
                    P.barrier()
            if upto == "gates":
                break
            if not last:
                with contextlib.ExitStack() as S:
                    setup_mod(l + 1, S)
                    P.barrier()
            with contextlib.ExitStack() as S:
                phase_B(l, S, last)
                P.barrier()
        P.emit()
    return nc, P


def _rope_tables(head_dim, reps):
    n_freq = head_dim // 4
    inv = (10000.0 ** (-np.arange(n_freq, dtype=np.float32) / np.float32(n_freq))).astype(np.float32)
    rows = np.repeat(np.arange(64, dtype=np.float32), 64)
    cols = np.tile(np.arange(64, dtype=np.float32), 64)
    ang = np.concatenate([rows[:, None] * inv, cols[:, None] * inv], axis=-1).astype(np.float32)
    c = np.cos(ang).astype(np.float32)
    s = np.sin(ang).astype(np.float32)
    half = head_dim // 2
    c = np.concatenate([np.ones((256, half), np.float32), c], 0)
    s = np.concatenate([np.zeros((256, half), np.float32), s], 0)
    return np.ascontiguousarray(np.tile(c, (1, reps))), np.ascontiguousarray(np.tile(s, (1, reps)))


def _perm_cols():
    seg = {}
    names = ["rk", "rv", "ak", "av", "rq", "rg", "aq", "ag", "cb", "cc", "cx", "cg"]
    for i, n in enumerate(names):
        seg[n] = i * 1024
    idx = []
    for h in range(4):
        for n in ("rk", "rv", "rq", "rg"):
            idx.extend(range(seg[n] + h * 256, seg[n] + (h + 1) * 256))
    for h in range(8):
        for n in ("aq", "ak", "av", "ag"):
            idx.extend(range(seg[n] + h * 128, seg[n] + (h + 1) * 128))
    for g in range(8):
        for n in ("cb", "cc", "cx", "cg"):
            idx.extend(range(seg[n] + g * 128, seg[n] + (g + 1) * 128))
    idx.extend(range(12288, 15360))
    return np.asarray(idx, dtype=np.int64)


def _host_inputs(x, c, ctx, c_ctx, w_mod, b_mod, w_in, ret_decay, conv_w, diff_lambda,
                 w_ret_out, w_conv_out, w_att_out, w_out, ln_g, ln_b):
    f = np.float32
    cosR, sinR = _rope_tables(256, 2)
    cosA, sinA = _rope_tables(64, 4)
    p = np.arange(128, dtype=f)
    dist = p[None, :] - p[:, None]
    dconst = np.stack([np.maximum(dist, 0), (dist >= 0).astype(f), np.maximum(-dist, 0), (dist < 0).astype(f)], 1).astype(f)
    posv = np.stack([p + 1, 128 - p, 127 - p, p], 1).astype(f)
    w_in_p = np.ascontiguousarray(np.asarray(w_in, f)[:, :, _perm_cols()])
    bm = np.asarray(b_mod, f)
    bmodT = np.ascontiguousarray(bm[:, :2048].reshape(DEPTH, 16, 128).transpose(0, 2, 1))
    bgate = np.ascontiguousarray(np.broadcast_to(bm[:, None, 2048:], (DEPTH, 128, D)))
    shared = dict(
        w_mod=np.ascontiguousarray(w_mod, f), bmodT=bmodT, bgate=bgate, w_in=w_in_p,
        rdec=np.ascontiguousarray(np.broadcast_to(np.asarray(ret_decay, f).reshape(1, 32), (128, 32))),
        convw=np.ascontiguousarray(np.asarray(conv_w, f).reshape(DEPTH, 3, 8, 128).transpose(0, 3, 2, 1)),
        dlam=np.ascontiguousarray(np.broadcast_to(np.asarray(diff_lambda, f).reshape(1, DEPTH * 256), (128, DEPTH * 256))),
        w_ro=np.ascontiguousarray(w_ret_out, f), w_co=np.ascontiguousarray(w_conv_out, f),
        w_ao=np.ascontiguousarray(w_att_out, f), w_oo=np.ascontiguousarray(w_out, f),
        lng=np.ascontiguousarray(np.broadcast_to(np.asarray(ln_g, f)[:, None, :], (DEPTH, 128, D))),
        lnb=np.ascontiguousarray(np.broadcast_to(np.asarray(ln_b, f)[:, None, :], (DEPTH, 128, D))),
        ident=np.eye(128, dtype=f), cosR=cosR, sinR=sinR, cosA=cosA, sinA=sinA, dconst=dconst, posv=posv,
    )
    in_maps = []
    for core in range(8):
        b = core % 4
        m = dict(shared)
        m["xin"] = np.ascontiguousarray(np.concatenate([np.asarray(ctx[b], f), np.asarray(x[b], f)], 0))
        cv = np.stack([np.asarray(c[b], f), np.asarray(c_ctx, f)], -1)
        m["cvec"] = np.ascontiguousarray(cv.reshape(KC, 128, 2).transpose(1, 0, 2))
        in_maps.append(m)
    return in_maps


_CACHE = {}


def kernel(x, c, ctx, c_ctx, w_mod, b_mod, w_in, ret_decay, conv_w, diff_lambda,
           w_ret_out, w_conv_out, w_att_out, w_out, ln_g, ln_b):
    in_maps = _host_inputs(x, c, ctx, c_ctx, w_mod, b_mod, w_in, ret_decay, conv_w, diff_lambda,
                           w_ret_out, w_conv_out, w_att_out, w_out, ln_g, ln_b)
    if "nc" not in _CACHE:
        _CACHE["nc"] = build_program()[0]
    res = run_bass_kernel_spmd(_CACHE["nc"], in_maps, core_ids=list(range(8)))
    out = np.stack([np.asarray(res.results[b]["out"], np.float32) for b in range(4)], 0)
    return out
```
